# Optimizing a Trainium2 kernel written in Bass

```python
import math
import jax, jax.numpy as jnp
from jax import lax
import numpy as np

D_MODEL = 1024
BATCH = 4
SEQ = 4096
DEPTH = 2
DEC_BATCH = 8
DEC_SEQ = 8192
PAST_LEN = 128

GRID_W = 64
NA_HEADS = 8
NA_HEAD_DIM = 64
NA_WIDTH = NA_HEADS * NA_HEAD_DIM
WIN_R = 8
WIN_C = 16
Q_BLOCK_C = 16
K_BLOCK_C = 32
DN_HEADS = 4
DN_HEAD_DIM = 128
DN_WIDTH = DN_HEADS * DN_HEAD_DIM
CONV_K = 5
CHUNK = 64
N_DIR = 2
EPS = 1e-6
PROJ_SIZES = (3 * NA_WIDTH, NA_WIDTH, 3 * DN_WIDTH, DN_WIDTH, N_DIR * DN_HEADS, N_DIR * DN_HEADS, D_MODEL, D_MODEL)
D_IN = 3 * NA_WIDTH + NA_WIDTH + 3 * DN_WIDTH + DN_WIDTH + 2 * N_DIR * DN_HEADS + 2 * D_MODEL

kernel_name = "hybrid_natten_gdn_encoder"


def _rms_norm(x, g):
    xf = x.astype(jnp.float32)
    y = xf * lax.rsqrt(jnp.mean(xf * xf, axis=-1, keepdims=True) + EPS)
    return (y * g.astype(jnp.float32)).astype(x.dtype)


def _l2_norm(x):
    return x * lax.rsqrt(jnp.sum(x * x, axis=-1, keepdims=True) + EPS)


def _neighbourhood_attention(q, k, v, rpb):
    B, T, H, d = q.shape
    rows = T // GRID_W
    wr = min(WIN_R, rows)
    n_cb = GRID_W // Q_BLOCK_C
    qcol = jnp.arange(GRID_W).reshape(n_cb, Q_BLOCK_C)
    col_start = jnp.clip(qcol - WIN_C // 2, 0, GRID_W - WIN_C)
    kcol = jnp.clip(jnp.arange(n_cb) * Q_BLOCK_C - WIN_C // 2, 0, GRID_W - K_BLOCK_C)[:, None] \
        + jnp.arange(K_BLOCK_C)
    col_ok = (kcol[:, None, :] >= col_start[..., None]) & (kcol[:, None, :] < col_start[..., None] + WIN_C)
    dc_idx = jnp.clip(kcol[:, None, :] - qcol[..., None], -(WIN_C - 1), WIN_C - 1) + WIN_C - 1
    qg = jnp.moveaxis(q.reshape(B, rows, GRID_W, H, d), 1, 0)
    kc = k.reshape(B, rows, GRID_W, H, d)[:, :, kcol]
    vc = v.reshape(B, rows, GRID_W, H, d)[:, :, kcol]
    scale = d ** -0.5

    def row_block(args):
        r, q_row = args
        rs = jnp.clip(r - WIN_R // 2, 0, rows - wr)
        k_blk = lax.dynamic_slice_in_dim(kc, rs, wr, axis=1)
        v_blk = lax.dynamic_slice_in_dim(vc, rs, wr, axis=1)
        qb = q_row.reshape(B, n_cb, Q_BLOCK_C, H, d)
        s = jnp.einsum('bjqhd,brjkhd->bhjqrk', qb, k_blk).astype(jnp.float32) * scale
        dr_idx = rs + jnp.arange(wr) - r + WIN_R - 1
        bias = rpb[:, dr_idx[None, None, :, None], dc_idx[:, :, None, :]]
        s = s + bias.astype(jnp.float32)
        s = jnp.where(col_ok[:, :, None, :], s, -jnp.inf)
        p = jax.nn.softmax(s, axis=(-2, -1))
        o = jnp.einsum('bhjqrk,brjkhd->bjqhd', p.astype(v.dtype), v_blk)
        return o.reshape(B, GRID_W, H, d)

    out = lax.map(row_block, (jnp.arange(rows), qg))
    return jnp.moveaxis(out, 0, 1).reshape(B, T, H, d)


def _gated_delta_chunked(q, k, v, g, beta):
    B, T, H, dk = q.shape
    dv = v.shape[-1]
    n = T // CHUNK

    def blocks(t):
        t = t.reshape((B, n, CHUNK, H) + t.shape[3:])
        return jnp.moveaxis(t, 3, 1)

    q, k, v, g, beta = blocks(q), blocks(k), blocks(v), blocks(g), blocks(beta)
    gc = jnp.cumsum(g, axis=-1)
    incl = jnp.tril(jnp.ones((CHUNK, CHUNK), dtype=bool))
    strict = jnp.tril(jnp.ones((CHUNK, CHUNK), dtype=bool), -1)
    diff = gc[..., :, None] - gc[..., None, :]
    decay = jnp.where(incl, jnp.exp(jnp.where(incl, diff, 0.0)), 0.0)
    kb = k * beta[..., None]
    a_mat = jnp.where(strict, jnp.einsum('bhnid,bhnjd->bhnij', kb, k) * decay, 0.0) + jnp.eye(CHUNK, dtype=q.dtype)
    rhs = jnp.concatenate([v * beta[..., None], kb * jnp.exp(gc)[..., None]], axis=-1)
    sol = lax.linalg.triangular_solve(a_mat, rhs, left_side=True, lower=True, unit_diagonal=True)
    u, w = sol[..., :dv], sol[..., dv:]
    intra = jnp.einsum('bhnid,bhnjd->bhnij', q, k) * decay
    q_dec = q * jnp.exp(gc)[..., None]
    k_dec = k * jnp.exp(gc[..., -1:] - gc)[..., None]
    g_last = jnp.exp(gc[..., -1])

    def step(S, xs):
        u_c, w_c, q_c, k_c, a_c, gl = xs
        v_new = u_c - jnp.einsum('bhck,bhkv->bhcv', w_c, S)
        o = jnp.einsum('bhck,bhkv->bhcv', q_c, S) + jnp.einsum('bhij,bhjv->bhiv', a_c, v_new)
        S = S * gl[..., None, None] + jnp.einsum('bhck,bhcv->bhkv', k_c, v_new)
        return S, o

    xs = (jnp.moveaxis(u, 2, 0), jnp.moveaxis(w, 2, 0), jnp.moveaxis(q_dec, 2, 0),
          jnp.moveaxis(k_dec, 2, 0), jnp.moveaxis(intra, 2, 0), jnp.moveaxis(g_last, 2, 0))
    S0 = jnp.zeros((B, H, dk, dv), jnp.float32)
    _, o = lax.scan(step, S0, xs)
    return jnp.transpose(o, (1, 0, 3, 2, 4)).reshape(B, T, H, dv)


def _deltanet_branch(qkv, z, beta_raw, alpha_raw, conv_w, a_log, dt_bias, norm_g):
    B, T, C = qkv.shape
    qkv = lax.conv_general_dilated(qkv, conv_w.reshape(CONV_K, 1, C), window_strides=(1,),
                                   padding=[(CONV_K // 2, CONV_K // 2)],
                                   dimension_numbers=('NWC', 'WIO', 'NWC'), feature_group_count=C)
    qkv = jax.nn.silu(qkv).astype(jnp.float32).reshape(B, T, 3, DN_HEADS, DN_HEAD_DIM)
    q = _l2_norm(qkv[:, :, 0]) * (DN_HEAD_DIM ** -0.5)
    k = _l2_norm(qkv[:, :, 1])
    v = qkv[:, :, 2]
    beta = jax.nn.sigmoid(beta_raw.astype(jnp.float32)).reshape(B, T, N_DIR, DN_HEADS)
    g = -jnp.exp(a_log.astype(jnp.float32)) * jax.nn.softplus(
        alpha_raw.astype(jnp.float32).reshape(B, T, N_DIR, DN_HEADS) + dt_bias.astype(jnp.float32))
    o_fwd = _gated_delta_chunked(q, k, v, g[:, :, 0], beta[:, :, 0])
    o_bwd = _gated_delta_chunked(q[:, ::-1], k[:, ::-1], v[:, ::-1], g[:, ::-1, 1], beta[:, ::-1, 1])[:, ::-1]
    o = _rms_norm(o_fwd + o_bwd, norm_g).astype(z.dtype).reshape(B, T, DN_WIDTH)
    return o * jax.nn.silu(z)


def _encoder_layer(x, norm_g, w_in, q_norm_g, k_norm_g, rpb, conv_w, a_log, dt_bias, dn_norm_g,
                   w_branch_a, w_branch_b, w_out):
    B, T, _ = x.shape
    h = _rms_norm(x, norm_g)
    proj = jnp.einsum('btd,de->bte', h, w_in)
    split_points = [int(s) for s in np.cumsum(PROJ_SIZES)[:-1]]
    qkv_a, z_a, qkv_b, z_b, beta_raw, alpha_raw, gate_a, gate_b = jnp.split(proj, split_points, axis=-1)
    qkv_a = qkv_a.reshape(B, T, 3, NA_HEADS, NA_HEAD_DIM)
    q_a = _rms_norm(qkv_a[:, :, 0], q_norm_g)
    k_a = _rms_norm(qkv_a[:, :, 1], k_norm_g)
    o_a = _neighbourhood_attention(q_a, k_a, qkv_a[:, :, 2], rpb).reshape(B, T, NA_WIDTH) * jax.nn.silu(z_a)
    y_a = jnp.einsum('bte,ed->btd', o_a, w_branch_a)
    o_b = _deltanet_branch(qkv_b, z_b, beta_raw, alpha_raw, conv_w, a_log, dt_bias, dn_norm_g)
    y_b = jnp.einsum('bte,ed->btd', o_b, w_branch_b)
    merged = jax.nn.sigmoid(gate_a) * y_a + jax.nn.sigmoid(gate_b) * y_b
    return x + jnp.einsum('btd,de->bte', merged, w_out)


def _trunk(x, norm_g, w_in, attn_q_norm_g, attn_k_norm_g, attn_rpb, dn_conv_w, dn_a_log, dn_dt_bias,
           dn_norm_g, w_branch_a, w_branch_b, w_out):
    for l in range(DEPTH):
        x = _encoder_layer(x, norm_g[l], w_in[l], attn_q_norm_g[l], attn_k_norm_g[l], attn_rpb[l],
                           dn_conv_w[l], dn_a_log[l], dn_dt_bias[l], dn_norm_g[l],
                           w_branch_a[l], w_branch_b[l], w_out[l])
    return x


def setup_inputs(seed: int = 0) -> dict:
    key = jax.random.key(seed)
    ks = jax.random.split(key, 14)
    nrm = jax.random.normal
    x_prompt = nrm(ks[0], (BATCH, SEQ, D_MODEL), jnp.float32)
    x_sample = nrm(ks[1], (DEC_BATCH, DEC_SEQ, D_MODEL), jnp.float32)
    norm_g = 1.0 + 0.02 * nrm(ks[2], (DEPTH, D_MODEL), jnp.float32)
    w_in = nrm(ks[3], (DEPTH, D_MODEL, D_IN), jnp.float32) * D_MODEL ** -0.5
    attn_q_norm_g = 1.0 + 0.02 * nrm(ks[4], (DEPTH, NA_HEAD_DIM), jnp.float32)
    attn_k_norm_g = 1.0 + 0.02 * nrm(ks[5], (DEPTH, NA_HEAD_DIM), jnp.float32)
    attn_rpb = 0.02 * nrm(ks[6], (DEPTH, NA_HEADS, 2 * WIN_R - 1, 2 * WIN_C - 1), jnp.float32)
    dn_conv_w = nrm(ks[7], (DEPTH, CONV_K, 3 * DN_WIDTH), jnp.float32) * CONV_K ** -0.5
    dn_a_log = jnp.log(jax.random.uniform(ks[8], (DEPTH, N_DIR, DN_HEADS), jnp.float32, minval=1.0, maxval=16.0))
    dt = jnp.exp(jax.random.uniform(ks[9], (DEPTH, N_DIR, DN_HEADS), jnp.float32,
                                    minval=math.log(1e-3), maxval=math.log(1e-1)))
    dn_dt_bias = dt + jnp.log(-jnp.expm1(-dt))
    dn_norm_g = 1.0 + 0.02 * nrm(ks[10], (DEPTH, DN_HEAD_DIM), jnp.float32)
    w_branch_a = nrm(ks[11], (DEPTH, NA_WIDTH, D_MODEL), jnp.float32) * NA_WIDTH ** -0.5
    w_branch_b = nrm(ks[12], (DEPTH, DN_WIDTH, D_MODEL), jnp.float32) * DN_WIDTH ** -0.5
    w_out = nrm(ks[13], (DEPTH, D_MODEL, D_MODEL), jnp.float32) * D_MODEL ** -0.5
    return {"x_prompt": x_prompt, "x_sample": x_sample, "norm_g": norm_g, "w_in": w_in,
            "attn_q_norm_g": attn_q_norm_g, "attn_k_norm_g": attn_k_norm_g, "attn_rpb": attn_rpb,
            "dn_conv_w": dn_conv_w, "dn_a_log": dn_a_log, "dn_dt_bias": dn_dt_bias, "dn_norm_g": dn_norm_g,
            "w_branch_a": w_branch_a, "w_branch_b": w_branch_b, "w_out": w_out}


def reference(x_prompt, x_sample, norm_g, w_in, attn_q_norm_g, attn_k_norm_g, attn_rpb, dn_conv_w,
              dn_a_log, dn_dt_bias, dn_norm_g, w_branch_a, w_branch_b, w_out):
    y_prompt = _trunk(x_prompt, norm_g, w_in, attn_q_norm_g, attn_k_norm_g, attn_rpb, dn_conv_w,
                      dn_a_log, dn_dt_bias, dn_norm_g, w_branch_a, w_branch_b, w_out)
    y_sample = _trunk(x_sample, norm_g, w_in, attn_q_norm_g, attn_k_norm_g, attn_rpb, dn_conv_w,
                      dn_a_log, dn_dt_bias, dn_norm_g, w_branch_a, w_branch_b, w_out)
    return (y_prompt, y_sample)
```

```python
import contextlib
import numpy as np
import concourse.bass as bass
import concourse.mybir as mybir
from concourse.bass_utils import run_bass_kernel_spmd

F32 = mybir.dt.float32
BF16 = mybir.dt.bfloat16
F32R = mybir.dt.float32r
USE_F32R = True


def R(ap):
    return ap.bitcast(F32R) if USE_F32R else ap
AF = mybir.ActivationFunctionType
ALU = mybir.AluOpType

D = 1024
DIN = 6160
EPS = 1e-6
NEG = -30000.0
BIG = 65536.0
TB = 512

C_ID = 0
C_UT = 128
C_LT = 256
C_BD = 384
C_ONE = 512
C_M_FWD = 640
C_M_BWD = 1152
NCONST = 1664


def make_consts():
    c = np.zeros((128, NCONST), np.float32)
    p = np.arange(128)[:, None]
    f = np.arange(128)[None, :]
    c[:, C_ID:C_ID + 128] = (p == f)
    c[:, C_UT:C_UT + 128] = (p <= f)
    c[:, C_LT:C_LT + 128] = (p >= f)
    c[:, C_BD:C_BD + 128] = ((p // 64) == (f // 64))
    c[:, C_ONE:C_ONE + 128] = 1.0
    c[:, C_M_FWD + 128:C_M_FWD + 256] = BIG * (f >= p)
    c[:, C_M_FWD + 256:C_M_FWD + 384] = -BIG * (f <= p)
    c[:, C_M_FWD + 384:C_M_FWD + 512] = -BIG * (f < p)
    c[:, C_M_BWD + 128:C_M_BWD + 256] = BIG * (f <= p)
    c[:, C_M_BWD + 256:C_M_BWD + 384] = -BIG * (f >= p)
    c[:, C_M_BWD + 384:C_M_BWD + 512] = -BIG * (f > p)
    return c


class Prog:
    ENG = ("pe", "act", "dve", "pool", "sp")
    NDS = 12

    def __init__(self, nc, stack):
        self.nc = nc
        self.ops = []
        self.lastw = {}
        self.readers = {}
        self.eng_map = {"pe": nc.tensor, "act": nc.scalar, "dve": nc.vector, "pool": nc.gpsimd, "sp": nc.sync}
        self.sems = {e: stack.enter_context(nc.semaphore("s_" + e)) for e in self.ENG}
        self.dsems = {e: [stack.enter_context(nc.semaphore("d_%s%d" % (e, i))) for i in range(self.NDS)] for e in ("sp", "pool")}
        self.cnt = {e: 0 for e in self.ENG}
        self.dcnt = {e: [0] * self.NDS for e in self.dsems}
        self.dnext = {e: 0 for e in self.dsems}
        self.sig = {}
        self.waited = {e: {} for e in self.ENG}
        self.emitted = 0
        self.n_instr = 0
        self.costs = []

    max_ops = None
    marks = {}
    attach_waits = True
    SYNC_LAT = 0.3

    def mark(self, name):
        self.marks.setdefault(name, len(self.ops))

    DEF_COST = {"pe": 0.07, "act": 0.40, "dve": 0.30, "pool": 0.6, "sp": 0.15}

    def op(self, eng, fn, reads=(), writes=(), dma=False, cost=None):
        if self.max_ops is not None and len(self.ops) >= self.max_ops:
            return -1
        self.costs.append(cost if cost is not None else ((0.8 if eng == "pool" else 0.15) if dma else self.DEF_COST[eng]))
        ex = [k for k in reads if isinstance(k, str) and k.startswith(("bk", "tp#", "acc#", "ssp#", "gbp", "P_"))]
        if ex:
            reads = [k for k in reads if k not in ex]
            writes = list(writes) + ex
        deps = set()
        for k in reads:
            w = self.lastw.get(k)
            if w is not None:
                deps.add(w)
        for k in writes:
            w = self.lastw.get(k)
            if w is not None:
                deps.add(w)
            rs = self.readers.get(k)
            if rs:
                deps.update(rs)
        idx = len(self.ops)
        self.ops.append((eng, fn, sorted(deps), dma))
        for k in reads:
            self.readers.setdefault(k, []).append(idx)
        for k in writes:
            self.lastw[k] = idx
            self.readers[k] = []
        return idx

    def flush(self):
        nc = self.nc
        ops = self.ops
        start, n = self.emitted, len(ops)
        import heapq
        costs = self.costs
        succ = {}
        indeg = {}
        for i in range(start, n):
            k = 0
            for d in ops[i][2]:
                if d >= start:
                    succ.setdefault(d, []).append(i)
                    k += 1
            indeg[i] = k
        blev = {}
        for i in range(n - 1, start - 1, -1):
            m_ = 0.0
            for j in succ.get(i, ()):
                if blev[j] > m_:
                    m_ = blev[j]
            blev[i] = m_ + costs[i] + (3.0 if ops[i][3] else 0.0)
        future = {e: [] for e in self.ENG}
        avail = {e: [] for e in self.ENG}
        ready = {}
        for i in range(start, n):
            if indeg[i] == 0:
                heapq.heappush(avail[ops[i][0]], (-blev[i], i))
        free = {e: 0.0 for e in self.ENG}
        finish = {}
        order = []
        remaining = n - start
        while remaining:
            best = None
            for e in self.ENG:
                if avail[e]:
                    st_ = free[e]
                elif future[e]:
                    st_ = max(free[e], future[e][0][0])
                else:
                    continue
                if best is None or st_ < best[0]:
                    best = (st_, e)
            st_, e = best
            fu = future[e]
            while fu and fu[0][0] <= st_:
                r_, j = heapq.heappop(fu)
                heapq.heappush(avail[e], (-blev[j], j))
            _, i = heapq.heappop(avail[e])
            dma = ops[i][3]
            free[e] = st_ + costs[i]
            finish[i] = st_ + (costs[i] + 3.0 if dma else costs[i])
            order.append(i)
            remaining -= 1
            for j in succ.get(i, ()):
                r_ = ready.get(j, 0.0)
                lat = 0.0 if (ops[j][0] == "pe" and e == "pe") else self.SYNC_LAT
                if finish[i] + lat > r_:
                    ready[j] = r_ = finish[i] + lat
                indeg[j] -= 1
                if indeg[j] == 0:
                    heapq.heappush(future[ops[j][0]], (r_, j))
        self.sim_time = max(finish.values()) if finish else 0.0
        need = {}
        lastop = {}
        for i in order:
            e, fn, deps, dma = ops[i]
            for d in deps:
                if e == "pe" and ops[d][0] == "pe":
                    continue
                if d >= start:
                    need[d] = True
                else:
                    assert d in self.sig, "cross-phase dep on unsignalled op"
            lastop[e] = i
            if dma:
                need[i] = True
        for j in lastop.values():
            need[j] = True
        NDS = self.NDS
        sems, dsems, cnt, dcnt, dnext, sig, waited = self.sems, self.dsems, self.cnt, self.dcnt, self.dnext, self.sig, self.waited
        for i in order:
            e, fn, deps, dma = ops[i]
            eo = self.eng_map[e]
            ws = {}
            for d in deps:
                if e == "pe" and ops[d][0] == "pe":
                    continue
                s, v = sig[d]
                key = id(s)
                if key not in ws or ws[key][1] < v:
                    ws[key] = (s, v)
            if dma:
                di = dnext[e] % NDS
                dnext[e] += 1
                ds = dsems[e][di]
                if dcnt[e][di] > 0:
                    key = id(ds)
                    v = dcnt[e][di]
                    if key not in ws or ws[key][1] < v:
                        ws[key] = (ds, v)
            wd = waited[e]
            pend = []
            for key, (s, v) in ws.items():
                if wd.get(key, 0) >= v:
                    continue
                wd[key] = v
                pend.append((s, v))
            for (s, v) in pend[1:]:
                eo.wait_ge(s, v)
                self.n_instr += 1
            ins = fn()
            if pend:
                if self.attach_waits:
                    ins._wait_ge(pend[0][0], pend[0][1])
                else:
                    eo.wait_ge(pend[0][0], pend[0][1])
            self.n_instr += 1
            if need.get(i):
                if dma:
                    dcnt[e][di] += 16
                    ins.then_inc(ds, 16)
                    sig[i] = (ds, dcnt[e][di])
                else:
                    cnt[e] += 1
                    ins.then_inc(sems[e], 1)
                    sig[i] = (sems[e], cnt[e])
            ops[i] = (e, None, None, dma)
        allsems = [(sems[x], cnt[x]) for x in self.ENG if cnt[x] > 0]
        for x in dsems:
            allsems += [(dsems[x][k], dcnt[x][k]) for k in range(NDS) if dcnt[x][k] > 0]
        for x in self.ENG:
            for (s_, v_) in allsems:
                if waited[x].get(id(s_), 0) < v_:
                    waited[x][id(s_)] = v_
                    self.eng_map[x].wait_ge(s_, v_)
                    self.n_instr += 1
        self.emitted = n
        self.lastw = {k: v for k, v in self.lastw.items() if isinstance(k, tuple)}
        self.readers = {k: v for k, v in self.readers.items() if isinstance(k, tuple)}


class Ring:
    def __init__(self, tiles, name):
        self.tiles = tiles
        self.name = name
        self.i = 0

    def next(self):
        k = self.i % len(self.tiles)
        self.i += 1
        return self.tiles[k], "%s#%d" % (self.name, k)


def build_program(seqs, n_layers=2, debug=False, stop_after=None, max_ops=None):
    nc = bass.Bass("TRN2", target_bir_lowering=False)
    L = n_layers
    Ttot = sum(seqs)
    bases = [sum(seqs[:i]) for i in range(len(seqs))]
    dk = "ExternalOutput" if debug else "Internal"

    def din(name, shape, dt=F32):
        return nc.dram_tensor(name, shape, dt, kind="ExternalInput").ap()

    xin = [din("x%d" % i, [T, D]) for i, T in enumerate(seqs)]
    yout = [nc.dram_tensor("y%d" % i, [T, D], F32, kind="ExternalOutput").ap() for i, T in enumerate(seqs)]
    w_in = din("w_in", [L, D, DIN])
    w_a = din("w_a", [L, 512, D])
    w_b = din("w_b", [L, 512, D])
    w_o = din("w_o", [L, D, D])
    norm_g = din("norm_g", [L, D])
    qg = din("qg", [L, 64])
    kg = din("kg", [L, 64])
    conv_w = din("conv_w", [L, 5, 1536])
    a_log = din("a_log", [L, 8])
    dt_bias = din("dt_bias", [L, 8])
    dn_g = din("dn_g", [L, 128])
    rpbT = din("rpbT", [L, 128, 8 * 14 * 64])
    consts = din("consts", [128, NCONST])

    def scr(name, shape, dt):
        return nc.dram_tensor(name, shape, dt, kind=dk).ap()

    qkT = scr("qkT", [1024, Ttot], BF16)
    va = scr("va", [Ttot, 512], BF16)
    zaT = scr("zaT", [512, Ttot], BF16)
    qkvbT = scr("qkvbT", [1536, Ttot], BF16)
    zbT = scr("zbT", [512, Ttot], BF16)
    gbs = scr("gbs", [Ttot, 16], F32)
    gaT = scr("gaT", [1024, Ttot], BF16)
    gbT = scr("gbT", [1024, Ttot], BF16)
    oT = scr("oT", [2, 512, Ttot], F32)
    prepT = scr("prepT", [4, 3, 128, Ttot], BF16)
    x1 = scr("xmid", [Ttot, D], F32)

    with contextlib.ExitStack() as gst:
        p = Prog(nc, gst)
        p.max_ops = max_ops
        uid = [0]

        def SB(st, name, shape, dt):
            uid[0] += 1
            return st.enter_context(nc.sbuf_tensor("%s_u%d" % (name, uid[0]), shape, dt))

        def PS(st, name, shape, dt=F32):
            uid[0] += 1
            return st.enter_context(nc.psum_tensor("%s_u%d" % (name, uid[0]), shape, dt))

        cst = SB(gst, "cst", [128, NCONST], F32)
        cstb = SB(gst, "cstb", [128, NCONST], BF16)
        p.op("sp", lambda: nc.sync.dma_start(out=cst[:], in_=consts), writes=["cst"], dma=True)
        p.op("dve", lambda: nc.vector.tensor_copy(out=cstb[:], in_=cst[:]), reads=["cst"], writes=["cstb"])
        identb = cstb[:, C_ID:C_ID + 128]
        bdb = cstb[:, C_BD:C_BD + 128]

        for l in range(L):
            with contextlib.ExitStack() as st:
                win = SB(st, "win", [128, 8, DIN], BF16)
                gcol = SB(st, "gcol", [128, 8], F32)
                qkgain = SB(st, "qkgain", [128, 2], F32)
                dtb = SB(st, "dtb", [128, 8], F32)
                nega = SB(st, "nega", [128, 8], F32)
                for kc in range(8):
                    p.op("pool", lambda kc=kc: nc.gpsimd.dma_start(out=win[:, kc, :], in_=w_in[l, kc * 128:(kc + 1) * 128, :]),
                         writes=["win%d" % kc], dma=True)
                p.op("sp", lambda: nc.sync.dma_start(out=gcol[:], in_=norm_g[l].rearrange("(kc p) -> p kc", p=128),
                                                     allow_slow_non_contiguous=True), writes=["gcol"], dma=True)
                for hh in range(2):
                    p.op("sp", lambda hh=hh: nc.sync.dma_start(out=qkgain[hh * 64:(hh + 1) * 64, 0:1], in_=qg[l].rearrange("(p o) -> p o", o=1),
                                                               allow_slow_non_contiguous=True), writes=["qkgain"], dma=True)
                    p.op("sp", lambda hh=hh: nc.sync.dma_start(out=qkgain[hh * 64:(hh + 1) * 64, 1:2], in_=kg[l].rearrange("(p o) -> p o", o=1),
                                                               allow_slow_non_contiguous=True), writes=["qkgain"], dma=True)
                p.op("act", lambda: nc.scalar.mul(out=qkgain[:, 0:1], in_=qkgain[:, 0:1], mul=0.125), reads=["qkgain"], writes=["qkgain"])
                p.op("sp", lambda: nc.sync.dma_start(out=dtb[:], in_=dt_bias[l:l + 1, :].broadcast_to([128, 8])), writes=["dtb"], dma=True)
                p.op("sp", lambda: nc.sync.dma_start(out=nega[:], in_=a_log[l:l + 1, :].broadcast_to([128, 8])), writes=["nega"], dma=True)
                p.op("act", lambda: nc.scalar.activation(out=nega[:], in_=nega[:], func=AF.Exp), reads=["nega"], writes=["nega"])
                p.op("dve", lambda: nc.vector.tensor_scalar(out=nega[:], in0=nega[:], scalar1=-1.0, scalar2=None, op0=ALU.mult),
                     reads=["nega"], writes=["nega"])

                xr = Ring([SB(st, "xt%d" % i, [128, 4, D], F32) for i in range(2)], "xt")
                xn = SB(st, "xn", [128, 4, D], BF16)
                junk = SB(st, "junk", [128, D], BF16)
                ssr = Ring([SB(st, "ss%d" % i, [128, 8], F32) for i in range(2)], "ss")
                hTr = Ring([SB(st, "hT%d" % i, [128, 8, TB], BF16) for i in range(2)], "hT")
                sqr = Ring([SB(st, "sq%d" % i, [128, TB], BF16) for i in range(2)], "sq")
                rsr = Ring([SB(st, "rs%d" % i, [128, TB], F32) for i in range(2)], "rs")
                sgr = Ring([SB(st, "sg%d" % i, [128, TB], F32) for i in range(3)], "sg")
                outr = Ring([SB(st, "ob%d" % i, [128, TB], BF16) for i in range(6)], "ob")
                gbt = SB(st, "gbt", [128, 4, 16], F32)
                gtmp = SB(st, "gtmp", [128, 4, 16], F32)
                tpr = Ring([PS(st, "tp%d" % i, [128, 2 * TB], BF16)[:, 0:TB] for i in range(2)], "tp")
                accr = Ring([PS(st, "acc%d" % i, [128, TB], F32) for i in range(4)], "acc")
                ssp = Ring([PS(st, "ssp%d" % i, [128, TB], F32) for i in range(1)], "ssp")
                gbp = PS(st, "gbp", [128, 4, 128], F32)[:, :, 0:16]
                winkeys = ["win%d" % kc for kc in range(8)]
                evac_flip = [0]

                def store_fm(dst, row0, nrows, gt0, tile, tkey, dkey):
                    return p.op("pool", lambda: nc.gpsimd.dma_start(out=dst[row0:row0 + nrows, gt0:gt0 + TB], in_=tile[0:nrows, :]),
                                reads=[tkey], writes=[dkey], dma=True)

                for si, T in enumerate(seqs):
                    xsrc = xin[si] if l == 0 else x1[bases[si]:bases[si] + T, :]
                    for b in range(T // TB):
                        t0 = b * TB
                        gt0 = bases[si] + t0
                        gblk = gt0 // TB
                        xt, kx = xr.next()
                        srck = [] if l == 0 else [("x1", gblk)]
                        p.op("sp", lambda xt=xt, t0=t0, xsrc=xsrc: nc.sync.dma_start(
                            out=xt[:, :, :], in_=xsrc[t0:t0 + TB, :].rearrange("(tt p) d -> p tt d", p=128)),
                            reads=srck, writes=[kx], dma=True)
                        ss, kss = ssr.next()
                        for tt in range(4):
                            p.op("act", lambda xt=xt, tt=tt, ss=ss: nc.scalar.activation(
                                out=junk[:, :], in_=xt[:, tt, :], func=AF.Square, accum_out=ss[:, tt:tt + 1]),
                                reads=[kx], writes=["junk", kss])
                        p.op("dve", lambda ss=ss: nc.vector.tensor_scalar(out=ss[:, 4:8], in0=ss[:, 0:4], scalar1=1.0 / D, scalar2=EPS,
                                                                          op0=ALU.mult, op1=ALU.add), reads=[kss], writes=[kss])
                        p.op("act", lambda ss=ss: nc.scalar.activation(out=ss[:, 4:8], in_=ss[:, 4:8], func=AF.Ln), reads=[kss], writes=[kss])
                        p.op("act", lambda ss=ss: nc.scalar.activation(out=ss[:, 4:8], in_=ss[:, 4:8], func=AF.Exp, scale=-0.5), reads=[kss], writes=[kss])
                        for tt in range(4):
                            p.op("act", lambda xt=xt, tt=tt, ss=ss: nc.scalar.activation(
                                out=xn[:, tt, :], in_=xt[:, tt, :], func=AF.Copy, scale=ss[:, 4 + tt:5 + tt]),
                                reads=[kx, kss], writes=["xn"])
                        hT, khT = hTr.next()
                        for kc in range(8):
                            tp, ktp = tpr.next()
                            for tt in range(4):
                                p.op("pe", lambda tp=tp, tt=tt, kc=kc: nc.tensor.transpose(
                                    out=tp[:, tt * 128:(tt + 1) * 128], in_=xn[:, tt, kc * 128:(kc + 1) * 128], identity=identb),
                                    reads=["xn", "cstb"], writes=[ktp])
                            if kc % 2 == 0:
                                p.op("act", lambda tp=tp, kc=kc, hT=hT: nc.scalar.activation(
                                    out=hT[:, kc, :], in_=tp[:, :], func=AF.Copy, scale=gcol[:, kc:kc + 1]),
                                    reads=[ktp, "gcol"], writes=[khT + "k%d" % kc])
                            else:
                                p.op("dve", lambda tp=tp, kc=kc, hT=hT: nc.vector.tensor_scalar(
                                    out=hT[:, kc, :], in0=tp[:, :], scalar1=gcol[:, kc:kc + 1], scalar2=None, op0=ALU.mult),
                                    reads=[ktp, "gcol"], writes=[khT + "k%d" % kc])
                        hkeys = [khT + "k%d" % kc for kc in range(8)]

                        def proj_fm(col0, M, hT=hT, hkeys=hkeys):
                            acc, kacc = accr.next()
                            for kc in range(8):
                                p.op("pe", lambda kc=kc, acc=acc: nc.tensor.matmul(
                                    acc[0:M, :], lhsT=win[:, kc, col0:col0 + M], rhs=hT[:, kc, :], start=(kc == 0), stop=(kc == 7)),
                                    reads=[winkeys[kc], hkeys[kc]], writes=[kacc], cost=0.23)
                            return acc, kacc

                        pending = []

                        def flush():
                            while pending:
                                pending.pop(0)()

                        for c in range(8):
                            acc, kacc = proj_fm(c * 128, 128)
                            flush()
                            sq, ksq = sqr.next()
                            p.op("act", lambda sq=sq, acc=acc: nc.scalar.activation(out=sq[:, :], in_=acc[:, :], func=AF.Square),
                                 reads=[kacc], writes=[ksq])

                            def later(c=c, acc=acc, kacc=kacc, sq=sq, ksq=ksq, gt0=gt0, gblk=gblk):
                                sp_, kssp = ssp.next()
                                p.op("pe", lambda: nc.tensor.matmul(sp_[:, :], lhsT=bdb, rhs=sq[:, :], start=True, stop=True),
                                     reads=[ksq, "cstb"], writes=[kssp], cost=0.23)
                                rs, krs = rsr.next()
                                p.op("dve", lambda: nc.vector.tensor_scalar(out=rs[:, :], in0=sp_[:, :], scalar1=1.0 / 64, scalar2=EPS,
                                                                            op0=ALU.mult, op1=ALU.add), reads=[kssp], writes=[krs])
                                p.op("act", lambda: nc.scalar.activation(out=rs[:, :], in_=rs[:, :], func=AF.Ln), reads=[krs], writes=[krs], cost=0.57)
                                p.op("act", lambda: nc.scalar.activation(out=rs[:, :], in_=rs[:, :], func=AF.Exp, scale=-0.5), reads=[krs], writes=[krs], cost=0.57)
                                ob, kob = outr.next()
                                gi = 0 if c < 4 else 1
                                p.op("dve", lambda: nc.vector.scalar_tensor_tensor(
                                    out=ob[:, :], in0=acc[:, :], scalar=qkgain[:, gi:gi + 1], in1=rs[:, :], op0=ALU.mult, op1=ALU.mult),
                                    reads=[kacc, krs, "qkgain"], writes=[kob])
                                store_fm(qkT, c * 128, 128, gt0, ob, kob, ("qkT", gblk))
                            pending.append(later)
                        for tt in range(4):
                            acc, kacc = accr.next()
                            for kc in range(8):
                                p.op("pe", lambda kc=kc, acc=acc, tt=tt, hT=hT: nc.tensor.matmul(
                                    acc[:, :], lhsT=hT[:, kc, tt * 128:(tt + 1) * 128], rhs=win[:, kc, 1024:1536], start=(kc == 0), stop=(kc == 7)),
                                    reads=[winkeys[kc], hkeys[kc]], writes=[kacc], cost=0.23)
                            flush()
                            ob, kob = outr.next()
                            p.op("dve", lambda ob=ob, acc=acc: nc.vector.tensor_copy(out=ob[:, :], in_=acc[:, :]), reads=[kacc], writes=[kob])
                            p.op("pool", lambda ob=ob, tt=tt, gt0=gt0: nc.gpsimd.dma_start(out=va[gt0 + tt * 128:gt0 + (tt + 1) * 128, :], in_=ob[:, :]),
                                 reads=[kob], writes=[("va", gblk)], dma=True)
                        plan = []
                        for c in range(4):
                            plan.append((1536 + c * 128, "silu", zaT, c * 128, "zaT"))
                        for c in range(4):
                            plan.append((3584 + c * 128, "silu", zbT, c * 128, "zbT"))
                        for c in range(8):
                            plan.append((4112 + c * 128, "sig", gaT, c * 128, "gaT"))
                        for c in range(8):
                            plan.append((5136 + c * 128, "sig", gbT, c * 128, "gbT"))
                        for c in range(12):
                            plan.append((2048 + c * 128, "copy", qkvbT, c * 128, "qkvbT"))
                        for (col0, kind, dst, row0, dname) in plan:
                            acc, kacc = proj_fm(col0, 128)
                            ob, kob = outr.next()
                            if kind in ("silu", "sig"):
                                sg, ksg = sgr.next()
                                p.op("act", lambda sg=sg, acc=acc: nc.scalar.activation(out=sg[:, :], in_=acc[:, :], func=AF.Exp, scale=-1.0),
                                     reads=[kacc], writes=[ksg], cost=0.57)
                                p.op("act", lambda sg=sg: nc.scalar.activation(out=sg[:, :], in_=sg[:, :], func=AF.Ln, bias=1.0), reads=[ksg], writes=[ksg], cost=0.57)
                                if kind == "sig":
                                    p.op("act", lambda sg=sg, ob=ob: nc.scalar.activation(out=ob[:, :], in_=sg[:, :], func=AF.Exp, scale=-1.0),
                                         reads=[ksg], writes=[kob], cost=0.57)
                                else:
                                    p.op("act", lambda sg=sg: nc.scalar.activation(out=sg[:, :], in_=sg[:, :], func=AF.Exp, scale=-1.0), reads=[ksg], writes=[ksg], cost=0.57)
                                    p.op("dve", lambda sg=sg, ob=ob, acc=acc: nc.vector.tensor_tensor(out=ob[:, :], in0=acc[:, :], in1=sg[:, :], op=ALU.mult),
                                         reads=[kacc, ksg], writes=[kob], cost=0.55)
                            else:
                                p.op("dve", lambda ob=ob, acc=acc: nc.vector.tensor_copy(out=ob[:, :], in_=acc[:, :]), reads=[kacc], writes=[kob])
                            store_fm(dst, row0, 128, gt0, ob, kob, (dname, gblk))
                        for tt in range(4):
                            for kc in range(8):
                                p.op("pe", lambda kc=kc, tt=tt, hT=hT: nc.tensor.matmul(
                                    gbp[:, tt, :], lhsT=hT[:, kc, tt * 128:(tt + 1) * 128], rhs=win[:, kc, 4096:4112], start=(kc == 0), stop=(kc == 7)),
                                    reads=[winkeys[kc], hkeys[kc]], writes=["gbp"])
                        p.op("dve", lambda: nc.vector.tensor_scalar(out=gtmp[:, :, 0:8], in0=gbp[:, :, 0:8], scalar1=-1.0, scalar2=None, op0=ALU.mult),
                             reads=["gbp"], writes=["gtmp"])
                        p.op("dve", lambda: nc.vector.tensor_tensor(out=gtmp[:, :, 8:16], in0=gbp[:, :, 8:16],
                                                                    in1=dtb[:, :].unsqueeze(1).broadcast_to([128, 4, 8]), op=ALU.add),
                             reads=["gbp", "dtb"], writes=["gtmp"])
                        p.op("act", lambda: nc.scalar.activation(out=gtmp[:, :, :], in_=gtmp[:, :, :], func=AF.Exp), reads=["gtmp"], writes=["gtmp"])
                        p.op("act", lambda: nc.scalar.activation(out=gtmp[:, :, :], in_=gtmp[:, :, :], func=AF.Ln, bias=1.0), reads=["gtmp"], writes=["gtmp"])
                        p.op("dve", lambda: nc.vector.tensor_scalar(out=gbt[:, :, 0:8], in0=gtmp[:, :, 0:8], scalar1=-1.0, scalar2=None, op0=ALU.mult),
                             reads=["gtmp"], writes=["gbt"])
                        p.op("dve", lambda: nc.vector.tensor_tensor(out=gbt[:, :, 8:16], in0=gtmp[:, :, 8:16],
                                                                    in1=nega[:, :].unsqueeze(1).broadcast_to([128, 4, 8]), op=ALU.mult),
                             reads=["gtmp", "nega"], writes=["gbt"])
                        p.op("pool", lambda gt0=gt0: nc.gpsimd.dma_start(out=gbs[gt0:gt0 + TB, :].rearrange("(tt p) c -> p tt c", p=128), in_=gbt[:, :, :]),
                             reads=["gbt"], writes=[("gbs", gblk)], dma=True)
                        flush()
                p.flush()
            if stop_after == "p1":
                break
            for d in ((1, 0) if stop_after != "p2" else (1,)):
                with contextlib.ExitStack() as st:
                    TRI = cst[:, C_LT:C_LT + 128] if d == 1 else cst[:, C_UT:C_UT + 128]
                    mbase = C_M_BWD if d == 1 else C_M_FWD
                    MASKB = cstb[:, mbase:mbase + 512]
                    LAST = 0 if d == 1 else 127
                    identf = cst[:, C_ID:C_ID + 128]
                    onesf = cst[:, C_ONE:C_ONE + 128]
                    onesb = cstb[:, C_ONE:C_ONE + 128]
                    cw = SB(st, "cw", [128, 12, 5], F32)
                    cwd = SB(st, "cwd", [128, 60, 128], BF16)
                    for tap in range(5):
                        p.op("sp", lambda tap=tap: nc.sync.dma_start(out=cw[:, :, tap], in_=conv_w[l, tap, :].rearrange("(j p) -> p j", p=128),
                                                                     allow_slow_non_contiguous=True), writes=["cw"], dma=True)
                    for j in range(12):
                        for tap in range(5):
                            p.op("dve", lambda j=j, tap=tap: nc.vector.tensor_scalar(
                                out=cwd[:, j * 5 + tap, :], in0=identf, scalar1=cw[:, j, tap:tap + 1], scalar2=None, op0=ALU.mult),
                                reads=["cw", "cst"], writes=["cwd"])
                    p.mark('dn_consts_done')
                    rawr = Ring([SB(st, "raw%d" % i, [128, 12, TB + 4], BF16) for i in range(2)], "raw")
                    gscr = Ring([SB(st, "gsc%d" % i, [128, 4, 16], F32) for i in range(2)], "gsc")
                    knq_r = [[SB(st, "knq%d_%d" % (h, i), [128, 2, TB], BF16) for h in range(4)] for i in range(2)]
                    vsT_r = [[SB(st, "vsT%d_%d" % (h, i), [128, TB], BF16) for h in range(4)] for i in range(2)]
                    sil = [SB(st, "sil%d" % h, [128, TB], F32) for h in range(4)]
                    sqb = [SB(st, "sqb%d" % h, [128, TB], BF16) for h in range(4)]
                    rst = [SB(st, "rst%d" % h, [128, TB], F32) for h in range(4)]
                    rhs4 = SB(st, "rhs4", [128, 4, 128], F32)
                    rhsL = SB(st, "rhsL", [128, 4, 128], F32)
                    sc = SB(st, "sc", [128, 24], F32)
                    E4 = [SB(st, "E4_%d" % h, [128, 128], F32) for h in range(4)]
                    E2 = [SB(st, "E2_%d" % h, [128, 128], F32) for h in range(4)]
                    E13 = [SB(st, "E13_%d" % h, [128, 256], F32) for h in range(4)]
                    WW = [[SB(st, "WW%d_%d" % (h, i), [128, 384], F32) for i in range(2)] for h in range(4)]
                    onesr = SB(st, "onesr", [128, 128], F32)
                    trir = SB(st, "trir", [128, 128], F32)
                    gr = SB(st, "gr", [128, 4], F32)
                    p.op("dve", lambda: nc.vector.tensor_copy(out=R(onesr[:, :]), in_=onesf), reads=["cst"], writes=["onesr"])
                    p.op("dve", lambda: nc.vector.tensor_copy(out=R(trir[:, :]), in_=TRI), reads=["cst"], writes=["trir"])
                    Tt = [SB(st, "Tt%d" % h, [128, 128], BF16) for h in range(4)]
                    M3 = [SB(st, "M3_%d" % h, [128, 128], BF16) for h in range(4)]
                    qdec = [SB(st, "qdec%d" % h, [128, 128], BF16) for h in range(4)]
                    kbg = [SB(st, "kbg%d" % h, [128, 128], BF16) for h in range(4)]
                    kdec = [SB(st, "kdec%d" % h, [128, 128], BF16) for h in range(4)]
                    vb = [SB(st, "vb%d" % h, [128, 128], BF16) for h in range(4)]
                    uu = [SB(st, "uu%d" % h, [128, 128], F32) for h in range(4)]
                    wT = [SB(st, "wT%d" % h, [128, 128], BF16) for h in range(4)]
                    vnew = [SB(st, "vnew%d" % h, [128, 128], BF16) for h in range(4)]
                    Sf = [SB(st, "Sf%d" % h, [128, 128], F32) for h in range(4)]
                    Sb = [SB(st, "Sb%d" % h, [128, 128], BF16) for h in range(4)]
                    obr = Ring([SB(st, "obuf%d" % i, [128, 4, TB], F32) for i in range(2)], "obuf")
                    bk0 = [PS(st, "bk0_%d" % h, [128, 512], F32) for h in range(4)]
                    bk1 = [PS(st, "bk1_%d" % h, [128, 512], F32) for h in range(4)]
                    k0 = ["bk0_%d" % h for h in range(4)]
                    k1 = ["bk1_%d" % h for h in range(4)]

                    for si, T in enumerate(seqs):
                        nb = T // TB
                        for h in range(4):
                            p.op("pool", lambda h=h: nc.gpsimd.memset(Sf[h][:, :], 0.0), writes=["Sf%d" % h])
                            p.op("pool", lambda h=h: nc.gpsimd.memset(Sb[h][:, :], 0.0), writes=["Sb%d" % h])
                        def _blk(bi, bp, knq, vsT, si=si, T=T, nb=nb):
                                b = nb - 1 - bi if d == 1 else bi
                                t0 = b * TB
                                gt0 = bases[si] + t0
                                gblk = gt0 // TB
                                if d == 1:
                                    raw, kraw = rawr.next()
                                    lo = 0 if b > 0 else 2
                                    hi = TB + 4 if b < nb - 1 else TB + 2
                                    if lo > 0:
                                        p.op("pool", lambda raw=raw: nc.gpsimd.memset(raw[:, :, 0:2], 0.0), writes=[kraw])
                                    if hi < TB + 4:
                                        p.op("pool", lambda raw=raw: nc.gpsimd.memset(raw[:, :, TB + 2:TB + 4], 0.0), writes=[kraw])
                                    rk = [("qkvbT", gblk)] + ([("qkvbT", gblk - 1)] if b > 0 else []) + ([("qkvbT", gblk + 1)] if b < nb - 1 else [])
                                    for part in range(3):
                                        p.op("sp", lambda raw=raw, lo=lo, hi=hi, gt0=gt0, part=part: nc.sync.dma_start(
                                            out=raw[:, part * 4:(part + 1) * 4, lo:hi],
                                            in_=qkvbT[part * 512:(part + 1) * 512, gt0 - 2 + lo:gt0 - 2 + hi].rearrange("(j p) t -> p j t", p=128)),
                                            reads=rk, writes=[kraw], dma=True)
                                gsc, kgsc = gscr.next()
                                p.op("sp", lambda gsc=gsc, gt0=gt0: nc.sync.dma_start(
                                    out=gsc[:, :, :], in_=gbs[gt0:gt0 + TB, :].rearrange("(c p) k -> p c k", p=128)),
                                    reads=[("gbs", gblk)], writes=[kgsc], dma=True)
                                if d == 1:
                                    p.mark('dn_loads_done')
                                    for h in range(4):
                                        for (which, j, bank, bkey) in ((0, 4 + h, bk0[h], k0[h]), (1, h, bk1[h], k1[h])):
                                            for tap in range(5):
                                                p.op("pe", lambda j=j, tap=tap, bank=bank, raw=raw: nc.tensor.matmul(
                                                    bank[:, :], lhsT=cwd[:, j * 5 + tap, :], rhs=raw[:, j, tap:tap + TB], start=(tap == 0), stop=(tap == 4)),
                                                    reads=["cwd", kraw], writes=[bkey], cost=0.23)
                                            p.op("act", lambda bank=bank, h=h: nc.scalar.activation(out=sil[h][:, :], in_=bank[:, :], func=AF.Exp, scale=-1.0),
                                                 reads=[bkey], writes=["sil%d" % h], cost=0.57)
                                            p.op("act", lambda h=h: nc.scalar.activation(out=sil[h][:, :], in_=sil[h][:, :], func=AF.Ln, bias=1.0),
                                                 reads=["sil%d" % h], writes=["sil%d" % h], cost=0.57)
                                            p.op("act", lambda h=h: nc.scalar.activation(out=sil[h][:, :], in_=sil[h][:, :], func=AF.Exp, scale=-1.0),
                                                 reads=["sil%d" % h], writes=["sil%d" % h], cost=0.57)
                                            p.op("dve", lambda bank=bank, h=h: nc.vector.tensor_tensor(out=sil[h][:, :], in0=bank[:, :], in1=sil[h][:, :], op=ALU.mult),
                                                 reads=[bkey, "sil%d" % h], writes=["sil%d" % h], cost=0.55)
                                            p.op("act", lambda h=h: nc.scalar.activation(out=sqb[h][:, :], in_=sil[h][:, :], func=AF.Square),
                                                 reads=["sil%d" % h], writes=["sqb%d" % h])
                                            p.op("pe", lambda bank=bank, h=h: nc.tensor.matmul(bank[:, :], lhsT=onesb, rhs=sqb[h][:, :], start=True, stop=True),
                                                 reads=["sqb%d" % h, "cstb"], writes=[bkey], cost=0.23)
                                            p.op("dve", lambda bank=bank, h=h: nc.vector.tensor_scalar(out=rst[h][:, :], in0=bank[:, :], scalar1=EPS, scalar2=None, op0=ALU.add),
                                                 reads=[bkey], writes=["rst%d" % h])
                                            p.op("act", lambda h=h: nc.scalar.activation(out=rst[h][:, :], in_=rst[h][:, :], func=AF.Ln),
                                                 reads=["rst%d" % h], writes=["rst%d" % h], cost=0.57)
                                            p.op("act", lambda h=h: nc.scalar.activation(out=rst[h][:, :], in_=rst[h][:, :], func=AF.Exp, scale=-0.5),
                                                 reads=["rst%d" % h], writes=["rst%d" % h], cost=0.57)
                                            sclq = (128.0 ** -0.5) if which == 1 else 1.0
                                            p.op("dve", lambda h=h, which=which, sclq=sclq: nc.vector.scalar_tensor_tensor(
                                                out=knq[h][:, which, :], in0=sil[h][:, :], scalar=sclq, in1=rst[h][:, :], op0=ALU.mult, op1=ALU.mult),
                                                reads=["sil%d" % h, "rst%d" % h], writes=["knq%d_%d" % (h, bp)])
                                        for tap in range(5):
                                            p.op("pe", lambda h=h, tap=tap, raw=raw: nc.tensor.matmul(
                                                bk0[h][:, :], lhsT=cwd[:, (8 + h) * 5 + tap, :], rhs=raw[:, 8 + h, tap:tap + TB], start=(tap == 0), stop=(tap == 4)),
                                                reads=["cwd", kraw], writes=[k0[h]], cost=0.23)
                                        p.op("act", lambda h=h: nc.scalar.activation(out=sil[h][:, :], in_=bk0[h][:, :], func=AF.Exp, scale=-1.0),
                                             reads=[k0[h]], writes=["sil%d" % h], cost=0.57)
                                        p.op("act", lambda h=h: nc.scalar.activation(out=sil[h][:, :], in_=sil[h][:, :], func=AF.Ln, bias=1.0),
                                             reads=["sil%d" % h], writes=["sil%d" % h], cost=0.57)
                                        p.op("act", lambda h=h: nc.scalar.activation(out=sil[h][:, :], in_=sil[h][:, :], func=AF.Exp, scale=-1.0),
                                             reads=["sil%d" % h], writes=["sil%d" % h], cost=0.57)
                                        p.op("dve", lambda h=h: nc.vector.tensor_tensor(out=vsT[h][:, :], in0=bk0[h][:, :], in1=sil[h][:, :], op=ALU.mult),
                                             reads=[k0[h], "sil%d" % h], writes=["vsT%d_%d" % (h, bp)], cost=0.55)
                                        p.op("pool", lambda h=h, gt0=gt0: nc.gpsimd.dma_start(out=prepT[h, 0:2, :, gt0:gt0 + TB].rearrange("w p t -> p w t"), in_=knq[h][:, :, :]),
                                             reads=["knq%d_%d" % (h, bp)], writes=[("prepT", gblk)], dma=True)
                                        p.op("pool", lambda h=h, gt0=gt0: nc.gpsimd.dma_start(out=prepT[h, 2, :, gt0:gt0 + TB], in_=vsT[h][:, :]),
                                             reads=["vsT%d_%d" % (h, bp)], writes=[("prepT", gblk)], dma=True)
                                else:
                                    for h in range(4):
                                        p.op("sp", lambda h=h, gt0=gt0: nc.sync.dma_start(out=knq[h][:, :, :], in_=prepT[h, 0:2, :, gt0:gt0 + TB].rearrange("w p t -> p w t")),
                                             reads=[("prepT", gblk)], writes=["knq%d_%d" % (h, bp)], dma=True)
                                        p.op("sp", lambda h=h, gt0=gt0: nc.sync.dma_start(out=vsT[h][:, :], in_=prepT[h, 2, :, gt0:gt0 + TB]),
                                             reads=[("prepT", gblk)], writes=["vsT%d_%d" % (h, bp)], dma=True)
                                p.mark('dn_prep_done')
                                ob, kob = obr.next()
                                for ci in range(4):
                                    c = 3 - ci if d == 1 else ci
                                    cs = slice(c * 128, (c + 1) * 128)
                                    gcol0 = 8 + d * 4
                                    p.op("dve", lambda c=c, gsc=gsc: nc.vector.tensor_copy(out=R(gr[:, :]), in_=gsc[:, c, gcol0:gcol0 + 4]), reads=[kgsc], writes=["gr"])
                                    p.op("pe", lambda: nc.tensor.matmul(bk1[0][:, 0:4], lhsT=R(trir[:, :]), rhs=R(gr[:, :]), start=True, stop=True),
                                         reads=["trir", "gr"], writes=[k1[0]])
                                    p.op("dve", lambda: nc.vector.tensor_copy(out=sc[:, 0:4], in_=bk1[0][:, 0:4]), reads=[k1[0]], writes=["sc"])
                                    p.op("dve", lambda c=c, gsc=gsc: nc.vector.tensor_tensor(out=sc[:, 4:8], in0=sc[:, 0:4], in1=gsc[:, c, d * 4:d * 4 + 4], op=ALU.add),
                                         reads=["sc", kgsc], writes=["sc"])
                                    p.op("dve", lambda: nc.vector.tensor_scalar(out=sc[:, 8:12], in0=sc[:, 0:4], scalar1=-1.0, scalar2=None, op0=ALU.mult),
                                         reads=["sc"], writes=["sc"])
                                    p.op("act", lambda: nc.scalar.activation(out=sc[:, 12:16], in_=sc[:, 4:8], func=AF.Exp), reads=["sc"], writes=["sc"])
                                    p.op("act", lambda c=c, gsc=gsc: nc.scalar.activation(out=sc[:, 16:20], in_=gsc[:, c, d * 4:d * 4 + 4], func=AF.Exp),
                                         reads=[kgsc, "sc"], writes=["sc"])
                                    p.op("dve", lambda c=c, gsc=gsc: nc.vector.tensor_tensor(
                                        out=R(rhs4[:, :, :]), in0=TRI.unsqueeze(1).broadcast_to([128, 4, 128]),
                                        in1=gsc[:, c, gcol0:gcol0 + 4].unsqueeze(2).broadcast_to([128, 4, 128]), op=ALU.mult),
                                        reads=["cst", kgsc], writes=["rhs4"])
                                    p.op("dve", lambda c=c, gsc=gsc: nc.vector.tensor_tensor(
                                        out=R(rhsL[:, :, :]), in0=identf.unsqueeze(1).broadcast_to([128, 4, 128]),
                                        in1=gsc[:, c, d * 4:d * 4 + 4].unsqueeze(2).broadcast_to([128, 4, 128]), op=ALU.mult),
                                        reads=["cst", kgsc], writes=["rhsL"])
                                    p.mark('dn_sc_done')
                                    for h in range(4):
                                        p.op("pe", lambda h=h: nc.tensor.matmul(bk0[h][:, :], lhsT=identb, rhs=MASKB, start=True, stop=False),
                                             reads=["cstb"], writes=[k0[h]], cost=0.23)
                                        for q4 in range(4):
                                            p.op("pe", lambda h=h, q4=q4: nc.tensor.matmul(bk0[h][:, q4 * 128:(q4 + 1) * 128], lhsT=R(onesr[:, :]), rhs=R(rhs4[:, h, :]),
                                                                                           start=False, stop=False),
                                                 reads=["onesr", "rhs4"], writes=[k0[h]], cost=0.07)
                                        p.op("pe", lambda h=h: nc.tensor.matmul(bk0[h][:, 256:384], lhsT=R(onesr[:, :]), rhs=R(rhsL[:, h, :]), start=False, stop=True),
                                             reads=["onesr", "rhsL"], writes=[k0[h]], cost=0.07)
                                        p.op("pe", lambda h=h, cs=cs: nc.tensor.matmul(bk1[h][:, 0:256], lhsT=knq[h][:, 0, cs], rhs=knq[h][:, :, cs], start=True, stop=True),
                                             reads=["knq%d_%d" % (h, bp)], writes=[k1[h]])
                                        p.op("act", lambda h=h: nc.scalar.activation(out=E4[h][:, :], in_=bk0[h][:, 0:128], func=AF.Exp), reads=[k0[h]], writes=["E4_%d" % h])
                                        p.op("act", lambda h=h: nc.scalar.activation(out=E2[h][:, :], in_=bk0[h][:, 128:256], func=AF.Exp, scale=-1.0, bias=sc[:, 4 + h:5 + h]),
                                             reads=[k0[h], "sc"], writes=["E2_%d" % h])
                                        p.op("act", lambda h=h: nc.scalar.activation(out=E13[h][:, :], in_=bk0[h][:, 256:512], func=AF.Exp, bias=sc[:, 8 + h:9 + h]),
                                             reads=[k0[h], "sc"], writes=["E13_%d" % h])
                                        p.op("act", lambda h=h: nc.scalar.activation(out=sc[:, 20 + h:21 + h], in_=bk0[h][:, LAST:LAST + 1], func=AF.Exp, bias=sc[:, 8 + h:9 + h]),
                                             reads=[k0[h], "sc"], writes=["sc"])
                                        p.op("dve", lambda h=h: nc.vector.tensor_tensor(out=R(WW[h][0][:, 128:256]), in0=bk1[h][:, 0:128], in1=E13[h][:, 0:128], op=ALU.mult),
                                             reads=[k1[h], "E13_%d" % h], writes=["WW%d_0" % h])
                                        p.op("dve", lambda h=h: nc.vector.tensor_tensor(out=R(WW[h][0][:, 256:384]), in0=bk1[h][:, 0:128], in1=E2[h][:, :], op=ALU.mult),
                                             reads=[k1[h], "E2_%d" % h], writes=["WW%d_0" % h])
                                        p.op("dve", lambda h=h: nc.vector.tensor_tensor(out=M3[h][:, :], in0=bk1[h][:, 128:256], in1=E13[h][:, 128:256], op=ALU.mult),
                                             reads=[k1[h], "E13_%d" % h], writes=["M3_%d" % h])
                                        p.op("dve", lambda h=h, cs=cs: nc.vector.tensor_tensor(out=qdec[h][:, :], in0=knq[h][:, 1, cs], in1=E4[h][:, :], op=ALU.mult),
                                             reads=["knq%d_%d" % (h, bp), "E4_%d" % h], writes=["qdec%d" % h])
                                        p.op("dve", lambda h=h: nc.vector.tensor_tensor(out=R(WW[h][1][:, 0:128]), in0=identf, in1=WW[h][0][:, 128:256], op=ALU.subtract),
                                             reads=["cst", "WW%d_0" % h], writes=["WW%d_1y" % h])
                                        p.op("pe", lambda h=h, cs=cs: nc.tensor.transpose(out=bk1[h][:, 256:384].bitcast(BF16)[:, 0:128], in_=knq[h][:, 0, cs], identity=identb),
                                             reads=["knq%d_%d" % (h, bp), "cstb"], writes=[k1[h]])
                                        p.op("pe", lambda h=h, cs=cs: nc.tensor.transpose(out=bk1[h][:, 384:512].bitcast(BF16)[:, 0:128], in_=vsT[h][:, cs], identity=identb),
                                             reads=["vsT%d_%d" % (h, bp), "cstb"], writes=[k1[h]])
                                        p.op("act", lambda h=h: nc.scalar.activation(out=kbg[h][:, :], in_=bk1[h][:, 256:384].bitcast(BF16)[:, 0:128], func=AF.Copy, scale=sc[:, 12 + h:13 + h]),
                                             reads=[k1[h], "sc"], writes=["kbg%d" % h])
                                        p.op("dve", lambda h=h: nc.vector.tensor_scalar(out=kdec[h][:, :], in0=bk1[h][:, 256:384].bitcast(BF16)[:, 0:128], scalar1=sc[:, 20 + h:21 + h], scalar2=None, op0=ALU.mult),
                                             reads=[k1[h], "sc"], writes=["kdec%d" % h])
                                        p.op("act", lambda h=h: nc.scalar.activation(out=vb[h][:, :], in_=bk1[h][:, 384:512].bitcast(BF16)[:, 0:128], func=AF.Copy, scale=sc[:, 16 + h:17 + h]),
                                             reads=[k1[h], "sc"], writes=["vb%d" % h])
                                    p.mark('dn_pg_done')
                                    for lev in range(0, 7):
                                        for h in range(4):
                                            cur, nxt = WW[h][lev % 2], WW[h][(lev + 1) % 2]
                                            kc_, kn_ = "WW%d_%d" % (h, lev % 2), "WW%d_%d" % (h, (lev + 1) % 2)
                                            kcy, kny = kc_ + "y", kn_ + "y"
                                            if lev == 0:
                                                p.op("pe", lambda h=h, cur=cur: nc.tensor.matmul(bk0[h][:, 128:256], lhsT=R(cur[:, 256:384]), rhs=R(cur[:, 128:256]), start=True, stop=True),
                                                     reads=[kc_], writes=[k0[h]])
                                            elif lev < 6:
                                                p.op("pe", lambda h=h, cur=cur: nc.tensor.matmul(bk0[h][:, 0:256], lhsT=R(cur[:, 256:384]), rhs=R(cur[:, 0:256]), start=True, stop=True),
                                                     reads=[kc_, kcy], writes=[k0[h]], cost=0.11)
                                            else:
                                                p.op("pe", lambda h=h, cur=cur: nc.tensor.matmul(bk0[h][:, 0:128], lhsT=R(cur[:, 256:384]), rhs=R(cur[:, 0:128]), start=True, stop=True),
                                                     reads=[kc_, kcy], writes=[k0[h]])
                                            if lev < 6:
                                                p.op("pe", lambda h=h, cur=cur: nc.tensor.matmul(bk0[h][:, 256:384], lhsT=R(cur[:, 128:256]), rhs=R(cur[:, 256:384]), start=True, stop=True),
                                                     reads=[kc_], writes=[k0[h]])
                                                if (lev + h) % 3 == 0:
                                                    p.op("dve", lambda h=h, nxt=nxt: nc.vector.tensor_copy(out=R(nxt[:, 128:384]), in_=bk0[h][:, 128:384]), reads=[k0[h]], writes=[kn_])
                                                else:
                                                    p.op("act", lambda h=h, nxt=nxt: nc.scalar.copy(out=R(nxt[:, 128:384]), in_=bk0[h][:, 128:384]), reads=[k0[h]], writes=[kn_])
                                            if 1 <= lev < 6:
                                                p.op("dve", lambda h=h, nxt=nxt, cur=cur: nc.vector.tensor_tensor(out=R(nxt[:, 0:128]), in0=bk0[h][:, 0:128], in1=cur[:, 0:128], op=ALU.add),
                                                     reads=[k0[h], kcy], writes=[kny])
                                            elif lev == 6:
                                                p.op("dve", lambda h=h, cur=cur: nc.vector.tensor_tensor(out=Tt[h][:, :], in0=bk0[h][:, 0:128], in1=cur[:, 0:128], op=ALU.add),
                                                     reads=[k0[h], kcy], writes=["Tt%d" % h])
                                    p.mark('dn_inv_done')
                                    for h in range(4):
                                        p.op("pe", lambda h=h: nc.tensor.matmul(bk0[h][:, 0:128], lhsT=Tt[h][:, :], rhs=vb[h][:, :], start=True, stop=True),
                                             reads=["Tt%d" % h, "vb%d" % h], writes=[k0[h]])
                                        p.op("pe", lambda h=h: nc.tensor.matmul(bk0[h][:, 128:256], lhsT=kbg[h][:, :], rhs=Tt[h][:, :], start=True, stop=True),
                                             reads=["Tt%d" % h, "kbg%d" % h], writes=[k0[h]])
                                        p.op("act", lambda h=h: nc.scalar.copy(out=uu[h][:, :], in_=bk0[h][:, 0:128]), reads=[k0[h]], writes=["uu%d" % h])
                                        p.op("dve", lambda h=h: nc.vector.tensor_copy(out=wT[h][:, :], in_=bk0[h][:, 128:256]), reads=[k0[h]], writes=["wT%d" % h])
                                    for h in range(4):
                                        p.op("pe", lambda h=h: nc.tensor.matmul(bk1[h][:, 0:128], lhsT=wT[h][:, :], rhs=Sb[h][:, :], start=True, stop=True),
                                             reads=["wT%d" % h, "Sb%d" % h], writes=[k1[h]])
                                        p.op("dve", lambda h=h: nc.vector.scalar_tensor_tensor(out=vnew[h][:, :], in0=bk1[h][:, 0:128], scalar=-1.0, in1=uu[h][:, :],
                                                                                              op0=ALU.mult, op1=ALU.add),
                                             reads=[k1[h], "uu%d" % h], writes=["vnew%d" % h])
                                    for h in range(4):
                                        p.op("pe", lambda h=h: nc.tensor.matmul(bk1[h][:, 128:256], lhsT=Sb[h][:, :], rhs=qdec[h][:, :], start=True, stop=False),
                                             reads=["Sb%d" % h, "qdec%d" % h], writes=[k1[h]])
                                        p.op("pe", lambda h=h: nc.tensor.matmul(bk1[h][:, 128:256], lhsT=vnew[h][:, :], rhs=M3[h][:, :], start=False, stop=True),
                                             reads=["vnew%d" % h, "M3_%d" % h], writes=[k1[h]])
                                        p.op("act", lambda h=h, ob=ob, cs=cs: nc.scalar.copy(out=ob[:, h, cs], in_=bk1[h][:, 128:256]), reads=[k1[h]], writes=[kob])
                                        p.op("pe", lambda h=h: nc.tensor.matmul(bk1[h][:, 256:384], lhsT=kdec[h][:, :], rhs=vnew[h][:, :], start=True, stop=True),
                                             reads=["kdec%d" % h, "vnew%d" % h], writes=[k1[h]])
                                        p.op("pool", lambda h=h: nc.gpsimd.tensor_scalar(out=Sf[h][:, :], in0=Sf[h][:, :], scalar1=E4[h][:, LAST:LAST + 1], scalar2=None, op0=ALU.mult),
                                             reads=["Sf%d" % h, "E4_%d" % h], writes=["Sf%d" % h])
                                        p.op("dve", lambda h=h: nc.vector.tensor_tensor(out=Sf[h][:, :], in0=bk1[h][:, 256:384], in1=Sf[h][:, :], op=ALU.add),
                                             reads=[k1[h], "Sf%d" % h], writes=["Sf%d" % h])
                                        p.op("pool", lambda h=h: nc.gpsimd.tensor_copy(out=Sb[h][:, :], in_=Sf[h][:, :]), reads=["Sf%d" % h], writes=["Sb%d" % h])
                                p.op("pool", lambda ob=ob, gt0=gt0: nc.gpsimd.dma_start(
                                    out=oT[d, :, gt0:gt0 + TB].rearrange("(h p) t -> p h t", p=128), in_=ob[:, :, :]),
                                    reads=[kob], writes=[("oT%d" % d, gblk)], dma=True)
                        for bi in range(nb):
                            _blk(bi, bi % 2, knq_r[bi % 2], vsT_r[bi % 2])
                    p.flush()
            if stop_after in ("p2", "p3"):
                break
            with contextlib.ExitStack() as st:
                onesb = cstb[:, C_ONE:C_ONE + 128]
                waT = SB(st, "waT", [128, 4, D], BF16)
                wbT = SB(st, "wbT", [128, 4, D], BF16)
                woT = SB(st, "woT", [128, 8, D], BF16)
                GTb = SB(st, "GTb", [128, 8 * 14 * 64], BF16)
                dng = SB(st, "dng", [128, 1], F32)
                p.op("pool", lambda: nc.gpsimd.dma_start(out=waT[:, :, :], in_=w_a[l].rearrange("(c p) m -> p c m", p=128)), writes=["waT"], dma=True)
                p.op("pool", lambda: nc.gpsimd.dma_start(out=wbT[:, :, :], in_=w_b[l].rearrange("(h d) m -> d h m", d=128)), writes=["wbT"], dma=True)
                for m in range(8):
                    p.op("pool", lambda m=m: nc.gpsimd.dma_start(out=woT[:, m, :], in_=w_o[l, m * 128:(m + 1) * 128, :]), writes=["woT"], dma=True)
                for q4 in range(4):
                    p.op("pool", lambda q4=q4: nc.gpsimd.dma_start(out=GTb[:, q4 * 1792:(q4 + 1) * 1792], in_=rpbT[l, :, q4 * 1792:(q4 + 1) * 1792]),
                         writes=["GTb"], dma=True)
                p.op("sp", lambda: nc.sync.dma_start(out=dng[:, :], in_=dn_g[l].rearrange("(p o) -> p o", o=1), allow_slow_non_contiguous=True),
                     writes=["dng"], dma=True)
                GT4 = GTb[:, :].rearrange("p (h m q) -> p h m q", h=8, m=14)
                qbd_r = [SB(st, "qbd%d" % i, [128, 4, 8, 128], BF16) for i in range(2)]
                for i_ in range(2):
                    p.op("pool", lambda i_=i_: nc.gpsimd.memset(qbd_r[i_][:, :, :, :], 0.0), writes=["qbd_%d" % i_])
                kTw_r = [SB(st, "kTw%d" % i, [128, 4, 1024], BF16) for i in range(2)]
                vw = [SB(st, "vw%d" % i, [128, 8, 512], BF16) for i in range(2)]
                zat_r = [SB(st, "zat%d" % i, [128, 4, TB], BF16) for i in range(2)]
                oag = SB(st, "oag", [128, 4, TB], BF16)
                pT = SB(st, "pT", [128, 4, 8, 64], BF16)
                rcp = SB(st, "rcp", [128, 512], F32)
                otmp = SB(st, "otmp", [128, 256], F32)
                ofr = Ring([SB(st, "of%d" % i, [128, TB], F32) for i in range(2)], "of")
                obr4 = Ring([SB(st, "ob4%d" % i, [128, TB], F32) for i in range(2)], "ob4")
                osum = SB(st, "osum", [128, TB], F32)
                osq = SB(st, "osq", [128, TB], BF16)
                orst = SB(st, "orst", [128, TB], F32)
                zbt_r = [SB(st, "zbt%d" % i, [128, 4, TB], BF16) for i in range(2)]
                obg = SB(st, "obg", [128, 4, TB], BF16)
                gar = Ring([SB(st, "ga%d" % i, [128, TB], BF16) for i in range(2)], "ga")
                gbr = Ring([SB(st, "gb%d" % i, [128, TB], BF16) for i in range(2)], "gb")
                t1 = SB(st, "t1", [128, TB], F32)
                t2 = SB(st, "t2", [128, TB], F32)
                mrg = SB(st, "mrg", [128, 8, TB], BF16)
                xrr = Ring([SB(st, "xr%d" % i, [128, D], F32) for i in range(2)], "xr")
                outr4 = Ring([SB(st, "o4_%d" % i, [128, 512], F32) for i in range(2)], "o4")
                Sr = Ring([PS(st, "P_S%d" % i, [128, 512], F32) for i in range(3)], "P_S")
                SUMp = PS(st, "P_SUM", [128, 512], F32)
                OTp = PS(st, "P_OT", [128, 512], F32)
                acc4 = Ring([PS(st, "P_acc%d" % i, [128, 512], F32) for i in range(3)], "P_acc")

                for si, T in enumerate(seqs):
                    rows = T // 64
                    xsrc = xin[si] if l == 0 else x1[bases[si]:bases[si] + T, :]
                    dst = yout[si] if l == L - 1 else x1[bases[si]:bases[si] + T, :]
                    def _blk4(b, bp, qbd, kTw, zat, zbt, si=si, T=T, rows=rows, xsrc=xsrc, dst=dst):
                            t0 = b * TB
                            gt0 = bases[si] + t0
                            gblk = gt0 // TB
                            r0 = b * 8
                            wlo = min(max(r0 - 4, 0), rows - 8)
                            wtok = wlo * 64
                            nk = min(T - wtok, 1024)
                            kblks = sorted(set((bases[si] + wtok + i) // TB for i in range(0, nk, 64)))
                            for h2 in range(2):
                                for c4 in range(4):
                                    p.op("sp", lambda gt0=gt0, h2=h2, c4=c4: nc.sync.dma_start(
                                        out=qbd[h2 * 64:(h2 + 1) * 64, c4, :, h2 * 64:(h2 + 1) * 64],
                                        in_=qkT[c4 * 128 + h2 * 64:c4 * 128 + (h2 + 1) * 64, gt0:gt0 + TB].rearrange("p (r q) -> p r q", q=64)),
                                        reads=[("qkT", gblk)], writes=["qbd_%d" % bp], dma=True)
                            p.op("sp", lambda si=si, wtok=wtok, nk=nk: nc.sync.dma_start(
                                out=kTw[:, :, 0:nk], in_=qkT[512:1024, bases[si] + wtok:bases[si] + wtok + nk].rearrange("(c p) t -> p c t", p=128)),
                                reads=[("qkT", kb) for kb in kblks], writes=["kTw_%d" % bp], dma=True)
                            for par in range(2):
                                vstart = wtok + par * 64
                                nfull = min(T - vstart, 1024) // 128
                                p.op("sp", lambda si=si, par=par, vstart=vstart, nfull=nfull: nc.sync.dma_start(
                                    out=vw[par][:, 0:nfull, :],
                                    in_=va[bases[si] + vstart:bases[si] + vstart + nfull * 128, :].rearrange("(s p) f -> p s f", p=128)),
                                    reads=[("va", kb) for kb in kblks], writes=["vw%d" % par], dma=True)
                            p.op("sp", lambda gt0=gt0: nc.sync.dma_start(out=zat[:, :, :], in_=zaT[:, gt0:gt0 + TB].rearrange("(c p) t -> p c t", p=128)),
                                 reads=[("zaT", gblk)], writes=["zat_%d" % bp], dma=True)
                            p.op("sp", lambda gt0=gt0: nc.sync.dma_start(out=zbt[:, :, :], in_=zbT[:, gt0:gt0 + TB].rearrange("(h d) t -> d h t", d=128)),
                                 reads=[("zbT", gblk)], writes=["zbt_%d" % bp], dma=True)
                            for h in range(4):
                                of_, kof = ofr.next()
                                ob_, kob4 = obr4.next()
                                p.op("sp", lambda h=h, of_=of_, gt0=gt0: nc.sync.dma_start(out=of_[:, :], in_=oT[0, h * 128:(h + 1) * 128, gt0:gt0 + TB]),
                                     reads=[("oT0", gblk)], writes=[kof], dma=True)
                                p.op("sp", lambda h=h, ob_=ob_, gt0=gt0: nc.sync.dma_start(out=ob_[:, :], in_=oT[1, h * 128:(h + 1) * 128, gt0:gt0 + TB]),
                                     reads=[("oT1", gblk)], writes=[kob4], dma=True)
                                p.op("pool", lambda of_=of_, ob_=ob_: nc.gpsimd.tensor_tensor(out=osum[:, :], in0=of_[:, :], in1=ob_[:, :], op=ALU.add),
                                     reads=[kof, kob4], writes=["osum"])
                                p.op("act", lambda: nc.scalar.activation(out=osq[:, :], in_=osum[:, :], func=AF.Square), reads=["osum"], writes=["osq"])
                                acc, kacc = acc4.next()
                                p.op("pe", lambda acc=acc: nc.tensor.matmul(acc[:, :], lhsT=onesb, rhs=osq[:, :], start=True, stop=True),
                                     reads=["osq"], writes=[kacc], cost=0.23)
                                p.op("dve", lambda acc=acc: nc.vector.tensor_scalar(out=orst[:, :], in0=acc[:, :], scalar1=1.0 / 128, scalar2=EPS, op0=ALU.mult, op1=ALU.add),
                                     reads=[kacc], writes=["orst"])
                                p.op("act", lambda: nc.scalar.activation(out=orst[:, :], in_=orst[:, :], func=AF.Ln), reads=["orst"], writes=["orst"], cost=0.57)
                                p.op("act", lambda: nc.scalar.activation(out=orst[:, :], in_=orst[:, :], func=AF.Exp, scale=-0.5), reads=["orst"], writes=["orst"], cost=0.57)
                                p.op("dve", lambda: nc.vector.scalar_tensor_tensor(out=osum[:, :], in0=osum[:, :], scalar=dng[:, 0:1], in1=orst[:, :], op0=ALU.mult, op1=ALU.mult),
                                     reads=["osum", "orst", "dng"], writes=["osum"])
                                p.op("pool", lambda h=h: nc.gpsimd.tensor_tensor(out=obg[:, h, :], in0=osum[:, :], in1=zbt[:, h, :], op=ALU.mult),
                                     reads=["osum", "zbt_%d" % bp], writes=["obg"])
                            for rr in range(8):
                                r = r0 + rr
                                rs = min(max(r - 4, 0), rows - 8)
                                o_ = r - rs
                                m0 = 7 - o_
                                par = (rs - wlo) % 2
                                slot0 = (rs - wlo - par) // 2
                                koff = (rs - wlo) * 64
                                qs = slice(rr * 64, (rr + 1) * 64)
                                for pr in range(4):
                                    S_, kS = Sr.next()
                                    p.op("pe", lambda S_=S_, pr=pr, m0=m0: nc.tensor.matmul(
                                        S_[:, :], lhsT=identb, rhs=GT4[:, 2 * pr:2 * pr + 2, m0:m0 + 7:2, :].rearrange("p h k q -> p k h q"), start=True, stop=False),
                                        reads=["GTb"], writes=[kS], cost=0.23)
                                    for kk in range(4):
                                        p.op("pe", lambda S_=S_, pr=pr, kk=kk, koff=koff, rr=rr: nc.tensor.matmul(
                                            S_[:, kk * 128:(kk + 1) * 128], lhsT=kTw[:, pr, koff + kk * 128:koff + (kk + 1) * 128],
                                            rhs=qbd[:, pr, rr, :], start=False, stop=(kk == 3)),
                                            reads=["kTw_%d" % bp, "qbd_%d" % bp], writes=[kS])
                                    p.op("act", lambda S_=S_, pr=pr: nc.scalar.activation(
                                        out=pT[:, :, 2 * pr:2 * pr + 2, :], in_=S_[:, :].rearrange("p (k h q) -> p k h q", k=4, h=2), func=AF.Exp),
                                        reads=[kS], writes=["pT"], cost=0.57)
                                for kk in range(4):
                                    p.op("pe", lambda kk=kk: nc.tensor.matmul(SUMp[:, :], lhsT=onesb, rhs=pT[:, kk, :, :], start=(kk == 0), stop=(kk == 3)),
                                         reads=["pT"], writes=["P_SUM"], cost=0.23)
                                for h in range(8):
                                    for kk in range(4):
                                        p.op("pe", lambda h=h, kk=kk, par=par, slot0=slot0: nc.tensor.matmul(
                                            OTp[(h % 2) * 64:(h % 2) * 64 + 64, (h // 2) * 64:(h // 2) * 64 + 64],
                                            lhsT=vw[par][:, slot0 + kk, h * 64:(h + 1) * 64], rhs=pT[:, kk, h, :],
                                            start=(kk == 0), stop=(kk == 3)),
                                            reads=["pT", "vw%d" % par], writes=["P_OT"])
                                p.op("act", lambda: nc.scalar.activation(out=rcp[:, :], in_=SUMp[:, :], func=AF.Ln), reads=["P_SUM"], writes=["rcp"], cost=0.57)
                                p.op("act", lambda: nc.scalar.activation(out=rcp[:, :], in_=rcp[:, :], func=AF.Exp, scale=-1.0), reads=["rcp"], writes=["rcp"], cost=0.57)
                                for h2 in range(2):
                                    hs = slice(h2 * 64, (h2 + 1) * 64)
                                    p.op("dve", lambda h2=h2, hs=hs: nc.vector.tensor_tensor(
                                        out=otmp[hs, :].rearrange("p (c q) -> p c q", c=4), in0=OTp[hs, 0:256].rearrange("p (c q) -> p c q", c=4),
                                        in1=rcp[hs, :].rearrange("p (c h q) -> p c h q", c=4, h=2)[:, :, h2, :], op=ALU.mult),
                                        reads=["P_OT", "rcp"], writes=["otmp"])
                                p.op("pool", lambda qs=qs: nc.gpsimd.tensor_tensor(out=oag[:, :, qs], in0=otmp[:, :].rearrange("p (c q) -> p c q", c=4), in1=zat[:, :, qs], op=ALU.mult),
                                     reads=["otmp", "zat_%d" % bp], writes=["oag"])
                            for m in range(8):
                                ms = slice(m * 128, (m + 1) * 128)
                                ga_, kga = gar.next()
                                gb_, kgb = gbr.next()
                                p.op("sp", lambda ga_=ga_, m=m, gt0=gt0: nc.sync.dma_start(out=ga_[:, :], in_=gaT[m * 128:(m + 1) * 128, gt0:gt0 + TB]),
                                     reads=[("gaT", gblk)], writes=[kga], dma=True)
                                p.op("sp", lambda gb_=gb_, m=m, gt0=gt0: nc.sync.dma_start(out=gb_[:, :], in_=gbT[m * 128:(m + 1) * 128, gt0:gt0 + TB]),
                                     reads=[("gbT", gblk)], writes=[kgb], dma=True)
                                ya, kya = acc4.next()
                                for h in range(4):
                                    p.op("pe", lambda ya=ya, h=h, ms=ms: nc.tensor.matmul(ya[:, :], lhsT=waT[:, h, ms], rhs=oag[:, h, :], start=(h == 0), stop=(h == 3)),
                                         reads=["waT", "oag"], writes=[kya], cost=0.23)
                                yb, kyb = acc4.next()
                                for h in range(4):
                                    p.op("pe", lambda yb=yb, h=h, ms=ms: nc.tensor.matmul(yb[:, :], lhsT=wbT[:, h, ms], rhs=obg[:, h, :], start=(h == 0), stop=(h == 3)),
                                         reads=["wbT", "obg"], writes=[kyb], cost=0.23)
                                p.op("dve", lambda ya=ya, ga_=ga_: nc.vector.tensor_tensor(out=t1[:, :], in0=ya[:, :], in1=ga_[:, :], op=ALU.mult),
                                     reads=[kya, kga], writes=["t1"])
                                p.op("dve", lambda yb=yb, gb_=gb_: nc.vector.tensor_tensor(out=t2[:, :], in0=yb[:, :], in1=gb_[:, :], op=ALU.mult),
                                     reads=[kyb, kgb], writes=["t2"])
                                p.op("pool", lambda m=m: nc.gpsimd.tensor_tensor(out=mrg[:, m, :], in0=t1[:, :], in1=t2[:, :], op=ALU.add),
                                     reads=["t1", "t2"], writes=["mrg"])
                            for tt in range(4):
                                xr_, kxr = xrr.next()
                                srck = [] if l == 0 else [("x1", gblk)]
                                p.op("sp", lambda xr_=xr_, tt=tt, t0=t0, xsrc=xsrc: nc.sync.dma_start(out=xr_[:, :], in_=xsrc[t0 + tt * 128:t0 + (tt + 1) * 128, :]),
                                     reads=srck, writes=[kxr], dma=True)
                                for eh in range(2):
                                    po, kpo = acc4.next()
                                    for m in range(8):
                                        p.op("pe", lambda po=po, m=m, tt=tt, eh=eh: nc.tensor.matmul(
                                            po[:, :], lhsT=mrg[:, m, tt * 128:(tt + 1) * 128], rhs=woT[:, m, eh * 512:(eh + 1) * 512], start=(m == 0), stop=(m == 7)),
                                            reads=["mrg", "woT"], writes=[kpo], cost=0.23)
                                    o4, ko4 = outr4.next()
                                    p.op("dve", lambda po=po, o4=o4, xr_=xr_, eh=eh: nc.vector.tensor_tensor(out=o4[:, :], in0=po[:, :], in1=xr_[:, eh * 512:(eh + 1) * 512], op=ALU.add),
                                         reads=[kpo, kxr], writes=[ko4])
                                    wk = [("x1", gblk)] if l < L - 1 else [("y", si, b)]
                                    p.op("pool", lambda o4=o4, tt=tt, eh=eh, t0=t0, dst=dst: nc.gpsimd.dma_start(
                                        out=dst[t0 + tt * 128:t0 + (tt + 1) * 128, eh * 512:(eh + 1) * 512], in_=o4[:, :]),
                                        reads=[ko4], writes=wk, dma=True)
                    for b in range(T // TB):
                        _blk4(b, b % 2, qbd_r[b % 2], kTw_r[b % 2], zat_r[b % 2], zbt_r[b % 2])
                p.flush()

        p.flush()
    return nc


def _rpb_table(rpb):
    L = rpb.shape[0]
    krl = (np.arange(128) // 64)[:, None, None]
    kc = (np.arange(128) % 64)[:, None, None]
    m = np.arange(14)[None, :, None]
    qc = np.arange(64)[None, None, :]
    cs = np.clip(qc - 8, 0, 48)
    valid = np.broadcast_to((kc >= cs) & (kc < cs + 16), (128, 14, 64))
    dc = np.clip(kc - qc + 15, 0, 30)
    out = np.empty((L, 128, 8, 14, 64), np.float32)
    for l in range(L):
        for h in range(8):
            out[l, :, h] = np.where(valid, rpb[l, h][(m + krl), dc], np.float32(NEG))
    return out.reshape(L, 128, 8 * 14 * 64)


_PROG_CACHE = {}


def kernel(x_prompt, x_sample, norm_g, w_in, attn_q_norm_g, attn_k_norm_g, attn_rpb, dn_conv_w,
           dn_a_log, dn_dt_bias, dn_norm_g, w_branch_a, w_branch_b, w_out):
    f = lambda a: np.ascontiguousarray(np.asarray(a, dtype=np.float32))
    x_prompt, x_sample = f(x_prompt), f(x_sample)
    n = 8
    Ts, Tp = x_sample.shape[1], x_prompt.shape[1]
    L = w_in.shape[0]
    key = (Ts, Tp, L)
    if key not in _PROG_CACHE:
        _PROG_CACHE[key] = build_program([Ts, Tp], n_layers=L)
    nc = _PROG_CACHE[key]
    shared = dict(w_in=f(w_in), w_a=f(w_branch_a), w_b=f(w_branch_b), w_o=f(w_out), norm_g=f(norm_g),
                  qg=f(attn_q_norm_g), kg=f(attn_k_norm_g), conv_w=f(dn_conv_w),
                  a_log=f(dn_a_log).reshape(L, 8), dt_bias=f(dn_dt_bias).reshape(L, 8), dn_g=f(dn_norm_g),
                  rpbT=_rpb_table(f(attn_rpb)), consts=make_consts())
    nP = x_prompt.shape[0]
    in_maps = []
    for i in range(n):
        d = dict(shared)
        d["x0"] = x_sample[i]
        d["x1"] = x_prompt[i % nP]
        in_maps.append(d)
    res = run_bass_kernel_spmd(nc, in_maps, core_ids=list(range(n)))
    y_sample = np.stack([np.asarray(res.results[i]["y0"], dtype=np.float32) for i in range(n)], 0)
    y_prompt = np.stack([np.asarray(res.results[i]["y1"], dtype=np.float32) for i in range(nP)], 0)
    return (y_prompt, y_sample)
```

```python
import contextlib
import numpy as np
import concourse.bass as bass
import concourse.mybir as mybir
from concourse.bass_utils import run_bass_kernel_spmd

F32 = mybir.dt.float32
BF16 = mybir.dt.bfloat16
F32R = mybir.dt.float32r
USE_F32R = True


def R(ap):
    return ap.bitcast(F32R) if USE_F32R else ap
AF = mybir.ActivationFunctionType
ALU = mybir.AluOpType

D = 1024
DIN = 6160
EPS = 1e-6
NEG = -30000.0
BIG = 65536.0
TB = 512

C_ID = 0
C_UT = 128
C_LT = 256
C_BD = 384
C_ONE = 512
C_M_FWD = 640
C_M_BWD = 1152
NCONST = 1664


def make_consts():
    c = np.zeros((128, NCONST), np.float32)
    p = np.arange(128)[:, None]
    f = np.arange(128)[None, :]
    c[:, C_ID:C_ID + 128] = (p == f)
    c[:, C_UT:C_UT + 128] = (p <= f)
    c[:, C_LT:C_LT + 128] = (p >= f)
    c[:, C_BD:C_BD + 128] = ((p // 64) == (f // 64))
    c[:, C_ONE:C_ONE + 128] = 1.0
    c[:, C_M_FWD + 128:C_M_FWD + 256] = BIG * (f >= p)
    c[:, C_M_FWD + 256:C_M_FWD + 384] = -BIG * (f <= p)
    c[:, C_M_FWD + 384:C_M_FWD + 512] = -BIG * (f < p)
    c[:, C_M_BWD + 128:C_M_BWD + 256] = BIG * (f <= p)
    c[:, C_M_BWD + 256:C_M_BWD + 384] = -BIG * (f >= p)
    c[:, C_M_BWD + 384:C_M_BWD + 512] = -BIG * (f > p)
    return c


class Prog:
    ENG = ("pe", "act", "dve", "pool", "sp")
    NDS = 12

    def __init__(self, nc, stack):
        self.nc = nc
        self.ops = []
        self.lastw = {}
        self.readers = {}
        self.eng_map = {"pe": nc.tensor, "act": nc.scalar, "dve": nc.vector, "pool": nc.gpsimd, "sp": nc.sync}
        self.sems = {e: stack.enter_context(nc.semaphore("s_" + e)) for e in self.ENG}
        self.dsems = {e: [stack.enter_context(nc.semaphore("d_%s%d" % (e, i))) for i in range(self.NDS)] for e in ("sp", "pool")}
        self.cnt = {e: 0 for e in self.ENG}
        self.dcnt = {e: [0] * self.NDS for e in self.dsems}
        self.dnext = {e: 0 for e in self.dsems}
        self.sig = {}
        self.waited = {e: {} for e in self.ENG}
        self.emitted = 0
        self.n_instr = 0
        self.costs = []

    max_ops = None
    marks = {}
    attach_waits = True
    SYNC_LAT = 0.3

    def mark(self, name):
        self.marks.setdefault(name, len(self.ops))

    DEF_COST = {"pe": 0.07, "act": 0.40, "dve": 0.30, "pool": 0.6, "sp": 0.15}

    def op(self, eng, fn, reads=(), writes=(), dma=False, cost=None):
        if self.max_ops is not None and len(self.ops) >= self.max_ops:
            return -1
        self.costs.append(cost if cost is not None else ((0.8 if eng == "pool" else 0.15) if dma else self.DEF_COST[eng]))
        ex = [k for k in reads if isinstance(k, str) and k.startswith(("bk", "tp#", "acc#", "ssp#", "gbp", "P_"))]
        if ex:
            reads = [k for k in reads if k not in ex]
            writes = list(writes) + ex
        deps = set()
        for k in reads:
            w = self.lastw.get(k)
            if w is not None:
                deps.add(w)
        for k in writes:
            w = self.lastw.get(k)
            if w is not None:
                deps.add(w)
            rs = self.readers.get(k)
            if rs:
                deps.update(rs)
        idx = len(self.ops)
        self.ops.append((eng, fn, sorted(deps), dma))
        for k in reads:
            self.readers.setdefault(k, []).append(idx)
        for k in writes:
            self.lastw[k] = idx
            self.readers[k] = []
        return idx

    def flush(self):
        nc = self.nc
        ops = self.ops
        start, n = self.emitted, len(ops)
        import heapq
        costs = self.costs
        succ = {}
        indeg = {}
        for i in range(start, n):
            k = 0
            for d in ops[i][2]:
                if d >= start:
                    succ.setdefault(d, []).append(i)
                    k += 1
            indeg[i] = k
        blev = {}
        for i in range(n - 1, start - 1, -1):
            m_ = 0.0
            for j in succ.get(i, ()):
                if blev[j] > m_:
                    m_ = blev[j]
            blev[i] = m_ + costs[i] + (3.0 if ops[i][3] else 0.0)
        future = {e: [] for e in self.ENG}
        avail = {e: [] for e in self.ENG}
        ready = {}
        for i in range(start, n):
            if indeg[i] == 0:
                heapq.heappush(avail[ops[i][0]], (-blev[i], i))
        free = {e: 0.0 for e in self.ENG}
        finish = {}
        order = []
        remaining = n - start
        while remaining:
            best = None
            for e in self.ENG:
                if avail[e]:
                    st_ = free[e]
                elif future[e]:
                    st_ = max(free[e], future[e][0][0])
                else:
                    continue
                if best is None or st_ < best[0]:
                    best = (st_, e)
            st_, e = best
            fu = future[e]
            while fu and fu[0][0] <= st_:
                r_, j = heapq.heappop(fu)
                heapq.heappush(avail[e], (-blev[j], j))
            _, i = heapq.heappop(avail[e])
            dma = ops[i][3]
            free[e] = st_ + costs[i]
            finish[i] = st_ + (costs[i] + 3.0 if dma else costs[i])
            order.append(i)
            remaining -= 1
            for j in succ.get(i, ()):
                r_ = ready.get(j, 0.0)
                lat = 0.0 if (ops[j][0] == "pe" and e == "pe") else self.SYNC_LAT
                if finish[i] + lat > r_:
                    ready[j] = r_ = finish[i] + lat
                indeg[j] -= 1
                if indeg[j] == 0:
                    heapq.heappush(future[ops[j][0]], (r_, j))
        self.sim_time = max(finish.values()) if finish else 0.0
        need = {}
        lastop = {}
        for i in order:
            e, fn, deps, dma = ops[i]
            for d in deps:
                if e == "pe" and ops[d][0] == "pe":
                    continue
                if d >= start:
                    need[d] = True
                else:
                    assert d in self.sig, "cross-phase dep on unsignalled op"
            lastop[e] = i
            if dma:
                need[i] = True
        for j in lastop.values():
            need[j] = True
        NDS = self.NDS
        sems, dsems, cnt, dcnt, dnext, sig, waited = self.sems, self.dsems, self.cnt, self.dcnt, self.dnext, self.sig, self.waited
        for i in order:
            e, fn, deps, dma = ops[i]
            eo = self.eng_map[e]
            ws = {}
            for d in deps:
                if e == "pe" and ops[d][0] == "pe":
                    continue
                s, v = sig[d]
                key = id(s)
                if key not in ws or ws[key][1] < v:
                    ws[key] = (s, v)
            if dma:
                di = dnext[e] % NDS
                dnext[e] += 1
                ds = dsems[e][di]
                if dcnt[e][di] > 0:
                    key = id(ds)
                    v = dcnt[e][di]
                    if key not in ws or ws[key][1] < v:
                        ws[key] = (ds, v)
            wd = waited[e]
            pend = []
            for key, (s, v) in ws.items():
                if wd.get(key, 0) >= v:
                    continue
                wd[key] = v
                pend.append((s, v))
            for (s, v) in pend[1:]:
                eo.wait_ge(s, v)
                self.n_instr += 1
            ins = fn()
            if pend:
                if self.attach_waits:
                    ins._wait_ge(pend[0][0], pend[0][1])
                else:
                    eo.wait_ge(pend[0][0], pend[0][1])
            self.n_instr += 1
            if need.get(i):
                if dma:
                    dcnt[e][di] += 16
                    ins.then_inc(ds, 16)
                    sig[i] = (ds, dcnt[e][di])
                else:
                    cnt[e] += 1
                    ins.then_inc(sems[e], 1)
                    sig[i] = (sems[e], cnt[e])
            ops[i] = (e, None, None, dma)
        allsems = [(sems[x], cnt[x]) for x in self.ENG if cnt[x] > 0]
        for x in dsems:
            allsems += [(dsems[x][k], dcnt[x][k]) for k in range(NDS) if dcnt[x][k] > 0]
        for x in self.ENG:
            for (s_, v_) in allsems:
                if waited[x].get(id(s_), 0) < v_:
                    waited[x][id(s_)] = v_
                    self.eng_map[x].wait_ge(s_, v_)
                    self.n_instr += 1
        self.emitted = n
        self.lastw = {k: v for k, v in self.lastw.items() if isinstance(k, tuple)}
        self.readers = {k: v for k, v in self.readers.items() if isinstance(k, tuple)}


class Ring:
    def __init__(self, tiles, name):
        self.tiles = tiles
        self.name = name
        self.i = 0

    def next(self):
        k = self.i % len(self.tiles)
        self.i += 1
        return self.tiles[k], "%s#%d" % (self.name, k)


def build_program(seqs, n_layers=2, debug=False, stop_after=None, max_ops=None):
    nc = bass.Bass("TRN2", target_bir_lowering=False)
    L = n_layers
    Ttot = sum(seqs)
    bases = [sum(seqs[:i]) for i in range(len(seqs))]
    dk = "ExternalOutput" if debug else "Internal"

    def din(name, shape, dt=F32):
        return nc.dram_tensor(name, shape, dt, kind="ExternalInput").ap()

    xin = [din("x%d" % i, [T, D]) for i, T in enumerate(seqs)]
    yout = [nc.dram_tensor("y%d" % i, [T, D], F32, kind="ExternalOutput").ap() for i, T in enumerate(seqs)]
    w_in = din("w_in", [L, D, DIN])
    w_a = din("w_a", [L, 512, D])
    w_b = din("w_b", [L, 512, D])
    w_o = din("w_o", [L, D, D])
    norm_g = din("norm_g", [L, D])
    qg = din("qg", [L, 64])
    kg = din("kg", [L, 64])
    conv_w = din("conv_w", [L, 5, 1536])
    a_log = din("a_log", [L, 8])
    dt_bias = din("dt_bias", [L, 8])
    dn_g = din("dn_g", [L, 128])
    rpbT = din("rpbT", [L, 128, 8 * 14 * 64])
    consts = din("consts", [128, NCONST])

    def scr(name, shape, dt):
        return nc.dram_tensor(name, shape, dt, kind=dk).ap()

    qkT = scr("qkT", [1024, Ttot], BF16)
    va = scr("va", [Ttot, 512], BF16)
    zaT = scr("zaT", [512, Ttot], BF16)
    qkvbT = scr("qkvbT", [1536, Ttot], BF16)
    zbT = scr("zbT", [512, Ttot], BF16)
    gbs = scr("gbs", [Ttot, 16], F32)
    gaT = scr("gaT", [1024, Ttot], BF16)
    gbT = scr("gbT", [1024, Ttot], BF16)
    oT = scr("oT", [2, 512, Ttot], F32)
    prepT = scr("prepT", [4, 3, 128, Ttot], BF16)
    x1 = scr("xmid", [Ttot, D], F32)

    with contextlib.ExitStack() as gst:
        p = Prog(nc, gst)
        p.max_ops = max_ops
        uid = [0]

        def SB(st, name, shape, dt):
            uid[0] += 1
            return st.enter_context(nc.sbuf_tensor("%s_u%d" % (name, uid[0]), shape, dt))

        def PS(st, name, shape, dt=F32):
            uid[0] += 1
            return st.enter_context(nc.psum_tensor("%s_u%d" % (name, uid[0]), shape, dt))

        cst = SB(gst, "cst", [128, NCONST], F32)
        cstb = SB(gst, "cstb", [128, NCONST], BF16)
        p.op("sp", lambda: nc.sync.dma_start(out=cst[:], in_=consts), writes=["cst"], dma=True)
        p.op("dve", lambda: nc.vector.tensor_copy(out=cstb[:], in_=cst[:]), reads=["cst"], writes=["cstb"])
        identb = cstb[:, C_ID:C_ID + 128]
        bdb = cstb[:, C_BD:C_BD + 128]

        for l in range(L):
            with contextlib.ExitStack() as st:
                win = SB(st, "win", [128, 8, DIN], BF16)
                gcol = SB(st, "gcol", [128, 8], F32)
                qkgain = SB(st, "qkgain", [128, 2], F32)
                dtb = SB(st, "dtb", [128, 8], F32)
                nega = SB(st, "nega", [128, 8], F32)
                for kc in range(8):
                    p.op("pool", lambda kc=kc: nc.gpsimd.dma_start(out=win[:, kc, :], in_=w_in[l, kc * 128:(kc + 1) * 128, :]),
                         writes=["win%d" % kc], dma=True)
                p.op("sp", lambda: nc.sync.dma_start(out=gcol[:], in_=norm_g[l].rearrange("(kc p) -> p kc", p=128),
                                                     allow_slow_non_contiguous=True), writes=["gcol"], dma=True)
                for hh in range(2):
                    p.op("sp", lambda hh=hh: nc.sync.dma_start(out=qkgain[hh * 64:(hh + 1) * 64, 0:1], in_=qg[l].rearrange("(p o) -> p o", o=1),
                                                               allow_slow_non_contiguous=True), writes=["qkgain"], dma=True)
                    p.op("sp", lambda hh=hh: nc.sync.dma_start(out=qkgain[hh * 64:(hh + 1) * 64, 1:2], in_=kg[l].rearrange("(p o) -> p o", o=1),
                                                               allow_slow_non_contiguous=True), writes=["qkgain"], dma=True)
                p.op("act", lambda: nc.scalar.mul(out=qkgain[:, 0:1], in_=qkgain[:, 0:1], mul=0.125), reads=["qkgain"], writes=["qkgain"])
                p.op("sp", lambda: nc.sync.dma_start(out=dtb[:], in_=dt_bias[l:l + 1, :].broadcast_to([128, 8])), writes=["dtb"], dma=True)
                p.op("sp", lambda: nc.sync.dma_start(out=nega[:], in_=a_log[l:l + 1, :].broadcast_to([128, 8])), writes=["nega"], dma=True)
                p.op("act", lambda: nc.scalar.activation(out=nega[:], in_=nega[:], func=AF.Exp), reads=["nega"], writes=["nega"])
                p.op("dve", lambda: nc.vector.tensor_scalar(out=nega[:], in0=nega[:], scalar1=-1.0, scalar2=None, op0=ALU.mult),
                     reads=["nega"], writes=["nega"])

                xr = Ring([SB(st, "xt%d" % i, [128, 4, D], F32) for i in range(2)], "xt")
                xn = SB(st, "xn", [128, 4, D], BF16)
                junk = SB(st, "junk", [128, D], BF16)
                ssr = Ring([SB(st, "ss%d" % i, [128, 8], F32) for i in range(2)], "ss")
                hTr = Ring([SB(st, "hT%d" % i, [128, 8, TB], BF16) for i in range(2)], "hT")
                sqr = Ring([SB(st, "sq%d" % i, [128, TB], BF16) for i in range(2)], "sq")
                rsr = Ring([SB(st, "rs%d" % i, [128, TB], F32) for i in range(2)], "rs")
                sgr = Ring([SB(st, "sg%d" % i, [128, TB], F32) for i in range(5)], "sg")
                outr = Ring([SB(st, "ob%d" % i, [128, TB], BF16) for i in range(10)], "ob")
                gbt = SB(st, "gbt", [128, 4, 16], F32)
                gtmp = SB(st, "gtmp", [128, 4, 16], F32)
                tpr = Ring([PS(st, "tp%d" % i, [128, 2 * TB], BF16)[:, 0:TB] for i in range(2)], "tp")
                accr = Ring([PS(st, "acc%d" % i, [128, TB], F32) for i in range(4)], "acc")
                ssp = Ring([PS(st, "ssp%d" % i, [128, TB], F32) for i in range(1)], "ssp")
                gbp = PS(st, "gbp", [128, 4, 128], F32)[:, :, 0:16]
                winkeys = ["win%d" % kc for kc in range(8)]
                evac_flip = [0]

                def store_fm(dst, row0, nrows, gt0, tile, tkey, dkey):
                    return p.op("pool", lambda: nc.gpsimd.dma_start(out=dst[row0:row0 + nrows, gt0:gt0 + TB], in_=tile[0:nrows, :]),
                                reads=[tkey], writes=[dkey], dma=True)

                for si, T in enumerate(seqs):
                    xsrc = xin[si] if l == 0 else x1[bases[si]:bases[si] + T, :]
                    for b in range(T // TB):
                        t0 = b * TB
                        gt0 = bases[si] + t0
                        gblk = gt0 // TB
                        xt, kx = xr.next()
                        srck = [] if l == 0 else [("x1", gblk)]
                        p.op("sp", lambda xt=xt, t0=t0, xsrc=xsrc: nc.sync.dma_start(
                            out=xt[:, :, :], in_=xsrc[t0:t0 + TB, :].rearrange("(tt p) d -> p tt d", p=128)),
                            reads=srck, writes=[kx], dma=True)
                        ss, kss = ssr.next()
                        for tt in range(4):
                            p.op("act", lambda xt=xt, tt=tt, ss=ss: nc.scalar.activation(
                                out=junk[:, :], in_=xt[:, tt, :], func=AF.Square, accum_out=ss[:, tt:tt + 1]),
                                reads=[kx], writes=["junk", kss])
                        p.op("dve", lambda ss=ss: nc.vector.tensor_scalar(out=ss[:, 4:8], in0=ss[:, 0:4], scalar1=1.0 / D, scalar2=EPS,
                                                                          op0=ALU.mult, op1=ALU.add), reads=[kss], writes=[kss])
                        p.op("act", lambda ss=ss: nc.scalar.activation(out=ss[:, 4:8], in_=ss[:, 4:8], func=AF.Ln), reads=[kss], writes=[kss])
                        p.op("act", lambda ss=ss: nc.scalar.activation(out=ss[:, 4:8], in_=ss[:, 4:8], func=AF.Exp, scale=-0.5), reads=[kss], writes=[kss])
                        for tt in range(4):
                            p.op("act", lambda xt=xt, tt=tt, ss=ss: nc.scalar.activation(
                                out=xn[:, tt, :], in_=xt[:, tt, :], func=AF.Copy, scale=ss[:, 4 + tt:5 + tt]),
                                reads=[kx, kss], writes=["xn"])
                        hT, khT = hTr.next()
                        for kc in range(8):
                            tp, ktp = tpr.next()
                            for tt in range(4):
                                p.op("pe", lambda tp=tp, tt=tt, kc=kc: nc.tensor.transpose(
                                    out=tp[:, tt * 128:(tt + 1) * 128], in_=xn[:, tt, kc * 128:(kc + 1) * 128], identity=identb),
                                    reads=["xn", "cstb"], writes=[ktp])
                            if kc % 2 == 0:
                                p.op("act", lambda tp=tp, kc=kc, hT=hT: nc.scalar.activation(
                                    out=hT[:, kc, :], in_=tp[:, :], func=AF.Copy, scale=gcol[:, kc:kc + 1]),
                                    reads=[ktp, "gcol"], writes=[khT + "k%d" % kc])
                            else:
                                p.op("dve", lambda tp=tp, kc=kc, hT=hT: nc.vector.tensor_scalar(
                                    out=hT[:, kc, :], in0=tp[:, :], scalar1=gcol[:, kc:kc + 1], scalar2=None, op0=ALU.mult),
                                    reads=[ktp, "gcol"], writes=[khT + "k%d" % kc])
                        hkeys = [khT + "k%d" % kc for kc in range(8)]

                        def proj_fm(col0, M, hT=hT, hkeys=hkeys):
                            acc, kacc = accr.next()
                            for kc in range(8):
                                p.op("pe", lambda kc=kc, acc=acc: nc.tensor.matmul(
                                    acc[0:M, :], lhsT=win[:, kc, col0:col0 + M], rhs=hT[:, kc, :], start=(kc == 0), stop=(kc == 7)),
                                    reads=[winkeys[kc], hkeys[kc]], writes=[kacc], cost=0.23)
                            return acc, kacc

                        pending = []

                        def flush():
                            while pending:
                                pending.pop(0)()

                        for c in range(8):
                            acc, kacc = proj_fm(c * 128, 128)
                            flush()
                            sq, ksq = sqr.next()
                            p.op("act", lambda sq=sq, acc=acc: nc.scalar.activation(out=sq[:, :], in_=acc[:, :], func=AF.Square),
                                 reads=[kacc], writes=[ksq])

                            def later(c=c, acc=acc, kacc=kacc, sq=sq, ksq=ksq, gt0=gt0, gblk=gblk):
                                sp_, kssp = ssp.next()
                                p.op("pe", lambda: nc.tensor.matmul(sp_[:, :], lhsT=bdb, rhs=sq[:, :], start=True, stop=True),
                                     reads=[ksq, "cstb"], writes=[kssp], cost=0.23)
                                rs, krs = rsr.next()
                                p.op("dve", lambda: nc.vector.tensor_scalar(out=rs[:, :], in0=sp_[:, :], scalar1=1.0 / 64, scalar2=EPS,
                                                                            op0=ALU.mult, op1=ALU.add), reads=[kssp], writes=[krs])
                                p.op("act", lambda: nc.scalar.activation(out=rs[:, :], in_=rs[:, :], func=AF.Ln), reads=[krs], writes=[krs], cost=0.57)
                                p.op("act", lambda: nc.scalar.activation(out=rs[:, :], in_=rs[:, :], func=AF.Exp, scale=-0.5), reads=[krs], writes=[krs], cost=0.57)
                                ob, kob = outr.next()
                                gi = 0 if c < 4 else 1
                                p.op("dve", lambda: nc.vector.scalar_tensor_tensor(
                                    out=ob[:, :], in0=acc[:, :], scalar=qkgain[:, gi:gi + 1], in1=rs[:, :], op0=ALU.mult, op1=ALU.mult),
                                    reads=[kacc, krs, "qkgain"], writes=[kob])
                                store_fm(qkT, c * 128, 128, gt0, ob, kob, ("qkT", gblk))
                            pending.append(later)
                        for tt in range(4):
                            acc, kacc = accr.next()
                            for kc in range(8):
                                p.op("pe", lambda kc=kc, acc=acc, tt=tt, hT=hT: nc.tensor.matmul(
                                    acc[:, :], lhsT=hT[:, kc, tt * 128:(tt + 1) * 128], rhs=win[:, kc, 1024:1536], start=(kc == 0), stop=(kc == 7)),
                                    reads=[winkeys[kc], hkeys[kc]], writes=[kacc], cost=0.23)
                            flush()
                            ob, kob = outr.next()
                            p.op("dve", lambda ob=ob, acc=acc: nc.vector.tensor_copy(out=ob[:, :], in_=acc[:, :]), reads=[kacc], writes=[kob])
                            p.op("pool", lambda ob=ob, tt=tt, gt0=gt0: nc.gpsimd.dma_start(out=va[gt0 + tt * 128:gt0 + (tt + 1) * 128, :], in_=ob[:, :]),
                                 reads=[kob], writes=[("va", gblk)], dma=True)
                        plan = []
                        for c in range(4):
                            plan.append((1536 + c * 128, "silu", zaT, c * 128, "zaT"))
                        for c in range(4):
                            plan.append((3584 + c * 128, "silu", zbT, c * 128, "zbT"))
                        for c in range(8):
                            plan.append((4112 + c * 128, "sig", gaT, c * 128, "gaT"))
                        for c in range(8):
                            plan.append((5136 + c * 128, "sig", gbT, c * 128, "gbT"))
                        for c in range(12):
                            plan.append((2048 + c * 128, "copy", qkvbT, c * 128, "qkvbT"))
                        for (col0, kind, dst, row0, dname) in plan:
                            acc, kacc = proj_fm(col0, 128)
                            ob, kob = outr.next()
                            if kind in ("silu", "sig"):
                                sg, ksg = sgr.next()
                                p.op("act", lambda sg=sg, acc=acc: nc.scalar.activation(out=sg[:, :], in_=acc[:, :], func=AF.Exp, scale=-1.0),
                                     reads=[kacc], writes=[ksg], cost=0.57)
                                p.op("act", lambda sg=sg: nc.scalar.activation(out=sg[:, :], in_=sg[:, :], func=AF.Ln, bias=1.0), reads=[ksg], writes=[ksg], cost=0.57)
                                if kind == "sig":
                                    p.op("act", lambda sg=sg, ob=ob: nc.scalar.activation(out=ob[:, :], in_=sg[:, :], func=AF.Exp, scale=-1.0),
                                         reads=[ksg], writes=[kob], cost=0.57)
                                else:
                                    p.op("act", lambda sg=sg: nc.scalar.activation(out=sg[:, :], in_=sg[:, :], func=AF.Exp, scale=-1.0), reads=[ksg], writes=[ksg], cost=0.57)
                                    p.op("dve", lambda sg=sg, ob=ob, acc=acc: nc.vector.tensor_tensor(out=ob[:, :], in0=acc[:, :], in1=sg[:, :], op=ALU.mult),
                                         reads=[kacc, ksg], writes=[kob], cost=0.55)
                            else:
                                p.op("dve", lambda ob=ob, acc=acc: nc.vector.tensor_copy(out=ob[:, :], in_=acc[:, :]), reads=[kacc], writes=[kob])
                            store_fm(dst, row0, 128, gt0, ob, kob, (dname, gblk))
                        for tt in range(4):
                            for kc in range(8):
                                p.op("pe", lambda kc=kc, tt=tt, hT=hT: nc.tensor.matmul(
                                    gbp[:, tt, :], lhsT=hT[:, kc, tt * 128:(tt + 1) * 128], rhs=win[:, kc, 4096:4112], start=(kc == 0), stop=(kc == 7)),
                                    reads=[winkeys[kc], hkeys[kc]], writes=["gbp"])
                        p.op("dve", lambda: nc.vector.tensor_scalar(out=gtmp[:, :, 0:8], in0=gbp[:, :, 0:8], scalar1=-1.0, scalar2=None, op0=ALU.mult),
                             reads=["gbp"], writes=["gtmp"])
                        p.op("dve", lambda: nc.vector.tensor_tensor(out=gtmp[:, :, 8:16], in0=gbp[:, :, 8:16],
                                                                    in1=dtb[:, :].unsqueeze(1).broadcast_to([128, 4, 8]), op=ALU.add),
                             reads=["gbp", "dtb"], writes=["gtmp"])
                        p.op("act", lambda: nc.scalar.activation(out=gtmp[:, :, :], in_=gtmp[:, :, :], func=AF.Exp), reads=["gtmp"], writes=["gtmp"])
                        p.op("act", lambda: nc.scalar.activation(out=gtmp[:, :, :], in_=gtmp[:, :, :], func=AF.Ln, bias=1.0), reads=["gtmp"], writes=["gtmp"])
                        p.op("dve", lambda: nc.vector.tensor_scalar(out=gbt[:, :, 0:8], in0=gtmp[:, :, 0:8], scalar1=-1.0, scalar2=None, op0=ALU.mult),
                             reads=["gtmp"], writes=["gbt"])
                        p.op("dve", lambda: nc.vector.tensor_tensor(out=gbt[:, :, 8:16], in0=gtmp[:, :, 8:16],
                                                                    in1=nega[:, :].unsqueeze(1).broadcast_to([128, 4, 8]), op=ALU.mult),
                             reads=["gtmp", "nega"], writes=["gbt"])
                        p.op("pool", lambda gt0=gt0: nc.gpsimd.dma_start(out=gbs[gt0:gt0 + TB, :].rearrange("(tt p) c -> p tt c", p=128), in_=gbt[:, :, :]),
                             reads=["gbt"], writes=[("gbs", gblk)], dma=True)
                        flush()
                p.flush()
            if stop_after == "p1":
                break
            for d in ((1, 0) if stop_after != "p2" else (1,)):
                with contextlib.ExitStack() as st:
                    TRI = cst[:, C_LT:C_LT + 128] if d == 1 else cst[:, C_UT:C_UT + 128]
                    mbase = C_M_BWD if d == 1 else C_M_FWD
                    MASKB = cstb[:, mbase:mbase + 512]
                    LAST = 0 if d == 1 else 127
                    identf = cst[:, C_ID:C_ID + 128]
                    onesf = cst[:, C_ONE:C_ONE + 128]
                    onesb = cstb[:, C_ONE:C_ONE + 128]
                    cw = SB(st, "cw", [128, 12, 5], F32)
                    cwd = SB(st, "cwd", [128, 60, 128], BF16)
                    for tap in range(5):
                        p.op("sp", lambda tap=tap: nc.sync.dma_start(out=cw[:, :, tap], in_=conv_w[l, tap, :].rearrange("(j p) -> p j", p=128),
                                                                     allow_slow_non_contiguous=True), writes=["cw"], dma=True)
                    for j in range(12):
                        for tap in range(5):
                            p.op("dve", lambda j=j, tap=tap: nc.vector.tensor_scalar(
                                out=cwd[:, j * 5 + tap, :], in0=identf, scalar1=cw[:, j, tap:tap + 1], scalar2=None, op0=ALU.mult),
                                reads=["cw", "cst"], writes=["cwd"])
                    p.mark('dn_consts_done')
                    rawr = Ring([SB(st, "raw%d" % i, [128, 12, TB + 4], BF16) for i in range(2)], "raw")
                    gscr = Ring([SB(st, "gsc%d" % i, [128, 4, 16], F32) for i in range(2)], "gsc")
                    knq_r = [[SB(st, "knq%d_%d" % (h, i), [128, 2, TB], BF16) for h in range(4)] for i in range(2)]
                    vsT_r = [[SB(st, "vsT%d_%d" % (h, i), [128, TB], BF16) for h in range(4)] for i in range(2)]
                    sil = [SB(st, "sil%d" % h, [128, TB], F32) for h in range(4)]
                    sqb = [SB(st, "sqb%d" % h, [128, TB], BF16) for h in range(4)]
                    rst = [SB(st, "rst%d" % h, [128, TB], F32) for h in range(4)]
                    rhs4 = SB(st, "rhs4", [128, 4, 128], F32)
                    rhsL = SB(st, "rhsL", [128, 4, 128], F32)
                    sc = SB(st, "sc", [128, 24], F32)
                    E4 = [SB(st, "E4_%d" % h, [128, 128], F32) for h in range(4)]
                    E2 = [SB(st, "E2_%d" % h, [128, 128], F32) for h in range(4)]
                    E13 = [SB(st, "E13_%d" % h, [128, 256], F32) for h in range(4)]
                    WW = [[SB(st, "WW%d_%d" % (h, i), [128, 384], F32) for i in range(2)] for h in range(4)]
                    onesr = SB(st, "onesr", [128, 128], F32)
                    trir = SB(st, "trir", [128, 128], F32)
                    gr = SB(st, "gr", [128, 4], F32)
                    p.op("dve", lambda: nc.vector.tensor_copy(out=R(onesr[:, :]), in_=onesf), reads=["cst"], writes=["onesr"])
                    p.op("dve", lambda: nc.vector.tensor_copy(out=R(trir[:, :]), in_=TRI), reads=["cst"], writes=["trir"])
                    Tt = [SB(st, "Tt%d" % h, [128, 128], BF16) for h in range(4)]
                    M3 = [SB(st, "M3_%d" % h, [128, 128], BF16) for h in range(4)]
                    qdec = [SB(st, "qdec%d" % h, [128, 128], BF16) for h in range(4)]
                    kbg = [SB(st, "kbg%d" % h, [128, 128], BF16) for h in range(4)]
                    kdec = [SB(st, "kdec%d" % h, [128, 128], BF16) for h in range(4)]
                    vb = [SB(st, "vb%d" % h, [128, 128], BF16) for h in range(4)]
                    uu = [SB(st, "uu%d" % h, [128, 128], F32) for h in range(4)]
                    wT = [SB(st, "wT%d" % h, [128, 128], BF16) for h in range(4)]
                    vnew = [SB(st, "vnew%d" % h, [128, 128], BF16) for h in range(4)]
                    Sf = [SB(st, "Sf%d" % h, [128, 128], F32) for h in range(4)]
                    Sb = [SB(st, "Sb%d" % h, [128, 128], BF16) for h in range(4)]
                    obr = Ring([SB(st, "obuf%d" % i, [128, 4, TB], F32) for i in range(2)], "obuf")
                    bk0 = [PS(st, "bk0_%d" % h, [128, 512], F32) for h in range(4)]
                    bk1 = [PS(st, "bk1_%d" % h, [128, 512], F32) for h in range(4)]
                    k0 = ["bk0_%d" % h for h in range(4)]
                    k1 = ["bk1_%d" % h for h in range(4)]

                    for si, T in enumerate(seqs):
                        nb = T // TB
                        for h in range(4):
                            p.op("pool", lambda h=h: nc.gpsimd.memset(Sf[h][:, :], 0.0), writes=["Sf%d" % h])
                            p.op("pool", lambda h=h: nc.gpsimd.memset(Sb[h][:, :], 0.0), writes=["Sb%d" % h])
                        def _blk(bi, bp, knq, vsT, si=si, T=T, nb=nb):
                                b = nb - 1 - bi if d == 1 else bi
                                t0 = b * TB
                                gt0 = bases[si] + t0
                                gblk = gt0 // TB
                                if d == 1:
                                    raw, kraw = rawr.next()
                                    lo = 0 if b > 0 else 2
                                    hi = TB + 4 if b < nb - 1 else TB + 2
                                    if lo > 0:
                                        p.op("pool", lambda raw=raw: nc.gpsimd.memset(raw[:, :, 0:2], 0.0), writes=[kraw])
                                    if hi < TB + 4:
                                        p.op("pool", lambda raw=raw: nc.gpsimd.memset(raw[:, :, TB + 2:TB + 4], 0.0), writes=[kraw])
                                    rk = [("qkvbT", gblk)] + ([("qkvbT", gblk - 1)] if b > 0 else []) + ([("qkvbT", gblk + 1)] if b < nb - 1 else [])
                                    for part in range(3):
                                        p.op("sp", lambda raw=raw, lo=lo, hi=hi, gt0=gt0, part=part: nc.sync.dma_start(
                                            out=raw[:, part * 4:(part + 1) * 4, lo:hi],
                                            in_=qkvbT[part * 512:(part + 1) * 512, gt0 - 2 + lo:gt0 - 2 + hi].rearrange("(j p) t -> p j t", p=128)),
                                            reads=rk, writes=[kraw], dma=True)
                                gsc, kgsc = gscr.next()
                                p.op("sp", lambda gsc=gsc, gt0=gt0: nc.sync.dma_start(
                                    out=gsc[:, :, :], in_=gbs[gt0:gt0 + TB, :].rearrange("(c p) k -> p c k", p=128)),
                                    reads=[("gbs", gblk)], writes=[kgsc], dma=True)
                                if d == 1:
                                    p.mark('dn_loads_done')
                                    for h in range(4):
                                        for (which, j, bank, bkey) in ((0, 4 + h, bk0[h], k0[h]), (1, h, bk1[h], k1[h])):
                                            for tap in range(5):
                                                p.op("pe", lambda j=j, tap=tap, bank=bank, raw=raw: nc.tensor.matmul(
                                                    bank[:, :], lhsT=cwd[:, j * 5 + tap, :], rhs=raw[:, j, tap:tap + TB], start=(tap == 0), stop=(tap == 4)),
                                                    reads=["cwd", kraw], writes=[bkey], cost=0.23)
                                            p.op("act", lambda bank=bank, h=h: nc.scalar.activation(out=sil[h][:, :], in_=bank[:, :], func=AF.Exp, scale=-1.0),
                                                 reads=[bkey], writes=["sil%d" % h], cost=0.57)
                                            p.op("act", lambda h=h: nc.scalar.activation(out=sil[h][:, :], in_=sil[h][:, :], func=AF.Ln, bias=1.0),
                                                 reads=["sil%d" % h], writes=["sil%d" % h], cost=0.57)
                                            p.op("act", lambda h=h: nc.scalar.activation(out=sil[h][:, :], in_=sil[h][:, :], func=AF.Exp, scale=-1.0),
                                                 reads=["sil%d" % h], writes=["sil%d" % h], cost=0.57)
                                            p.op("dve", lambda bank=bank, h=h: nc.vector.tensor_tensor(out=sil[h][:, :], in0=bank[:, :], in1=sil[h][:, :], op=ALU.mult),
                                                 reads=[bkey, "sil%d" % h], writes=["sil%d" % h], cost=0.55)
                                            p.op("act", lambda h=h: nc.scalar.activation(out=sqb[h][:, :], in_=sil[h][:, :], func=AF.Square),
                                                 reads=["sil%d" % h], writes=["sqb%d" % h])
                                            p.op("pe", lambda bank=bank, h=h: nc.tensor.matmul(bank[:, :], lhsT=onesb, rhs=sqb[h][:, :], start=True, stop=True),
                                                 reads=["sqb%d" % h, "cstb"], writes=[bkey], cost=0.23)
                                            p.op("dve", lambda bank=bank, h=h: nc.vector.tensor_scalar(out=rst[h][:, :], in0=bank[:, :], scalar1=EPS, scalar2=None, op0=ALU.add),
                                                 reads=[bkey], writes=["rst%d" % h])
                                            p.op("act", lambda h=h: nc.scalar.activation(out=rst[h][:, :], in_=rst[h][:, :], func=AF.Ln),
                                                 reads=["rst%d" % h], writes=["rst%d" % h], cost=0.57)
                                            p.op("act", lambda h=h: nc.scalar.activation(out=rst[h][:, :], in_=rst[h][:, :], func=AF.Exp, scale=-0.5),
                                                 reads=["rst%d" % h], writes=["rst%d" % h], cost=0.57)
                                            sclq = (128.0 ** -0.5) if which == 1 else 1.0
                                            p.op("dve", lambda h=h, which=which, sclq=sclq: nc.vector.scalar_tensor_tensor(
                                                out=knq[h][:, which, :], in0=sil[h][:, :], scalar=sclq, in1=rst[h][:, :], op0=ALU.mult, op1=ALU.mult),
                                                reads=["sil%d" % h, "rst%d" % h], writes=["knq%d_%d" % (h, bp)])
                                        for tap in range(5):
                                            p.op("pe", lambda h=h, tap=tap, raw=raw: nc.tensor.matmul(
                                                bk0[h][:, :], lhsT=cwd[:, (8 + h) * 5 + tap, :], rhs=raw[:, 8 + h, tap:tap + TB], start=(tap == 0), stop=(tap == 4)),
                                                reads=["cwd", kraw], writes=[k0[h]], cost=0.23)
                                        p.op("act", lambda h=h: nc.scalar.activation(out=sil[h][:, :], in_=bk0[h][:, :], func=AF.Exp, scale=-1.0),
                                             reads=[k0[h]], writes=["sil%d" % h], cost=0.57)
                                        p.op("act", lambda h=h: nc.scalar.activation(out=sil[h][:, :], in_=sil[h][:, :], func=AF.Ln, bias=1.0),
                                             reads=["sil%d" % h], writes=["sil%d" % h], cost=0.57)
                                        p.op("act", lambda h=h: nc.scalar.activation(out=sil[h][:, :], in_=sil[h][:, :], func=AF.Exp, scale=-1.0),
                                             reads=["sil%d" % h], writes=["sil%d" % h], cost=0.57)
                                        p.op("dve", lambda h=h: nc.vector.tensor_tensor(out=vsT[h][:, :], in0=bk0[h][:, :], in1=sil[h][:, :], op=ALU.mult),
                                             reads=[k0[h], "sil%d" % h], writes=["vsT%d_%d" % (h, bp)], cost=0.55)
                                        p.op("pool", lambda h=h, gt0=gt0: nc.gpsimd.dma_start(out=prepT[h, 0:2, :, gt0:gt0 + TB].rearrange("w p t -> p w t"), in_=knq[h][:, :, :]),
                                             reads=["knq%d_%d" % (h, bp)], writes=[("prepT", gblk)], dma=True)
                                        p.op("pool", lambda h=h, gt0=gt0: nc.gpsimd.dma_start(out=prepT[h, 2, :, gt0:gt0 + TB], in_=vsT[h][:, :]),
                                             reads=["vsT%d_%d" % (h, bp)], writes=[("prepT", gblk)], dma=True)
                                else:
                                    for h in range(4):
                                        p.op("sp", lambda h=h, gt0=gt0: nc.sync.dma_start(out=knq[h][:, :, :], in_=prepT[h, 0:2, :, gt0:gt0 + TB].rearrange("w p t -> p w t")),
                                             reads=[("prepT", gblk)], writes=["knq%d_%d" % (h, bp)], dma=True)
                                        p.op("sp", lambda h=h, gt0=gt0: nc.sync.dma_start(out=vsT[h][:, :], in_=prepT[h, 2, :, gt0:gt0 + TB]),
                                             reads=[("prepT", gblk)], writes=["vsT%d_%d" % (h, bp)], dma=True)
                                p.mark('dn_prep_done')
                                ob, kob = obr.next()
                                for ci in range(4):
                                    c = 3 - ci if d == 1 else ci
                                    cs = slice(c * 128, (c + 1) * 128)
                                    gcol0 = 8 + d * 4
                                    p.op("dve", lambda c=c, gsc=gsc: nc.vector.tensor_copy(out=R(gr[:, :]), in_=gsc[:, c, gcol0:gcol0 + 4]), reads=[kgsc], writes=["gr"])
                                    p.op("pe", lambda: nc.tensor.matmul(bk1[0][:, 0:4], lhsT=R(trir[:, :]), rhs=R(gr[:, :]), start=True, stop=True),
                                         reads=["trir", "gr"], writes=[k1[0]])
                                    p.op("dve", lambda: nc.vector.tensor_copy(out=sc[:, 0:4], in_=bk1[0][:, 0:4]), reads=[k1[0]], writes=["sc"])
                                    p.op("dve", lambda c=c, gsc=gsc: nc.vector.tensor_tensor(out=sc[:, 4:8], in0=sc[:, 0:4], in1=gsc[:, c, d * 4:d * 4 + 4], op=ALU.add),
                                         reads=["sc", kgsc], writes=["sc"])
                                    p.op("dve", lambda: nc.vector.tensor_scalar(out=sc[:, 8:12], in0=sc[:, 0:4], scalar1=-1.0, scalar2=None, op0=ALU.mult),
                                         reads=["sc"], writes=["sc"])
                                    p.op("act", lambda: nc.scalar.activation(out=sc[:, 12:16], in_=sc[:, 4:8], func=AF.Exp), reads=["sc"], writes=["sc"])
                                    p.op("act", lambda c=c, gsc=gsc: nc.scalar.activation(out=sc[:, 16:20], in_=gsc[:, c, d * 4:d * 4 + 4], func=AF.Exp),
                                         reads=[kgsc, "sc"], writes=["sc"])
                                    p.op("dve", lambda c=c, gsc=gsc: nc.vector.tensor_tensor(
                                        out=R(rhs4[:, :, :]), in0=TRI.unsqueeze(1).broadcast_to([128, 4, 128]),
                                        in1=gsc[:, c, gcol0:gcol0 + 4].unsqueeze(2).broadcast_to([128, 4, 128]), op=ALU.mult),
                                        reads=["cst", kgsc], writes=["rhs4"])
                                    p.op("dve", lambda c=c, gsc=gsc: nc.vector.tensor_tensor(
                                        out=R(rhsL[:, :, :]), in0=identf.unsqueeze(1).broadcast_to([128, 4, 128]),
                                        in1=gsc[:, c, d * 4:d * 4 + 4].unsqueeze(2).broadcast_to([128, 4, 128]), op=ALU.mult),
                                        reads=["cst", kgsc], writes=["rhsL"])
                                    p.mark('dn_sc_done')
                                    for h in range(4):
                                        p.op("pe", lambda h=h: nc.tensor.matmul(bk0[h][:, :], lhsT=identb, rhs=MASKB, start=True, stop=False),
                                             reads=["cstb"], writes=[k0[h]], cost=0.23)
                                        for q4 in range(4):
                                            p.op("pe", lambda h=h, q4=q4: nc.tensor.matmul(bk0[h][:, q4 * 128:(q4 + 1) * 128], lhsT=R(onesr[:, :]), rhs=R(rhs4[:, h, :]),
                                                                                           start=False, stop=False),
                                                 reads=["onesr", "rhs4"], writes=[k0[h]], cost=0.07)
                                        p.op("pe", lambda h=h: nc.tensor.matmul(bk0[h][:, 256:384], lhsT=R(onesr[:, :]), rhs=R(rhsL[:, h, :]), start=False, stop=True),
                                             reads=["onesr", "rhsL"], writes=[k0[h]], cost=0.07)
                                        p.op("pe", lambda h=h, cs=cs: nc.tensor.matmul(bk1[h][:, 0:256], lhsT=knq[h][:, 0, cs], rhs=knq[h][:, :, cs], start=True, stop=True),
                                             reads=["knq%d_%d" % (h, bp)], writes=[k1[h]])
                                        p.op("act", lambda h=h: nc.scalar.activation(out=E4[h][:, :], in_=bk0[h][:, 0:128], func=AF.Exp), reads=[k0[h]], writes=["E4_%d" % h])
                                        p.op("act", lambda h=h: nc.scalar.activation(out=E2[h][:, :], in_=bk0[h][:, 128:256], func=AF.Exp, scale=-1.0, bias=sc[:, 4 + h:5 + h]),
                                             reads=[k0[h], "sc"], writes=["E2_%d" % h])
                                        p.op("act", lambda h=h: nc.scalar.activation(out=E13[h][:, :], in_=bk0[h][:, 256:512], func=AF.Exp, bias=sc[:, 8 + h:9 + h]),
                                             reads=[k0[h], "sc"], writes=["E13_%d" % h])
                                        p.op("act", lambda h=h: nc.scalar.activation(out=sc[:, 20 + h:21 + h], in_=bk0[h][:, LAST:LAST + 1], func=AF.Exp, bias=sc[:, 8 + h:9 + h]),
                                             reads=[k0[h], "sc"], writes=["sc"])
                                        p.op("dve", lambda h=h: nc.vector.tensor_tensor(out=R(WW[h][0][:, 128:256]), in0=bk1[h][:, 0:128], in1=E13[h][:, 0:128], op=ALU.mult),
                                             reads=[k1[h], "E13_%d" % h], writes=["WW%d_0" % h])
                                        p.op("dve", lambda h=h: nc.vector.tensor_tensor(out=R(WW[h][0][:, 256:384]), in0=bk1[h][:, 0:128], in1=E2[h][:, :], op=ALU.mult),
                                             reads=[k1[h], "E2_%d" % h], writes=["WW%d_0" % h])
                                        p.op("dve", lambda h=h: nc.vector.tensor_tensor(out=M3[h][:, :], in0=bk1[h][:, 128:256], in1=E13[h][:, 128:256], op=ALU.mult),
                                             reads=[k1[h], "E13_%d" % h], writes=["M3_%d" % h])
                                        p.op("dve", lambda h=h, cs=cs: nc.vector.tensor_tensor(out=qdec[h][:, :], in0=knq[h][:, 1, cs], in1=E4[h][:, :], op=ALU.mult),
                                             reads=["knq%d_%d" % (h, bp), "E4_%d" % h], writes=["qdec%d" % h])
                                        p.op("dve", lambda h=h: nc.vector.tensor_tensor(out=R(WW[h][1][:, 0:128]), in0=identf, in1=WW[h][0][:, 128:256], op=ALU.subtract),
                                             reads=["cst", "WW%d_0" % h], writes=["WW%d_1y" % h])
                                        p.op("pe", lambda h=h, cs=cs: nc.tensor.transpose(out=bk1[h][:, 256:384].bitcast(BF16)[:, 0:128], in_=knq[h][:, 0, cs], identity=identb),
                                             reads=["knq%d_%d" % (h, bp), "cstb"], writes=[k1[h]])
                                        p.op("pe", lambda h=h, cs=cs: nc.tensor.transpose(out=bk1[h][:, 384:512].bitcast(BF16)[:, 0:128], in_=vsT[h][:, cs], identity=identb),
                                             reads=["vsT%d_%d" % (h, bp), "cstb"], writes=[k1[h]])
                                        p.op("act", lambda h=h: nc.scalar.activation(out=kbg[h][:, :], in_=bk1[h][:, 256:384].bitcast(BF16)[:, 0:128], func=AF.Copy, scale=sc[:, 12 + h:13 + h]),
                                             reads=[k1[h], "sc"], writes=["kbg%d" % h])
                                        p.op("dve", lambda h=h: nc.vector.tensor_scalar(out=kdec[h][:, :], in0=bk1[h][:, 256:384].bitcast(BF16)[:, 0:128], scalar1=sc[:, 20 + h:21 + h], scalar2=None, op0=ALU.mult),
                                             reads=[k1[h], "sc"], writes=["kdec%d" % h])
                                        p.op("act", lambda h=h: nc.scalar.activation(out=vb[h][:, :], in_=bk1[h][:, 384:512].bitcast(BF16)[:, 0:128], func=AF.Copy, scale=sc[:, 16 + h:17 + h]),
                                             reads=[k1[h], "sc"], writes=["vb%d" % h])
                                    p.mark('dn_pg_done')
                                    for lev in range(0, 7):
                                        for h in range(4):
                                            cur, nxt = WW[h][lev % 2], WW[h][(lev + 1) % 2]
                                            kc_, kn_ = "WW%d_%d" % (h, lev % 2), "WW%d_%d" % (h, (lev + 1) % 2)
                                            kcy, kny = kc_ + "y", kn_ + "y"
                                            if lev == 0:
                                                p.op("pe", lambda h=h, cur=cur: nc.tensor.matmul(bk0[h][:, 128:256], lhsT=R(cur[:, 256:384]), rhs=R(cur[:, 128:256]), start=True, stop=True),
                                                     reads=[kc_], writes=[k0[h]])
                                            elif lev < 6:
                                                p.op("pe", lambda h=h, cur=cur: nc.tensor.matmul(bk0[h][:, 0:256], lhsT=R(cur[:, 256:384]), rhs=R(cur[:, 0:256]), start=True, stop=True),
                                                     reads=[kc_, kcy], writes=[k0[h]], cost=0.11)
                                            else:
                                                p.op("pe", lambda h=h, cur=cur: nc.tensor.matmul(bk0[h][:, 0:128], lhsT=R(cur[:, 256:384]), rhs=R(cur[:, 0:128]), start=True, stop=True),
                                                     reads=[kc_, kcy], writes=[k0[h]])
                                            if lev < 6:
                                                p.op("pe", lambda h=h, cur=cur: nc.tensor.matmul(bk0[h][:, 256:384], lhsT=R(cur[:, 128:256]), rhs=R(cur[:, 256:384]), start=True, stop=True),
                                                     reads=[kc_], writes=[k0[h]])
                                                if (lev + h) % 3 == 0:
                                                    p.op("dve", lambda h=h, nxt=nxt: nc.vector.tensor_copy(out=R(nxt[:, 128:384]), in_=bk0[h][:, 128:384]), reads=[k0[h]], writes=[kn_])
                                                else:
                                                    p.op("act", lambda h=h, nxt=nxt: nc.scalar.copy(out=R(nxt[:, 128:384]), in_=bk0[h][:, 128:384]), reads=[k0[h]], writes=[kn_])
                                            if 1 <= lev < 6:
                                                p.op("dve", lambda h=h, nxt=nxt, cur=cur: nc.vector.tensor_tensor(out=R(nxt[:, 0:128]), in0=bk0[h][:, 0:128], in1=cur[:, 0:128], op=ALU.add),
                                                     reads=[k0[h], kcy], writes=[kny])
                                            elif lev == 6:
                                                p.op("dve", lambda h=h, cur=cur: nc.vector.tensor_tensor(out=Tt[h][:, :], in0=bk0[h][:, 0:128], in1=cur[:, 0:128], op=ALU.add),
                                                     reads=[k0[h], kcy], writes=["Tt%d" % h])
                                    p.mark('dn_inv_done')
                                    for h in range(4):
                                        p.op("pe", lambda h=h: nc.tensor.matmul(bk0[h][:, 0:128], lhsT=Tt[h][:, :], rhs=vb[h][:, :], start=True, stop=True),
                                             reads=["Tt%d" % h, "vb%d" % h], writes=[k0[h]])
                                        p.op("pe", lambda h=h: nc.tensor.matmul(bk0[h][:, 128:256], lhsT=kbg[h][:, :], rhs=Tt[h][:, :], start=True, stop=True),
                                             reads=["Tt%d" % h, "kbg%d" % h], writes=[k0[h]])
                                        p.op("act", lambda h=h: nc.scalar.copy(out=uu[h][:, :], in_=bk0[h][:, 0:128]), reads=[k0[h]], writes=["uu%d" % h])
                                        p.op("dve", lambda h=h: nc.vector.tensor_copy(out=wT[h][:, :], in_=bk0[h][:, 128:256]), reads=[k0[h]], writes=["wT%d" % h])
                                    for h in range(4):
                                        p.op("pe", lambda h=h: nc.tensor.matmul(bk1[h][:, 0:128], lhsT=wT[h][:, :], rhs=Sb[h][:, :], start=True, stop=True),
                                             reads=["wT%d" % h, "Sb%d" % h], writes=[k1[h]])
                                        p.op("dve", lambda h=h: nc.vector.scalar_tensor_tensor(out=vnew[h][:, :], in0=bk1[h][:, 0:128], scalar=-1.0, in1=uu[h][:, :],
                                                                                              op0=ALU.mult, op1=ALU.add),
                                             reads=[k1[h], "uu%d" % h], writes=["vnew%d" % h])
                                    for h in range(4):
                                        p.op("pe", lambda h=h: nc.tensor.matmul(bk1[h][:, 128:256], lhsT=Sb[h][:, :], rhs=qdec[h][:, :], start=True, stop=False),
                                             reads=["Sb%d" % h, "qdec%d" % h], writes=[k1[h]])
                                        p.op("pe", lambda h=h: nc.tensor.matmul(bk1[h][:, 128:256], lhsT=vnew[h][:, :], rhs=M3[h][:, :], start=False, stop=True),
                                             reads=["vnew%d" % h, "M3_%d" % h], writes=[k1[h]])
                                        p.op("act", lambda h=h, ob=ob, cs=cs: nc.scalar.copy(out=ob[:, h, cs], in_=bk1[h][:, 128:256]), reads=[k1[h]], writes=[kob])
                                        p.op("pe", lambda h=h: nc.tensor.matmul(bk1[h][:, 256:384], lhsT=kdec[h][:, :], rhs=vnew[h][:, :], start=True, stop=True),
                                             reads=["kdec%d" % h, "vnew%d" % h], writes=[k1[h]])
                                        p.op("pool", lambda h=h: nc.gpsimd.tensor_scalar(out=Sf[h][:, :], in0=Sf[h][:, :], scalar1=E4[h][:, LAST:LAST + 1], scalar2=None, op0=ALU.mult),
                                             reads=["Sf%d" % h, "E4_%d" % h], writes=["Sf%d" % h])
                                        p.op("dve", lambda h=h: nc.vector.tensor_tensor(out=Sf[h][:, :], in0=bk1[h][:, 256:384], in1=Sf[h][:, :], op=ALU.add),
                                             reads=[k1[h], "Sf%d" % h], writes=["Sf%d" % h])
                                        p.op("pool", lambda h=h: nc.gpsimd.tensor_copy(out=Sb[h][:, :], in_=Sf[h][:, :]), reads=["Sf%d" % h], writes=["Sb%d" % h])
                                p.op("pool", lambda ob=ob, gt0=gt0: nc.gpsimd.dma_start(
                                    out=oT[d, :, gt0:gt0 + TB].rearrange("(h p) t -> p h t", p=128), in_=ob[:, :, :]),
                                    reads=[kob], writes=[("oT%d" % d, gblk)], dma=True)
                        for bi in range(nb):
                            _blk(bi, bi % 2, knq_r[bi % 2], vsT_r[bi % 2])
                    p.flush()
            if stop_after in ("p2", "p3"):
                break
            with contextlib.ExitStack() as st:
                onesb = cstb[:, C_ONE:C_ONE + 128]
                waT = SB(st, "waT", [128, 4, D], BF16)
                wbT = SB(st, "wbT", [128, 4, D], BF16)
                woT = SB(st, "woT", [128, 8, D], BF16)
                GTb = SB(st, "GTb", [128, 8 * 14 * 64], BF16)
                dng = SB(st, "dng", [128, 1], F32)
                p.op("pool", lambda: nc.gpsimd.dma_start(out=waT[:, :, :], in_=w_a[l].rearrange("(c p) m -> p c m", p=128)), writes=["waT"], dma=True)
                p.op("pool", lambda: nc.gpsimd.dma_start(out=wbT[:, :, :], in_=w_b[l].rearrange("(h d) m -> d h m", d=128)), writes=["wbT"], dma=True)
                for m in range(8):
                    p.op("pool", lambda m=m: nc.gpsimd.dma_start(out=woT[:, m, :], in_=w_o[l, m * 128:(m + 1) * 128, :]), writes=["woT"], dma=True)
                for q4 in range(4):
                    p.op("pool", lambda q4=q4: nc.gpsimd.dma_start(out=GTb[:, q4 * 1792:(q4 + 1) * 1792], in_=rpbT[l, :, q4 * 1792:(q4 + 1) * 1792]),
                         writes=["GTb"], dma=True)
                p.op("sp", lambda: nc.sync.dma_start(out=dng[:, :], in_=dn_g[l].rearrange("(p o) -> p o", o=1), allow_slow_non_contiguous=True),
                     writes=["dng"], dma=True)
                GT4 = GTb[:, :].rearrange("p (h m q) -> p h m q", h=8, m=14)
                qbd_r = [SB(st, "qbd%d" % i, [128, 4, 8, 128], BF16) for i in range(2)]
                for i_ in range(2):
                    p.op("pool", lambda i_=i_: nc.gpsimd.memset(qbd_r[i_][:, :, :, :], 0.0), writes=["qbd_%d" % i_])
                kTw_r = [SB(st, "kTw%d" % i, [128, 4, 1024], BF16) for i in range(2)]
                vw = [SB(st, "vw%d" % i, [128, 8, 512], BF16) for i in range(2)]
                zat_r = [SB(st, "zat%d" % i, [128, 4, TB], BF16) for i in range(2)]
                oag = SB(st, "oag", [128, 4, TB], BF16)
                pT = SB(st, "pT", [128, 4, 8, 64], BF16)
                rcp = SB(st, "rcp", [128, 512], F32)
                otmp = SB(st, "otmp", [128, 256], F32)
                ofr = Ring([SB(st, "of%d" % i, [128, TB], F32) for i in range(2)], "of")
                obr4 = Ring([SB(st, "ob4%d" % i, [128, TB], F32) for i in range(2)], "ob4")
                osum = SB(st, "osum", [128, TB], F32)
                osq = SB(st, "osq", [128, TB], BF16)
                orst = SB(st, "orst", [128, TB], F32)
                zbt_r = [SB(st, "zbt%d" % i, [128, 4, TB], BF16) for i in range(2)]
                obg = SB(st, "obg", [128, 4, TB], BF16)
                gar = Ring([SB(st, "ga%d" % i, [128, TB], BF16) for i in range(2)], "ga")
                gbr = Ring([SB(st, "gb%d" % i, [128, TB], BF16) for i in range(2)], "gb")
                t1 = SB(st, "t1", [128, TB], F32)
                t2 = SB(st, "t2", [128, TB], F32)
                mrg = SB(st, "mrg", [128, 8, TB], BF16)
                xrr = Ring([SB(st, "xr%d" % i, [128, D], F32) for i in range(2)], "xr")
                outr4 = Ring([SB(st, "o4_%d" % i, [128, 512], F32) for i in range(2)], "o4")
                Sr = Ring([PS(st, "P_S%d" % i, [128, 512], F32) for i in range(3)], "P_S")
                SUMp = PS(st, "P_SUM", [128, 512], F32)
                OTp = PS(st, "P_OT", [128, 512], F32)
                acc4 = Ring([PS(st, "P_acc%d" % i, [128, 512], F32) for i in range(3)], "P_acc")

                for si, T in enumerate(seqs):
                    rows = T // 64
                    xsrc = xin[si] if l == 0 else x1[bases[si]:bases[si] + T, :]
                    dst = yout[si] if l == L - 1 else x1[bases[si]:bases[si] + T, :]
                    def _blk4(b, bp, qbd, kTw, zat, zbt, si=si, T=T, rows=rows, xsrc=xsrc, dst=dst):
                            t0 = b * TB
                            gt0 = bases[si] + t0
                            gblk = gt0 // TB
                            r0 = b * 8
                            wlo = min(max(r0 - 4, 0), rows - 8)
                            wtok = wlo * 64
                            nk = min(T - wtok, 1024)
                            kblks = sorted(set((bases[si] + wtok + i) // TB for i in range(0, nk, 64)))
                            for h2 in range(2):
                                for c4 in range(4):
                                    p.op("sp", lambda gt0=gt0, h2=h2, c4=c4: nc.sync.dma_start(
                                        out=qbd[h2 * 64:(h2 + 1) * 64, c4, :, h2 * 64:(h2 + 1) * 64],
                                        in_=qkT[c4 * 128 + h2 * 64:c4 * 128 + (h2 + 1) * 64, gt0:gt0 + TB].rearrange("p (r q) -> p r q", q=64)),
                                        reads=[("qkT", gblk)], writes=["qbd_%d" % bp], dma=True)
                            p.op("sp", lambda si=si, wtok=wtok, nk=nk: nc.sync.dma_start(
                                out=kTw[:, :, 0:nk], in_=qkT[512:1024, bases[si] + wtok:bases[si] + wtok + nk].rearrange("(c p) t -> p c t", p=128)),
                                reads=[("qkT", kb) for kb in kblks], writes=["kTw_%d" % bp], dma=True)
                            for par in range(2):
                                vstart = wtok + par * 64
                                nfull = min(T - vstart, 1024) // 128
                                p.op("sp", lambda si=si, par=par, vstart=vstart, nfull=nfull: nc.sync.dma_start(
                                    out=vw[par][:, 0:nfull, :],
                                    in_=va[bases[si] + vstart:bases[si] + vstart + nfull * 128, :].rearrange("(s p) f -> p s f", p=128)),
                                    reads=[("va", kb) for kb in kblks], writes=["vw%d" % par], dma=True)
                            p.op("sp", lambda gt0=gt0: nc.sync.dma_start(out=zat[:, :, :], in_=zaT[:, gt0:gt0 + TB].rearrange("(c p) t -> p c t", p=128)),
                                 reads=[("zaT", gblk)], writes=["zat_%d" % bp], dma=True)
                            p.op("sp", lambda gt0=gt0: nc.sync.dma_start(out=zbt[:, :, :], in_=zbT[:, gt0:gt0 + TB].rearrange("(h d) t -> d h t", d=128)),
                                 reads=[("zbT", gblk)], writes=["zbt_%d" % bp], dma=True)
                            for h in range(4):
                                of_, kof = ofr.next()
                                ob_, kob4 = obr4.next()
                                p.op("sp", lambda h=h, of_=of_, gt0=gt0: nc.sync.dma_start(out=of_[:, :], in_=oT[0, h * 128:(h + 1) * 128, gt0:gt0 + TB]),
                                     reads=[("oT0", gblk)], writes=[kof], dma=True)
                                p.op("sp", lambda h=h, ob_=ob_, gt0=gt0: nc.sync.dma_start(out=ob_[:, :], in_=oT[1, h * 128:(h + 1) * 128, gt0:gt0 + TB]),
                                     reads=[("oT1", gblk)], writes=[kob4], dma=True)
                                p.op("pool", lambda of_=of_, ob_=ob_: nc.gpsimd.tensor_tensor(out=osum[:, :], in0=of_[:, :], in1=ob_[:, :], op=ALU.add),
                                     reads=[kof, kob4], writes=["osum"])
                                p.op("act", lambda: nc.scalar.activation(out=osq[:, :], in_=osum[:, :], func=AF.Square), reads=["osum"], writes=["osq"])
                                acc, kacc = acc4.next()
                                p.op("pe", lambda acc=acc: nc.tensor.matmul(acc[:, :], lhsT=onesb, rhs=osq[:, :], start=True, stop=True),
                                     reads=["osq"], writes=[kacc], cost=0.23)
                                p.op("dve", lambda acc=acc: nc.vector.tensor_scalar(out=orst[:, :], in0=acc[:, :], scalar1=1.0 / 128, scalar2=EPS, op0=ALU.mult, op1=ALU.add),
                                     reads=[kacc], writes=["orst"])
                                p.op("act", lambda: nc.scalar.activation(out=orst[:, :], in_=orst[:, :], func=AF.Ln), reads=["orst"], writes=["orst"], cost=0.57)
                                p.op("act", lambda: nc.scalar.activation(out=orst[:, :], in_=orst[:, :], func=AF.Exp, scale=-0.5), reads=["orst"], writes=["orst"], cost=0.57)
                                p.op("dve", lambda: nc.vector.scalar_tensor_tensor(out=osum[:, :], in0=osum[:, :], scalar=dng[:, 0:1], in1=orst[:, :], op0=ALU.mult, op1=ALU.mult),
                                     reads=["osum", "orst", "dng"], writes=["osum"])
                                p.op("pool", lambda h=h: nc.gpsimd.tensor_tensor(out=obg[:, h, :], in0=osum[:, :], in1=zbt[:, h, :], op=ALU.mult),
                                     reads=["osum", "zbt_%d" % bp], writes=["obg"])
                            for rr in range(8):
                                r = r0 + rr
                                rs = min(max(r - 4, 0), rows - 8)
                                o_ = r - rs
                                m0 = 7 - o_
                                par = (rs - wlo) % 2
                                slot0 = (rs - wlo - par) // 2
                                koff = (rs - wlo) * 64
                                qs = slice(rr * 64, (rr + 1) * 64)
                                for pr in range(4):
                                    S_, kS = Sr.next()
                                    p.op("pe", lambda S_=S_, pr=pr, m0=m0: nc.tensor.matmul(
                                        S_[:, :], lhsT=identb, rhs=GT4[:, 2 * pr:2 * pr + 2, m0:m0 + 7:2, :].rearrange("p h k q -> p k h q"), start=True, stop=False),
                                        reads=["GTb"], writes=[kS], cost=0.23)
                                    for kk in range(4):
                                        p.op("pe", lambda S_=S_, pr=pr, kk=kk, koff=koff, rr=rr: nc.tensor.matmul(
                                            S_[:, kk * 128:(kk + 1) * 128], lhsT=kTw[:, pr, koff + kk * 128:koff + (kk + 1) * 128],
                                            rhs=qbd[:, pr, rr, :], start=False, stop=(kk == 3)),
                                            reads=["kTw_%d" % bp, "qbd_%d" % bp], writes=[kS])
                                    p.op("act", lambda S_=S_, pr=pr: nc.scalar.activation(
                                        out=pT[:, :, 2 * pr:2 * pr + 2, :], in_=S_[:, :].rearrange("p (k h q) -> p k h q", k=4, h=2), func=AF.Exp),
                                        reads=[kS], writes=["pT"], cost=0.57)
                                for kk in range(4):
                                    p.op("pe", lambda kk=kk: nc.tensor.matmul(SUMp[:, :], lhsT=onesb, rhs=pT[:, kk, :, :], start=(kk == 0), stop=(kk == 3)),
                                         reads=["pT"], writes=["P_SUM"], cost=0.23)
                                for h in range(8):
                                    for kk in range(4):
                                        p.op("pe", lambda h=h, kk=kk, par=par, slot0=slot0: nc.tensor.matmul(
                                            OTp[(h % 2) * 64:(h % 2) * 64 + 64, (h // 2) * 64:(h // 2) * 64 + 64],
                                            lhsT=vw[par][:, slot0 + kk, h * 64:(h + 1) * 64], rhs=pT[:, kk, h, :],
                                            start=(kk == 0), stop=(kk == 3)),
                                            reads=["pT", "vw%d" % par], writes=["P_OT"])
                                p.op("act", lambda: nc.scalar.activation(out=rcp[:, :], in_=SUMp[:, :], func=AF.Ln), reads=["P_SUM"], writes=["rcp"], cost=0.57)
                                p.op("act", lambda: nc.scalar.activation(out=rcp[:, :], in_=rcp[:, :], func=AF.Exp, scale=-1.0), reads=["rcp"], writes=["rcp"], cost=0.57)
                                for h2 in range(2):
                                    hs = slice(h2 * 64, (h2 + 1) * 64)
                                    p.op("dve", lambda h2=h2, hs=hs: nc.vector.tensor_tensor(
                                        out=otmp[hs, :].rearrange("p (c q) -> p c q", c=4), in0=OTp[hs, 0:256].rearrange("p (c q) -> p c q", c=4),
                                        in1=rcp[hs, :].rearrange("p (c h q) -> p c h q", c=4, h=2)[:, :, h2, :], op=ALU.mult),
                                        reads=["P_OT", "rcp"], writes=["otmp"])
                                p.op("pool", lambda qs=qs: nc.gpsimd.tensor_tensor(out=oag[:, :, qs], in0=otmp[:, :].rearrange("p (c q) -> p c q", c=4), in1=zat[:, :, qs], op=ALU.mult),
                                     reads=["otmp", "zat_%d" % bp], writes=["oag"])
                            for m in range(8):
                                ms = slice(m * 128, (m + 1) * 128)
                                ga_, kga = gar.next()
                                gb_, kgb = gbr.next()
                                p.op("sp", lambda ga_=ga_, m=m, gt0=gt0: nc.sync.dma_start(out=ga_[:, :], in_=gaT[m * 128:(m + 1) * 128, gt0:gt0 + TB]),
                                     reads=[("gaT", gblk)], writes=[kga], dma=True)
                                p.op("sp", lambda gb_=gb_, m=m, gt0=gt0: nc.sync.dma_start(out=gb_[:, :], in_=gbT[m * 128:(m + 1) * 128, gt0:gt0 + TB]),
                                     reads=[("gbT", gblk)], writes=[kgb], dma=True)
                                ya, kya = acc4.next()
                                for h in range(4):
                                    p.op("pe", lambda ya=ya, h=h, ms=ms: nc.tensor.matmul(ya[:, :], lhsT=waT[:, h, ms], rhs=oag[:, h, :], start=(h == 0), stop=(h == 3)),
                                         reads=["waT", "oag"], writes=[kya], cost=0.23)
                                yb, kyb = acc4.next()
                                for h in range(4):
                                    p.op("pe", lambda yb=yb, h=h, ms=ms: nc.tensor.matmul(yb[:, :], lhsT=wbT[:, h, ms], rhs=obg[:, h, :], start=(h == 0), stop=(h == 3)),
                                         reads=["wbT", "obg"], writes=[kyb], cost=0.23)
                                p.op("dve", lambda ya=ya, ga_=ga_: nc.vector.tensor_tensor(out=t1[:, :], in0=ya[:, :], in1=ga_[:, :], op=ALU.mult),
                                     reads=[kya, kga], writes=["t1"])
                                p.op("dve", lambda yb=yb, gb_=gb_: nc.vector.tensor_tensor(out=t2[:, :], in0=yb[:, :], in1=gb_[:, :], op=ALU.mult),
                                     reads=[kyb, kgb], writes=["t2"])
                                p.op("pool", lambda m=m: nc.gpsimd.tensor_tensor(out=mrg[:, m, :], in0=t1[:, :], in1=t2[:, :], op=ALU.add),
                                     reads=["t1", "t2"], writes=["mrg"])
                            for tt in range(4):
                                xr_, kxr = xrr.next()
                                srck = [] if l == 0 else [("x1", gblk)]
                                p.op("sp", lambda xr_=xr_, tt=tt, t0=t0, xsrc=xsrc: nc.sync.dma_start(out=xr_[:, :], in_=xsrc[t0 + tt * 128:t0 + (tt + 1) * 128, :]),
                                     reads=srck, writes=[kxr], dma=True)
                                for eh in range(2):
                                    po, kpo = acc4.next()
                                    for m in range(8):
                                        p.op("pe", lambda po=po, m=m, tt=tt, eh=eh: nc.tensor.matmul(
                                            po[:, :], lhsT=mrg[:, m, tt * 128:(tt + 1) * 128], rhs=woT[:, m, eh * 512:(eh + 1) * 512], start=(m == 0), stop=(m == 7)),
                                            reads=["mrg", "woT"], writes=[kpo], cost=0.23)
                                    o4, ko4 = outr4.next()
                                    p.op("dve", lambda po=po, o4=o4, xr_=xr_, eh=eh: nc.vector.tensor_tensor(out=o4[:, :], in0=po[:, :], in1=xr_[:, eh * 512:(eh + 1) * 512], op=ALU.add),
                                         reads=[kpo, kxr], writes=[ko4])
                                    wk = [("x1", gblk)] if l < L - 1 else [("y", si, b)]
                                    p.op("pool", lambda o4=o4, tt=tt, eh=eh, t0=t0, dst=dst: nc.gpsimd.dma_start(
                                        out=dst[t0 + tt * 128:t0 + (tt + 1) * 128, eh * 512:(eh + 1) * 512], in_=o4[:, :]),
                                        reads=[ko4], writes=wk, dma=True)
                    for b in range(T // TB):
                        _blk4(b, b % 2, qbd_r[b % 2], kTw_r[b % 2], zat_r[b % 2], zbt_r[b % 2])
                p.flush()

        p.flush()
    return nc


def _rpb_table(rpb):
    L = rpb.shape[0]
    krl = (np.arange(128) // 64)[:, None, None]
    kc = (np.arange(128) % 64)[:, None, None]
    m = np.arange(14)[None, :, None]
    qc = np.arange(64)[None, None, :]
    cs = np.clip(qc - 8, 0, 48)
    valid = np.broadcast_to((kc >= cs) & (kc < cs + 16), (128, 14, 64))
    dc = np.clip(kc - qc + 15, 0, 30)
    out = np.empty((L, 128, 8, 14, 64), np.float32)
    for l in range(L):
        for h in range(8):
            out[l, :, h] = np.where(valid, rpb[l, h][(m + krl), dc], np.float32(NEG))
    return out.reshape(L, 128, 8 * 14 * 64)


_PROG_CACHE = {}


def kernel(x_prompt, x_sample, norm_g, w_in, attn_q_norm_g, attn_k_norm_g, attn_rpb, dn_conv_w,
           dn_a_log, dn_dt_bias, dn_norm_g, w_branch_a, w_branch_b, w_out):
    f = lambda a: np.ascontiguousarray(np.asarray(a, dtype=np.float32))
    x_prompt, x_sample = f(x_prompt), f(x_sample)
    n = 8
    Ts, Tp = x_sample.shape[1], x_prompt.shape[1]
    L = w_in.shape[0]
    key = (Ts, Tp, L)
    if key not in _PROG_CACHE:
        _PROG_CACHE[key] = build_program([Ts, Tp], n_layers=L)
    nc = _PROG_CACHE[key]
    shared = dict(w_in=f(w_in), w_a=f(w_branch_a), w_b=f(w_branch_b), w_o=f(w_out), norm_g=f(norm_g),
                  qg=f(attn_q_norm_g), kg=f(attn_k_norm_g), conv_w=f(dn_conv_w),
                  a_log=f(dn_a_log).reshape(L, 8), dt_bias=f(dn_dt_bias).reshape(L, 8), dn_g=f(dn_norm_g),
                  rpbT=_rpb_table(f(attn_rpb)), consts=make_consts())
    nP = x_prompt.shape[0]
    in_maps = []
    for i in range(n):
        d = dict(shared)
        d["x0"] = x_sample[i]
        d["x1"] = x_prompt[i % nP]
        in_maps.append(d)
    res = run_bass_kernel_spmd(nc, in_maps, core_ids=list(range(n)))
    y_sample = np.stack([np.asarray(res.results[i]["y0"], dtype=np.float32) for i in range(n)], 0)
    y_prompt = np.stack([np.asarray(res.results[i]["y1"], dtype=np.float32) for i in range(nP)], 0)
    return (y_prompt, y_sample)
```

```python
import contextlib
import numpy as np
import concourse.bass as bass
import concourse.mybir as mybir
from concourse.bass_utils import run_bass_kernel_spmd

F32 = mybir.dt.float32
BF16 = mybir.dt.bfloat16
F32R = mybir.dt.float32r
USE_F32R = True


def R(ap):
    return ap.bitcast(F32R) if USE_F32R else ap
AF = mybir.ActivationFunctionType
ALU = mybir.AluOpType

D = 1024
DIN = 6160
EPS = 1e-6
NEG = -30000.0
BIG = 65536.0
TB = 512

C_ID = 0
C_UT = 128
C_LT = 256
C_BD = 384
C_ONE = 512
C_M_FWD = 640
C_M_BWD = 1152
NCONST = 1664


def make_consts():
    c = np.zeros((128, NCONST), np.float32)
    p = np.arange(128)[:, None]
    f = np.arange(128)[None, :]
    c[:, C_ID:C_ID + 128] = (p == f)
    c[:, C_UT:C_UT + 128] = (p <= f)
    c[:, C_LT:C_LT + 128] = (p >= f)
    c[:, C_BD:C_BD + 128] = ((p // 64) == (f // 64))
    c[:, C_ONE:C_ONE + 128] = 1.0
    c[:, C_M_FWD + 128:C_M_FWD + 256] = BIG * (f >= p)
    c[:, C_M_FWD + 256:C_M_FWD + 384] = -BIG * (f <= p)
    c[:, C_M_FWD + 384:C_M_FWD + 512] = -BIG * (f < p)
    c[:, C_M_BWD + 128:C_M_BWD + 256] = BIG * (f <= p)
    c[:, C_M_BWD + 256:C_M_BWD + 384] = -BIG * (f >= p)
    c[:, C_M_BWD + 384:C_M_BWD + 512] = -BIG * (f > p)
    return c


class Prog:
    ENG = ("pe", "act", "dve", "pool", "sp")
    NDS = 12

    def __init__(self, nc, stack):
        self.nc = nc
        self.ops = []
        self.lastw = {}
        self.readers = {}
        self.eng_map = {"pe": nc.tensor, "act": nc.scalar, "dve": nc.vector, "pool": nc.gpsimd, "sp": nc.sync}
        self.sems = {e: stack.enter_context(nc.semaphore("s_" + e)) for e in self.ENG}
        self.dsems = {e: [stack.enter_context(nc.semaphore("d_%s%d" % (e, i))) for i in range(self.NDS)] for e in ("sp", "pool")}
        self.cnt = {e: 0 for e in self.ENG}
        self.dcnt = {e: [0] * self.NDS for e in self.dsems}
        self.dnext = {e: 0 for e in self.dsems}
        self.sig = {}
        self.waited = {e: {} for e in self.ENG}
        self.emitted = 0
        self.n_instr = 0
        self.costs = []

    max_ops = None
    marks = {}
    attach_waits = True
    SYNC_LAT = 0.3

    def mark(self, name):
        self.marks.setdefault(name, len(self.ops))

    DEF_COST = {"pe": 0.07, "act": 0.40, "dve": 0.30, "pool": 0.6, "sp": 0.15}

    def op(self, eng, fn, reads=(), writes=(), dma=False, cost=None):
        if self.max_ops is not None and len(self.ops) >= self.max_ops:
            return -1
        self.costs.append(cost if cost is not None else ((0.8 if eng == "pool" else 0.15) if dma else self.DEF_COST[eng]))
        ex = [k for k in reads if isinstance(k, str) and k.startswith(("bk", "tp#", "acc#", "ssp#", "gbp", "P_"))]
        if ex:
            reads = [k for k in reads if k not in ex]
            writes = list(writes) + ex
        deps = set()
        for k in reads:
            w = self.lastw.get(k)
            if w is not None:
                deps.add(w)
        for k in writes:
            w = self.lastw.get(k)
            if w is not None:
                deps.add(w)
            rs = self.readers.get(k)
            if rs:
                deps.update(rs)
        idx = len(self.ops)
        self.ops.append((eng, fn, sorted(deps), dma))
        for k in reads:
            self.readers.setdefault(k, []).append(idx)
        for k in writes:
            self.lastw[k] = idx
            self.readers[k] = []
        return idx

    def flush(self):
        nc = self.nc
        ops = self.ops
        start, n = self.emitted, len(ops)
        import heapq
        costs = self.costs
        succ = {}
        indeg = {}
        for i in range(start, n):
            k = 0
            for d in ops[i][2]:
                if d >= start:
                    succ.setdefault(d, []).append(i)
                    k += 1
            indeg[i] = k
        blev = {}
        for i in range(n - 1, start - 1, -1):
            m_ = 0.0
            for j in succ.get(i, ()):
                if blev[j] > m_:
                    m_ = blev[j]
            blev[i] = m_ + costs[i] + (3.0 if ops[i][3] else 0.0)
        future = {e: [] for e in self.ENG}
        avail = {e: [] for e in self.ENG}
        ready = {}
        for i in range(start, n):
            if indeg[i] == 0:
                heapq.heappush(avail[ops[i][0]], (-blev[i], i))
        free = {e: 0.0 for e in self.ENG}
        finish = {}
        order = []
        remaining = n - start
        while remaining:
            best = None
            for e in self.ENG:
                if avail[e]:
                    st_ = free[e]
                elif future[e]:
                    st_ = max(free[e], future[e][0][0])
                else:
                    continue
                if best is None or st_ < best[0]:
                    best = (st_, e)
            st_, e = best
            fu = future[e]
            while fu and fu[0][0] <= st_:
                r_, j = heapq.heappop(fu)
                heapq.heappush(avail[e], (-blev[j], j))
            _, i = heapq.heappop(avail[e])
            dma = ops[i][3]
            free[e] = st_ + costs[i]
            finish[i] = st_ + (costs[i] + 3.0 if dma else costs[i])
            order.append(i)
            remaining -= 1
            for j in succ.get(i, ()):
                r_ = ready.get(j, 0.0)
                lat = 0.0 if (ops[j][0] == "pe" and e == "pe") else self.SYNC_LAT
                if finish[i] + lat > r_:
                    ready[j] = r_ = finish[i] + lat
                indeg[j] -= 1
                if indeg[j] == 0:
                    heapq.heappush(future[ops[j][0]], (r_, j))
        self.sim_time = max(finish.values()) if finish else 0.0
        need = {}
        lastop = {}
        for i in order:
            e, fn, deps, dma = ops[i]
            for d in deps:
                if e == "pe" and ops[d][0] == "pe":
                    continue
                if d >= start:
                    need[d] = True
                else:
                    assert d in self.sig, "cross-phase dep on unsignalled op"
            lastop[e] = i
            if dma:
                need[i] = True
        for j in lastop.values():
            need[j] = True
        NDS = self.NDS
        sems, dsems, cnt, dcnt, dnext, sig, waited = self.sems, self.dsems, self.cnt, self.dcnt, self.dnext, self.sig, self.waited
        for i in order:
            e, fn, deps, dma = ops[i]
            eo = self.eng_map[e]
            ws = {}
            for d in deps:
                if e == "pe" and ops[d][0] == "pe":
                    continue
                s, v = sig[d]
                key = id(s)
                if key not in ws or ws[key][1] < v:
                    ws[key] = (s, v)
            if dma:
                di = dnext[e] % NDS
                dnext[e] += 1
                ds = dsems[e][di]
                if dcnt[e][di] > 0:
                    key = id(ds)
                    v = dcnt[e][di]
                    if key not in ws or ws[key][1] < v:
                        ws[key] = (ds, v)
            wd = waited[e]
            pend = []
            for key, (s, v) in ws.items():
                if wd.get(key, 0) >= v:
                    continue
                wd[key] = v
                pend.append((s, v))
            for (s, v) in pend[1:]:
                eo.wait_ge(s, v)
                self.n_instr += 1
            ins = fn()
            if pend:
                if self.attach_waits:
                    ins._wait_ge(pend[0][0], pend[0][1])
                else:
                    eo.wait_ge(pend[0][0], pend[0][1])
            self.n_instr += 1
            if need.get(i):
                if dma:
                    dcnt[e][di] += 16
                    ins.then_inc(ds, 16)
                    sig[i] = (ds, dcnt[e][di])
                else:
                    cnt[e] += 1
                    ins.then_inc(sems[e], 1)
                    sig[i] = (sems[e], cnt[e])
            ops[i] = (e, None, None, dma)
        allsems = [(sems[x], cnt[x]) for x in self.ENG if cnt[x] > 0]
        for x in dsems:
            allsems += [(dsems[x][k], dcnt[x][k]) for k in range(NDS) if dcnt[x][k] > 0]
        for x in self.ENG:
            for (s_, v_) in allsems:
                if waited[x].get(id(s_), 0) < v_:
                    waited[x][id(s_)] = v_
                    self.eng_map[x].wait_ge(s_, v_)
                    self.n_instr += 1
        self.emitted = n
        self.lastw = {k: v for k, v in self.lastw.items() if isinstance(k, tuple)}
        self.readers = {k: v for k, v in self.readers.items() if isinstance(k, tuple)}


class Ring:
    def __init__(self, tiles, name):
        self.tiles = tiles
        self.name = name
        self.i = 0

    def next(self):
        k = self.i % len(self.tiles)
        self.i += 1
        return self.tiles[k], "%s#%d" % (self.name, k)


def build_program(seqs, n_layers=2, debug=False, stop_after=None, max_ops=None):
    nc = bass.Bass("TRN2", target_bir_lowering=False)
    L = n_layers
    Ttot = sum(seqs)
    bases = [sum(seqs[:i]) for i in range(len(seqs))]
    dk = "ExternalOutput" if debug else "Internal"

    def din(name, shape, dt=F32):
        return nc.dram_tensor(name, shape, dt, kind="ExternalInput").ap()

    xin = [din("x%d" % i, [T, D]) for i, T in enumerate(seqs)]
    yout = [nc.dram_tensor("y%d" % i, [T, D], F32, kind="ExternalOutput").ap() for i, T in enumerate(seqs)]
    w_in = din("w_in", [L, D, DIN])
    w_a = din("w_a", [L, 512, D])
    w_b = din("w_b", [L, 512, D])
    w_o = din("w_o", [L, D, D])
    norm_g = din("norm_g", [L, D])
    qg = din("qg", [L, 64])
    kg = din("kg", [L, 64])
    conv_w = din("conv_w", [L, 5, 1536])
    a_log = din("a_log", [L, 8])
    dt_bias = din("dt_bias", [L, 8])
    dn_g = din("dn_g", [L, 128])
    rpbT = din("rpbT", [L, 128, 8 * 14 * 64])
    consts = din("consts", [128, NCONST])

    def scr(name, shape, dt):
        return nc.dram_tensor(name, shape, dt, kind=dk).ap()

    qkT = scr("qkT", [1024, Ttot], BF16)
    va = scr("va", [Ttot, 512], BF16)
    zaT = scr("zaT", [512, Ttot], BF16)
    qkvbT = scr("qkvbT", [1536, Ttot], BF16)
    zbT = scr("zbT", [512, Ttot], BF16)
    gbs = scr("gbs", [Ttot, 16], F32)
    gaT = scr("gaT", [1024, Ttot], BF16)
    gbT = scr("gbT", [1024, Ttot], BF16)
    oT = scr("oT", [2, 512, Ttot], F32)
    prepT = scr("prepT", [4, 3, 128, Ttot], BF16)
    x1 = scr("xmid", [Ttot, D], F32)

    with contextlib.ExitStack() as gst:
        p = Prog(nc, gst)
        p.max_ops = max_ops
        uid = [0]

        def SB(st, name, shape, dt):
            uid[0] += 1
            return st.enter_context(nc.sbuf_tensor("%s_u%d" % (name, uid[0]), shape, dt))

        def PS(st, name, shape, dt=F32):
            uid[0] += 1
            return st.enter_context(nc.psum_tensor("%s_u%d" % (name, uid[0]), shape, dt))

        cst = SB(gst, "cst", [128, NCONST], F32)
        cstb = SB(gst, "cstb", [128, NCONST], BF16)
        p.op("sp", lambda: nc.sync.dma_start(out=cst[:], in_=consts), writes=["cst"], dma=True)
        p.op("dve", lambda: nc.vector.tensor_copy(out=cstb[:], in_=cst[:]), reads=["cst"], writes=["cstb"])
        identb = cstb[:, C_ID:C_ID + 128]
        bdb = cstb[:, C_BD:C_BD + 128]

        for l in range(L):
            with contextlib.ExitStack() as st:
                win = SB(st, "win", [128, 8, DIN], BF16)
                gcol = SB(st, "gcol", [128, 8], F32)
                qkgain = SB(st, "qkgain", [128, 2], F32)
                dtb = SB(st, "dtb", [128, 8], F32)
                nega = SB(st, "nega", [128, 8], F32)
                for kc in range(8):
                    p.op("pool", lambda kc=kc: nc.gpsimd.dma_start(out=win[:, kc, :], in_=w_in[l, kc * 128:(kc + 1) * 128, :]),
                         writes=["win%d" % kc], dma=True)
                p.op("sp", lambda: nc.sync.dma_start(out=gcol[:], in_=norm_g[l].rearrange("(kc p) -> p kc", p=128),
                                                     allow_slow_non_contiguous=True), writes=["gcol"], dma=True)
                for hh in range(2):
                    p.op("sp", lambda hh=hh: nc.sync.dma_start(out=qkgain[hh * 64:(hh + 1) * 64, 0:1], in_=qg[l].rearrange("(p o) -> p o", o=1),
                                                               allow_slow_non_contiguous=True), writes=["qkgain"], dma=True)
                    p.op("sp", lambda hh=hh: nc.sync.dma_start(out=qkgain[hh * 64:(hh + 1) * 64, 1:2], in_=kg[l].rearrange("(p o) -> p o", o=1),
                                                               allow_slow_non_contiguous=True), writes=["qkgain"], dma=True)
                p.op("act", lambda: nc.scalar.mul(out=qkgain[:, 0:1], in_=qkgain[:, 0:1], mul=0.125), reads=["qkgain"], writes=["qkgain"])
                p.op("sp", lambda: nc.sync.dma_start(out=dtb[:], in_=dt_bias[l:l + 1, :].broadcast_to([128, 8])), writes=["dtb"], dma=True)
                p.op("sp", lambda: nc.sync.dma_start(out=nega[:], in_=a_log[l:l + 1, :].broadcast_to([128, 8])), writes=["nega"], dma=True)
                p.op("act", lambda: nc.scalar.activation(out=nega[:], in_=nega[:], func=AF.Exp), reads=["nega"], writes=["nega"])
                p.op("dve", lambda: nc.vector.tensor_scalar(out=nega[:], in0=nega[:], scalar1=-1.0, scalar2=None, op0=ALU.mult),
                     reads=["nega"], writes=["nega"])

                xr = Ring([SB(st, "xt%d" % i, [128, 4, D], F32) for i in range(2)], "xt")
                xn = SB(st, "xn", [128, 4, D], BF16)
                junk = SB(st, "junk", [128, D], BF16)
                ssr = Ring([SB(st, "ss%d" % i, [128, 8], F32) for i in range(2)], "ss")
                hTr = Ring([SB(st, "hT%d" % i, [128, 8, TB], BF16) for i in range(2)], "hT")
                sqr = Ring([SB(st, "sq%d" % i, [128, TB], BF16) for i in range(2)], "sq")
                rsr = Ring([SB(st, "rs%d" % i, [128, TB], F32) for i in range(2)], "rs")
                sgr = Ring([SB(st, "sg%d" % i, [128, TB], F32) for i in range(5)], "sg")
                outr = Ring([SB(st, "ob%d" % i, [128, TB], BF16) for i in range(10)], "ob")
                gbt = SB(st, "gbt", [128, 4, 16], F32)
                gtmp = SB(st, "gtmp", [128, 4, 16], F32)
                tpr = Ring([PS(st, "tp%d" % i, [128, 2 * TB], BF16)[:, 0:TB] for i in range(2)], "tp")
                accr = Ring([PS(st, "acc%d" % i, [128, TB], F32) for i in range(4)], "acc")
                ssp = Ring([PS(st, "ssp%d" % i, [128, TB], F32) for i in range(1)], "ssp")
                gbp = PS(st, "gbp", [128, 4, 128], F32)[:, :, 0:16]
                winkeys = ["win%d" % kc for kc in range(8)]
                evac_flip = [0]

                def store_fm(dst, row0, nrows, gt0, tile, tkey, dkey):
                    evac_flip[0] += 1
                    if evac_flip[0] % 2:
                        return p.op("sp", lambda: nc.sync.dma_start(out=dst[row0:row0 + nrows, gt0:gt0 + TB], in_=tile[0:nrows, :]),
                                    reads=[tkey], writes=[dkey], dma=True)
                    return p.op("pool", lambda: nc.gpsimd.dma_start(out=dst[row0:row0 + nrows, gt0:gt0 + TB], in_=tile[0:nrows, :]),
                                reads=[tkey], writes=[dkey], dma=True)

                for si, T in enumerate(seqs):
                    xsrc = xin[si] if l == 0 else x1[bases[si]:bases[si] + T, :]
                    for b in range(T // TB):
                        t0 = b * TB
                        gt0 = bases[si] + t0
                        gblk = gt0 // TB
                        xt, kx = xr.next()
                        srck = [] if l == 0 else [("x1", gblk)]
                        p.op("sp", lambda xt=xt, t0=t0, xsrc=xsrc: nc.sync.dma_start(
                            out=xt[:, :, :], in_=xsrc[t0:t0 + TB, :].rearrange("(tt p) d -> p tt d", p=128)),
                            reads=srck, writes=[kx], dma=True)
                        ss, kss = ssr.next()
                        for tt in range(4):
                            p.op("act", lambda xt=xt, tt=tt, ss=ss: nc.scalar.activation(
                                out=junk[:, :], in_=xt[:, tt, :], func=AF.Square, accum_out=ss[:, tt:tt + 1]),
                                reads=[kx], writes=["junk", kss])
                        p.op("dve", lambda ss=ss: nc.vector.tensor_scalar(out=ss[:, 4:8], in0=ss[:, 0:4], scalar1=1.0 / D, scalar2=EPS,
                                                                          op0=ALU.mult, op1=ALU.add), reads=[kss], writes=[kss])
                        p.op("act", lambda ss=ss: nc.scalar.activation(out=ss[:, 4:8], in_=ss[:, 4:8], func=AF.Ln), reads=[kss], writes=[kss])
                        p.op("act", lambda ss=ss: nc.scalar.activation(out=ss[:, 4:8], in_=ss[:, 4:8], func=AF.Exp, scale=-0.5), reads=[kss], writes=[kss])
                        for tt in range(4):
                            p.op("act", lambda xt=xt, tt=tt, ss=ss: nc.scalar.activation(
                                out=xn[:, tt, :], in_=xt[:, tt, :], func=AF.Copy, scale=ss[:, 4 + tt:5 + tt]),
                                reads=[kx, kss], writes=["xn"])
                        hT, khT = hTr.next()
                        for kc in range(8):
                            tp, ktp = tpr.next()
                            for tt in range(4):
                                p.op("pe", lambda tp=tp, tt=tt, kc=kc: nc.tensor.transpose(
                                    out=tp[:, tt * 128:(tt + 1) * 128], in_=xn[:, tt, kc * 128:(kc + 1) * 128], identity=identb),
                                    reads=["xn", "cstb"], writes=[ktp])
                            if kc % 2 == 0:
                                p.op("act", lambda tp=tp, kc=kc, hT=hT: nc.scalar.activation(
                                    out=hT[:, kc, :], in_=tp[:, :], func=AF.Copy, scale=gcol[:, kc:kc + 1]),
                                    reads=[ktp, "gcol"], writes=[khT + "k%d" % kc])
                            else:
                                p.op("dve", lambda tp=tp, kc=kc, hT=hT: nc.vector.tensor_scalar(
                                    out=hT[:, kc, :], in0=tp[:, :], scalar1=gcol[:, kc:kc + 1], scalar2=None, op0=ALU.mult),
                                    reads=[ktp, "gcol"], writes=[khT + "k%d" % kc])
                        hkeys = [khT + "k%d" % kc for kc in range(8)]

                        def proj_fm(col0, M, hT=hT, hkeys=hkeys):
                            acc, kacc = accr.next()
                            for kc in range(8):
                                p.op("pe", lambda kc=kc, acc=acc: nc.tensor.matmul(
                                    acc[0:M, :], lhsT=win[:, kc, col0:col0 + M], rhs=hT[:, kc, :], start=(kc == 0), stop=(kc == 7)),
                                    reads=[winkeys[kc], hkeys[kc]], writes=[kacc], cost=0.23)
                            return acc, kacc

                        pending = []

                        def flush():
                            while pending:
                                pending.pop(0)()

                        for c in range(8):
                            acc, kacc = proj_fm(c * 128, 128)
                            flush()
                            sq, ksq = sqr.next()
                            p.op("act", lambda sq=sq, acc=acc: nc.scalar.activation(out=sq[:, :], in_=acc[:, :], func=AF.Square),
                                 reads=[kacc], writes=[ksq])

                            def later(c=c, acc=acc, kacc=kacc, sq=sq, ksq=ksq, gt0=gt0, gblk=gblk):
                                sp_, kssp = ssp.next()
                                p.op("pe", lambda: nc.tensor.matmul(sp_[:, :], lhsT=bdb, rhs=sq[:, :], start=True, stop=True),
                                     reads=[ksq, "cstb"], writes=[kssp], cost=0.23)
                                rs, krs = rsr.next()
                                p.op("dve", lambda: nc.vector.tensor_scalar(out=rs[:, :], in0=sp_[:, :], scalar1=1.0 / 64, scalar2=EPS,
                                                                            op0=ALU.mult, op1=ALU.add), reads=[kssp], writes=[krs])
                                p.op("act", lambda: nc.scalar.activation(out=rs[:, :], in_=rs[:, :], func=AF.Ln), reads=[krs], writes=[krs], cost=0.57)
                                p.op("act", lambda: nc.scalar.activation(out=rs[:, :], in_=rs[:, :], func=AF.Exp, scale=-0.5), reads=[krs], writes=[krs], cost=0.57)
                                ob, kob = outr.next()
                                gi = 0 if c < 4 else 1
                                p.op("dve", lambda: nc.vector.scalar_tensor_tensor(
                                    out=ob[:, :], in0=acc[:, :], scalar=qkgain[:, gi:gi + 1], in1=rs[:, :], op0=ALU.mult, op1=ALU.mult),
                                    reads=[kacc, krs, "qkgain"], writes=[kob])
                                store_fm(qkT, c * 128, 128, gt0, ob, kob, ("qkT", gblk))
                            pending.append(later)
                        for tt in range(4):
                            acc, kacc = accr.next()
                            for kc in range(8):
                                p.op("pe", lambda kc=kc, acc=acc, tt=tt, hT=hT: nc.tensor.matmul(
                                    acc[:, :], lhsT=hT[:, kc, tt * 128:(tt + 1) * 128], rhs=win[:, kc, 1024:1536], start=(kc == 0), stop=(kc == 7)),
                                    reads=[winkeys[kc], hkeys[kc]], writes=[kacc], cost=0.23)
                            flush()
                            ob, kob = outr.next()
                            p.op("dve", lambda ob=ob, acc=acc: nc.vector.tensor_copy(out=ob[:, :], in_=acc[:, :]), reads=[kacc], writes=[kob])
                            p.op("pool", lambda ob=ob, tt=tt, gt0=gt0: nc.gpsimd.dma_start(out=va[gt0 + tt * 128:gt0 + (tt + 1) * 128, :], in_=ob[:, :]),
                                 reads=[kob], writes=[("va", gblk)], dma=True)
                        plan = []
                        for c in range(4):
                            plan.append((1536 + c * 128, "silu", zaT, c * 128, "zaT"))
                        for c in range(4):
                            plan.append((3584 + c * 128, "silu", zbT, c * 128, "zbT"))
                        for c in range(8):
                            plan.append((4112 + c * 128, "sig", gaT, c * 128, "gaT"))
                        for c in range(8):
                            plan.append((5136 + c * 128, "sig", gbT, c * 128, "gbT"))
                        for c in range(12):
                            plan.append((2048 + c * 128, "copy", qkvbT, c * 128, "qkvbT"))
                        for (col0, kind, dst, row0, dname) in plan:
                            acc, kacc = proj_fm(col0, 128)
                            ob, kob = outr.next()
                            if kind in ("silu", "sig"):
                                sg, ksg = sgr.next()
                                p.op("act", lambda sg=sg, acc=acc: nc.scalar.activation(out=sg[:, :], in_=acc[:, :], func=AF.Exp, scale=-1.0),
                                     reads=[kacc], writes=[ksg], cost=0.57)
                                p.op("act", lambda sg=sg: nc.scalar.activation(out=sg[:, :], in_=sg[:, :], func=AF.Ln, bias=1.0), reads=[ksg], writes=[ksg], cost=0.57)
                                if kind == "sig":
                                    p.op("act", lambda sg=sg, ob=ob: nc.scalar.activation(out=ob[:, :], in_=sg[:, :], func=AF.Exp, scale=-1.0),
                                         reads=[ksg], writes=[kob], cost=0.57)
                                else:
                                    p.op("act", lambda sg=sg: nc.scalar.activation(out=sg[:, :], in_=sg[:, :], func=AF.Exp, scale=-1.0), reads=[ksg], writes=[ksg], cost=0.57)
                                    p.op("dve", lambda sg=sg, ob=ob, acc=acc: nc.vector.tensor_tensor(out=ob[:, :], in0=acc[:, :], in1=sg[:, :], op=ALU.mult),
                                         reads=[kacc, ksg], writes=[kob], cost=0.55)
                            else:
                                p.op("dve", lambda ob=ob, acc=acc: nc.vector.tensor_copy(out=ob[:, :], in_=acc[:, :]), reads=[kacc], writes=[kob])
                            store_fm(dst, row0, 128, gt0, ob, kob, (dname, gblk))
                        for tt in range(4):
                            for kc in range(8):
                                p.op("pe", lambda kc=kc, tt=tt, hT=hT: nc.tensor.matmul(
                                    gbp[:, tt, :], lhsT=hT[:, kc, tt * 128:(tt + 1) * 128], rhs=win[:, kc, 4096:4112], start=(kc == 0), stop=(kc == 7)),
                                    reads=[winkeys[kc], hkeys[kc]], writes=["gbp"])
                        p.op("dve", lambda: nc.vector.tensor_scalar(out=gtmp[:, :, 0:8], in0=gbp[:, :, 0:8], scalar1=-1.0, scalar2=None, op0=ALU.mult),
                             reads=["gbp"], writes=["gtmp"])
                        p.op("dve", lambda: nc.vector.tensor_tensor(out=gtmp[:, :, 8:16], in0=gbp[:, :, 8:16],
                                                                    in1=dtb[:, :].unsqueeze(1).broadcast_to([128, 4, 8]), op=ALU.add),
                             reads=["gbp", "dtb"], writes=["gtmp"])
                        p.op("act", lambda: nc.scalar.activation(out=gtmp[:, :, :], in_=gtmp[:, :, :], func=AF.Exp), reads=["gtmp"], writes=["gtmp"])
                        p.op("act", lambda: nc.scalar.activation(out=gtmp[:, :, :], in_=gtmp[:, :, :], func=AF.Ln, bias=1.0), reads=["gtmp"], writes=["gtmp"])
                        p.op("dve", lambda: nc.vector.tensor_scalar(out=gbt[:, :, 0:8], in0=gtmp[:, :, 0:8], scalar1=-1.0, scalar2=None, op0=ALU.mult),
                             reads=["gtmp"], writes=["gbt"])
                        p.op("dve", lambda: nc.vector.tensor_tensor(out=gbt[:, :, 8:16], in0=gtmp[:, :, 8:16],
                                                                    in1=nega[:, :].unsqueeze(1).broadcast_to([128, 4, 8]), op=ALU.mult),
                             reads=["gtmp", "nega"], writes=["gbt"])
                        p.op("pool", lambda gt0=gt0: nc.gpsimd.dma_start(out=gbs[gt0:gt0 + TB, :].rearrange("(tt p) c -> p tt c", p=128), in_=gbt[:, :, :]),
                             reads=["gbt"], writes=[("gbs", gblk)], dma=True)
                        flush()
                p.flush()
            if stop_after == "p1":
                break
            for d in ((1, 0) if stop_after != "p2" else (1,)):
                with contextlib.ExitStack() as st:
                    TRI = cst[:, C_LT:C_LT + 128] if d == 1 else cst[:, C_UT:C_UT + 128]
                    mbase = C_M_BWD if d == 1 else C_M_FWD
                    MASKB = cstb[:, mbase:mbase + 512]
                    LAST = 0 if d == 1 else 127
                    identf = cst[:, C_ID:C_ID + 128]
                    onesf = cst[:, C_ONE:C_ONE + 128]
                    onesb = cstb[:, C_ONE:C_ONE + 128]
                    cw = SB(st, "cw", [128, 12, 5], F32)
                    cwd = SB(st, "cwd", [128, 60, 128], BF16)
                    for tap in range(5):
                        p.op("sp", lambda tap=tap: nc.sync.dma_start(out=cw[:, :, tap], in_=conv_w[l, tap, :].rearrange("(j p) -> p j", p=128),
                                                                     allow_slow_non_contiguous=True), writes=["cw"], dma=True)
                    for j in range(12):
                        for tap in range(5):
                            p.op("dve", lambda j=j, tap=tap: nc.vector.tensor_scalar(
                                out=cwd[:, j * 5 + tap, :], in0=identf, scalar1=cw[:, j, tap:tap + 1], scalar2=None, op0=ALU.mult),
                                reads=["cw", "cst"], writes=["cwd"])
                    p.mark('dn_consts_done')
                    rawr = Ring([SB(st, "raw%d" % i, [128, 12, TB + 4], BF16) for i in range(2)], "raw")
                    gscr = Ring([SB(st, "gsc%d" % i, [128, 4, 16], F32) for i in range(2)], "gsc")
                    knq_r = [[SB(st, "knq%d_%d" % (h, i), [128, 2, TB], BF16) for h in range(4)] for i in range(2)]
                    vsT_r = [[SB(st, "vsT%d_%d" % (h, i), [128, TB], BF16) for h in range(4)] for i in range(2)]
                    sil = [SB(st, "sil%d" % h, [128, TB], F32) for h in range(4)]
                    sqb = [SB(st, "sqb%d" % h, [128, TB], BF16) for h in range(4)]
                    rst = [SB(st, "rst%d" % h, [128, TB], F32) for h in range(4)]
                    rhs4 = SB(st, "rhs4", [128, 4, 128], F32)
                    rhsL = SB(st, "rhsL", [128, 4, 128], F32)
                    sc = SB(st, "sc", [128, 24], F32)
                    E4 = [SB(st, "E4_%d" % h, [128, 128], F32) for h in range(4)]
                    E2 = [SB(st, "E2_%d" % h, [128, 128], F32) for h in range(4)]
                    E13 = [SB(st, "E13_%d" % h, [128, 256], F32) for h in range(4)]
                    WW = [[SB(st, "WW%d_%d" % (h, i), [128, 384], F32) for i in range(2)] for h in range(4)]
                    onesr = SB(st, "onesr", [128, 128], F32)
                    trir = SB(st, "trir", [128, 128], F32)
                    gr = SB(st, "gr", [128, 4], F32)
                    p.op("dve", lambda: nc.vector.tensor_copy(out=R(onesr[:, :]), in_=onesf), reads=["cst"], writes=["onesr"])
                    p.op("dve", lambda: nc.vector.tensor_copy(out=R(trir[:, :]), in_=TRI), reads=["cst"], writes=["trir"])
                    Tt = [SB(st, "Tt%d" % h, [128, 128], BF16) for h in range(4)]
                    M3 = [SB(st, "M3_%d" % h, [128, 128], BF16) for h in range(4)]
                    qdec = [SB(st, "qdec%d" % h, [128, 128], BF16) for h in range(4)]
                    kbg = [SB(st, "kbg%d" % h, [128, 128], BF16) for h in range(4)]
                    kdec = [SB(st, "kdec%d" % h, [128, 128], BF16) for h in range(4)]
                    vb = [SB(st, "vb%d" % h, [128, 128], BF16) for h in range(4)]
                    uu = [SB(st, "uu%d" % h, [128, 128], F32) for h in range(4)]
                    wT = [SB(st, "wT%d" % h, [128, 128], BF16) for h in range(4)]
                    vnew = [SB(st, "vnew%d" % h, [128, 128], BF16) for h in range(4)]
                    Sf = [SB(st, "Sf%d" % h, [128, 128], F32) for h in range(4)]
                    Sb = [SB(st, "Sb%d" % h, [128, 128], BF16) for h in range(4)]
                    obr = Ring([SB(st, "obuf%d" % i, [128, 4, TB], F32) for i in range(2)], "obuf")
                    bk0 = [PS(st, "bk0_%d" % h, [128, 512], F32) for h in range(4)]
                    bk1 = [PS(st, "bk1_%d" % h, [128, 512], F32) for h in range(4)]
                    k0 = ["bk0_%d" % h for h in range(4)]
                    k1 = ["bk1_%d" % h for h in range(4)]

                    for si, T in enumerate(seqs):
                        nb = T // TB
                        for h in range(4):
                            p.op("pool", lambda h=h: nc.gpsimd.memset(Sf[h][:, :], 0.0), writes=["Sf%d" % h])
                            p.op("pool", lambda h=h: nc.gpsimd.memset(Sb[h][:, :], 0.0), writes=["Sb%d" % h])
                        def _blk(bi, bp, knq, vsT, si=si, T=T, nb=nb):
                                b = nb - 1 - bi if d == 1 else bi
                                t0 = b * TB
                                gt0 = bases[si] + t0
                                gblk = gt0 // TB
                                if d == 1:
                                    raw, kraw = rawr.next()
                                    lo = 0 if b > 0 else 2
                                    hi = TB + 4 if b < nb - 1 else TB + 2
                                    if lo > 0:
                                        p.op("pool", lambda raw=raw: nc.gpsimd.memset(raw[:, :, 0:2], 0.0), writes=[kraw])
                                    if hi < TB + 4:
                                        p.op("pool", lambda raw=raw: nc.gpsimd.memset(raw[:, :, TB + 2:TB + 4], 0.0), writes=[kraw])
                                    rk = [("qkvbT", gblk)] + ([("qkvbT", gblk - 1)] if b > 0 else []) + ([("qkvbT", gblk + 1)] if b < nb - 1 else [])
                                    for part in range(3):
                                        p.op("sp", lambda raw=raw, lo=lo, hi=hi, gt0=gt0, part=part: nc.sync.dma_start(
                                            out=raw[:, part * 4:(part + 1) * 4, lo:hi],
                                            in_=qkvbT[part * 512:(part + 1) * 512, gt0 - 2 + lo:gt0 - 2 + hi].rearrange("(j p) t -> p j t", p=128)),
                                            reads=rk, writes=[kraw], dma=True)
                                gsc, kgsc = gscr.next()
                                p.op("sp", lambda gsc=gsc, gt0=gt0: nc.sync.dma_start(
                                    out=gsc[:, :, :], in_=gbs[gt0:gt0 + TB, :].rearrange("(c p) k -> p c k", p=128)),
                                    reads=[("gbs", gblk)], writes=[kgsc], dma=True)
                                if d == 1:
                                    p.mark('dn_loads_done')
                                    for h in range(4):
                                        for (which, j, bank, bkey) in ((0, 4 + h, bk0[h], k0[h]), (1, h, bk1[h], k1[h])):
                                            for tap in range(5):
                                                p.op("pe", lambda j=j, tap=tap, bank=bank, raw=raw: nc.tensor.matmul(
                                                    bank[:, :], lhsT=cwd[:, j * 5 + tap, :], rhs=raw[:, j, tap:tap + TB], start=(tap == 0), stop=(tap == 4)),
                                                    reads=["cwd", kraw], writes=[bkey], cost=0.23)
                                            p.op("act", lambda bank=bank, h=h: nc.scalar.activation(out=sil[h][:, :], in_=bank[:, :], func=AF.Exp, scale=-1.0),
                                                 reads=[bkey], writes=["sil%d" % h], cost=0.57)
                                            p.op("act", lambda h=h: nc.scalar.activation(out=sil[h][:, :], in_=sil[h][:, :], func=AF.Ln, bias=1.0),
                                                 reads=["sil%d" % h], writes=["sil%d" % h], cost=0.57)
                                            p.op("act", lambda h=h: nc.scalar.activation(out=sil[h][:, :], in_=sil[h][:, :], func=AF.Exp, scale=-1.0),
                                                 reads=["sil%d" % h], writes=["sil%d" % h], cost=0.57)
                                            p.op("dve", lambda bank=bank, h=h: nc.vector.tensor_tensor(out=sil[h][:, :], in0=bank[:, :], in1=sil[h][:, :], op=ALU.mult),
                                                 reads=[bkey, "sil%d" % h], writes=["sil%d" % h], cost=0.55)
                                            p.op("act", lambda h=h: nc.scalar.activation(out=sqb[h][:, :], in_=sil[h][:, :], func=AF.Square),
                                                 reads=["sil%d" % h], writes=["sqb%d" % h])
                                            p.op("pe", lambda bank=bank, h=h: nc.tensor.matmul(bank[:, :], lhsT=onesb, rhs=sqb[h][:, :], start=True, stop=True),
                                                 reads=["sqb%d" % h, "cstb"], writes=[bkey], cost=0.23)
                                            p.op("dve", lambda bank=bank, h=h: nc.vector.tensor_scalar(out=rst[h][:, :], in0=bank[:, :], scalar1=EPS, scalar2=None, op0=ALU.add),
                                                 reads=[bkey], writes=["rst%d" % h])
                                            p.op("act", lambda h=h: nc.scalar.activation(out=rst[h][:, :], in_=rst[h][:, :], func=AF.Ln),
                                                 reads=["rst%d" % h], writes=["rst%d" % h], cost=0.57)
                                            p.op("act", lambda h=h: nc.scalar.activation(out=rst[h][:, :], in_=rst[h][:, :], func=AF.Exp, scale=-0.5),
                                                 reads=["rst%d" % h], writes=["rst%d" % h], cost=0.57)
                                            sclq = (128.0 ** -0.5) if which == 1 else 1.0
                                            p.op("dve", lambda h=h, which=which, sclq=sclq: nc.vector.scalar_tensor_tensor(
                                                out=knq[h][:, which, :], in0=sil[h][:, :], scalar=sclq, in1=rst[h][:, :], op0=ALU.mult, op1=ALU.mult),
                                                reads=["sil%d" % h, "rst%d" % h], writes=["knq%d_%d" % (h, bp)])
                                        for tap in range(5):
                                            p.op("pe", lambda h=h, tap=tap, raw=raw: nc.tensor.matmul(
                                                bk0[h][:, :], lhsT=cwd[:, (8 + h) * 5 + tap, :], rhs=raw[:, 8 + h, tap:tap + TB], start=(tap == 0), stop=(tap == 4)),
                                                reads=["cwd", kraw], writes=[k0[h]], cost=0.23)
                                        p.op("act", lambda h=h: nc.scalar.activation(out=sil[h][:, :], in_=bk0[h][:, :], func=AF.Exp, scale=-1.0),
                                             reads=[k0[h]], writes=["sil%d" % h], cost=0.57)
                                        p.op("act", lambda h=h: nc.scalar.activation(out=sil[h][:, :], in_=sil[h][:, :], func=AF.Ln, bias=1.0),
                                             reads=["sil%d" % h], writes=["sil%d" % h], cost=0.57)
                                        p.op("act", lambda h=h: nc.scalar.activation(out=sil[h][:, :], in_=sil[h][:, :], func=AF.Exp, scale=-1.0),
                                             reads=["sil%d" % h], writes=["sil%d" % h], cost=0.57)
                                        p.op("dve", lambda h=h: nc.vector.tensor_tensor(out=vsT[h][:, :], in0=bk0[h][:, :], in1=sil[h][:, :], op=ALU.mult),
                                             reads=[k0[h], "sil%d" % h], writes=["vsT%d_%d" % (h, bp)], cost=0.55)
                                        p.op("pool", lambda h=h, gt0=gt0: nc.gpsimd.dma_start(out=prepT[h, 0:2, :, gt0:gt0 + TB].rearrange("w p t -> p w t"), in_=knq[h][:, :, :]),
                                             reads=["knq%d_%d" % (h, bp)], writes=[("prepT", gblk)], dma=True)
                                        p.op("pool", lambda h=h, gt0=gt0: nc.gpsimd.dma_start(out=prepT[h, 2, :, gt0:gt0 + TB], in_=vsT[h][:, :]),
                                             reads=["vsT%d_%d" % (h, bp)], writes=[("prepT", gblk)], dma=True)
                                else:
                                    for h in range(4):
                                        p.op("sp", lambda h=h, gt0=gt0: nc.sync.dma_start(out=knq[h][:, :, :], in_=prepT[h, 0:2, :, gt0:gt0 + TB].rearrange("w p t -> p w t")),
                                             reads=[("prepT", gblk)], writes=["knq%d_%d" % (h, bp)], dma=True)
                                        p.op("sp", lambda h=h, gt0=gt0: nc.sync.dma_start(out=vsT[h][:, :], in_=prepT[h, 2, :, gt0:gt0 + TB]),
                                             reads=[("prepT", gblk)], writes=["vsT%d_%d" % (h, bp)], dma=True)
                                p.mark('dn_prep_done')
                                ob, kob = obr.next()
                                for ci in range(4):
                                    c = 3 - ci if d == 1 else ci
                                    cs = slice(c * 128, (c + 1) * 128)
                                    gcol0 = 8 + d * 4
                                    p.op("dve", lambda c=c, gsc=gsc: nc.vector.tensor_copy(out=R(gr[:, :]), in_=gsc[:, c, gcol0:gcol0 + 4]), reads=[kgsc], writes=["gr"])
                                    p.op("pe", lambda: nc.tensor.matmul(bk1[0][:, 0:4], lhsT=R(trir[:, :]), rhs=R(gr[:, :]), start=True, stop=True),
                                         reads=["trir", "gr"], writes=[k1[0]])
                                    p.op("dve", lambda: nc.vector.tensor_copy(out=sc[:, 0:4], in_=bk1[0][:, 0:4]), reads=[k1[0]], writes=["sc"])
                                    p.op("dve", lambda c=c, gsc=gsc: nc.vector.tensor_tensor(out=sc[:, 4:8], in0=sc[:, 0:4], in1=gsc[:, c, d * 4:d * 4 + 4], op=ALU.add),
                                         reads=["sc", kgsc], writes=["sc"])
                                    p.op("dve", lambda: nc.vector.tensor_scalar(out=sc[:, 8:12], in0=sc[:, 0:4], scalar1=-1.0, scalar2=None, op0=ALU.mult),
                                         reads=["sc"], writes=["sc"])
                                    p.op("act", lambda: nc.scalar.activation(out=sc[:, 12:16], in_=sc[:, 4:8], func=AF.Exp), reads=["sc"], writes=["sc"])
                                    p.op("act", lambda c=c, gsc=gsc: nc.scalar.activation(out=sc[:, 16:20], in_=gsc[:, c, d * 4:d * 4 + 4], func=AF.Exp),
                                         reads=[kgsc, "sc"], writes=["sc"])
                                    p.op("dve", lambda c=c, gsc=gsc: nc.vector.tensor_tensor(
                                        out=R(rhs4[:, :, :]), in0=TRI.unsqueeze(1).broadcast_to([128, 4, 128]),
                                        in1=gsc[:, c, gcol0:gcol0 + 4].unsqueeze(2).broadcast_to([128, 4, 128]), op=ALU.mult),
                                        reads=["cst", kgsc], writes=["rhs4"])
                                    p.op("dve", lambda c=c, gsc=gsc: nc.vector.tensor_tensor(
                                        out=R(rhsL[:, :, :]), in0=identf.unsqueeze(1).broadcast_to([128, 4, 128]),
                                        in1=gsc[:, c, d * 4:d * 4 + 4].unsqueeze(2).broadcast_to([128, 4, 128]), op=ALU.mult),
                                        reads=["cst", kgsc], writes=["rhsL"])
                                    p.mark('dn_sc_done')
                                    for h in range(4):
                                        p.op("pe", lambda h=h: nc.tensor.matmul(bk0[h][:, :], lhsT=identb, rhs=MASKB, start=True, stop=False),
                                             reads=["cstb"], writes=[k0[h]], cost=0.23)
                                        for q4 in range(4):
                                            p.op("pe", lambda h=h, q4=q4: nc.tensor.matmul(bk0[h][:, q4 * 128:(q4 + 1) * 128], lhsT=R(onesr[:, :]), rhs=R(rhs4[:, h, :]),
                                                                                           start=False, stop=False),
                                                 reads=["onesr", "rhs4"], writes=[k0[h]], cost=0.07)
                                        p.op("pe", lambda h=h: nc.tensor.matmul(bk0[h][:, 256:384], lhsT=R(onesr[:, :]), rhs=R(rhsL[:, h, :]), start=False, stop=True),
                                             reads=["onesr", "rhsL"], writes=[k0[h]], cost=0.07)
                                        p.op("pe", lambda h=h, cs=cs: nc.tensor.matmul(bk1[h][:, 0:256], lhsT=knq[h][:, 0, cs], rhs=knq[h][:, :, cs], start=True, stop=True),
                                             reads=["knq%d_%d" % (h, bp)], writes=[k1[h]])
                                        p.op("act", lambda h=h: nc.scalar.activation(out=E4[h][:, :], in_=bk0[h][:, 0:128], func=AF.Exp), reads=[k0[h]], writes=["E4_%d" % h])
                                        p.op("act", lambda h=h: nc.scalar.activation(out=E2[h][:, :], in_=bk0[h][:, 128:256], func=AF.Exp, scale=-1.0, bias=sc[:, 4 + h:5 + h]),
                                             reads=[k0[h], "sc"], writes=["E2_%d" % h])
                                        p.op("act", lambda h=h: nc.scalar.activation(out=E13[h][:, :], in_=bk0[h][:, 256:512], func=AF.Exp, bias=sc[:, 8 + h:9 + h]),
                                             reads=[k0[h], "sc"], writes=["E13_%d" % h])
                                        p.op("act", lambda h=h: nc.scalar.activation(out=sc[:, 20 + h:21 + h], in_=bk0[h][:, LAST:LAST + 1], func=AF.Exp, bias=sc[:, 8 + h:9 + h]),
                                             reads=[k0[h], "sc"], writes=["sc"])
                                        p.op("dve", lambda h=h: nc.vector.tensor_tensor(out=R(WW[h][0][:, 128:256]), in0=bk1[h][:, 0:128], in1=E13[h][:, 0:128], op=ALU.mult),
                                             reads=[k1[h], "E13_%d" % h], writes=["WW%d_0" % h])
                                        p.op("dve", lambda h=h: nc.vector.tensor_tensor(out=R(WW[h][0][:, 256:384]), in0=bk1[h][:, 0:128], in1=E2[h][:, :], op=ALU.mult),
                                             reads=[k1[h], "E2_%d" % h], writes=["WW%d_0" % h])
                                        p.op("dve", lambda h=h: nc.vector.tensor_tensor(out=M3[h][:, :], in0=bk1[h][:, 128:256], in1=E13[h][:, 128:256], op=ALU.mult),
                                             reads=[k1[h], "E13_%d" % h], writes=["M3_%d" % h])
                                        p.op("dve", lambda h=h, cs=cs: nc.vector.tensor_tensor(out=qdec[h][:, :], in0=knq[h][:, 1, cs], in1=E4[h][:, :], op=ALU.mult),
                                             reads=["knq%d_%d" % (h, bp), "E4_%d" % h], writes=["qdec%d" % h])
                                        p.op("dve", lambda h=h: nc.vector.tensor_tensor(out=R(WW[h][1][:, 0:128]), in0=identf, in1=WW[h][0][:, 128:256], op=ALU.subtract),
                                             reads=["cst", "WW%d_0" % h], writes=["WW%d_1y" % h])
                                        p.op("pe", lambda h=h, cs=cs: nc.tensor.transpose(out=bk1[h][:, 256:384].bitcast(BF16)[:, 0:128], in_=knq[h][:, 0, cs], identity=identb),
                                             reads=["knq%d_%d" % (h, bp), "cstb"], writes=[k1[h]])
                                        p.op("pe", lambda h=h, cs=cs: nc.tensor.transpose(out=bk1[h][:, 384:512].bitcast(BF16)[:, 0:128], in_=vsT[h][:, cs], identity=identb),
                                             reads=["vsT%d_%d" % (h, bp), "cstb"], writes=[k1[h]])
                                        p.op("act", lambda h=h: nc.scalar.activation(out=kbg[h][:, :], in_=bk1[h][:, 256:384].bitcast(BF16)[:, 0:128], func=AF.Copy, scale=sc[:, 12 + h:13 + h]),
                                             reads=[k1[h], "sc"], writes=["kbg%d" % h])
                                        p.op("dve", lambda h=h: nc.vector.tensor_scalar(out=kdec[h][:, :], in0=bk1[h][:, 256:384].bitcast(BF16)[:, 0:128], scalar1=sc[:, 20 + h:21 + h], scalar2=None, op0=ALU.mult),
                                             reads=[k1[h], "sc"], writes=["kdec%d" % h])
                                        p.op("act", lambda h=h: nc.scalar.activation(out=vb[h][:, :], in_=bk1[h][:, 384:512].bitcast(BF16)[:, 0:128], func=AF.Copy, scale=sc[:, 16 + h:17 + h]),
                                             reads=[k1[h], "sc"], writes=["vb%d" % h])
                                    p.mark('dn_pg_done')
                                    for lev in range(0, 7):
                                        for h in range(4):
                                            cur, nxt = WW[h][lev % 2], WW[h][(lev + 1) % 2]
                                            kc_, kn_ = "WW%d_%d" % (h, lev % 2), "WW%d_%d" % (h, (lev + 1) % 2)
                                            kcy, kny = kc_ + "y", kn_ + "y"
                                            if lev == 0:
                                                p.op("pe", lambda h=h, cur=cur: nc.tensor.matmul(bk0[h][:, 128:256], lhsT=R(cur[:, 256:384]), rhs=R(cur[:, 128:256]), start=True, stop=True),
                                                     reads=[kc_], writes=[k0[h]])
                                            elif lev < 6:
                                                p.op("pe", lambda h=h, cur=cur: nc.tensor.matmul(bk0[h][:, 0:256], lhsT=R(cur[:, 256:384]), rhs=R(cur[:, 0:256]), start=True, stop=True),
                                                     reads=[kc_, kcy], writes=[k0[h]], cost=0.11)
                                            else:
                                                p.op("pe", lambda h=h, cur=cur: nc.tensor.matmul(bk0[h][:, 0:128], lhsT=R(cur[:, 256:384]), rhs=R(cur[:, 0:128]), start=True, stop=True),
                                                     reads=[kc_, kcy], writes=[k0[h]])
                                            if lev < 6:
                                                p.op("pe", lambda h=h, cur=cur: nc.tensor.matmul(bk0[h][:, 256:384], lhsT=R(cur[:, 128:256]), rhs=R(cur[:, 256:384]), start=True, stop=True),
                                                     reads=[kc_], writes=[k0[h]])
                                                if (lev + h) % 3 == 0:
                                                    p.op("dve", lambda h=h, nxt=nxt: nc.vector.tensor_copy(out=R(nxt[:, 128:384]), in_=bk0[h][:, 128:384]), reads=[k0[h]], writes=[kn_])
                                                else:
                                                    p.op("act", lambda h=h, nxt=nxt: nc.scalar.copy(out=R(nxt[:, 128:384]), in_=bk0[h][:, 128:384]), reads=[k0[h]], writes=[kn_])
                                            if 1 <= lev < 6:
                                                p.op("dve", lambda h=h, nxt=nxt, cur=cur: nc.vector.tensor_tensor(out=R(nxt[:, 0:128]), in0=bk0[h][:, 0:128], in1=cur[:, 0:128], op=ALU.add),
                                                     reads=[k0[h], kcy], writes=[kny])
                                            elif lev == 6:
                                                p.op("dve", lambda h=h, cur=cur: nc.vector.tensor_tensor(out=Tt[h][:, :], in0=bk0[h][:, 0:128], in1=cur[:, 0:128], op=ALU.add),
                                                     reads=[k0[h], kcy], writes=["Tt%d" % h])
                                    p.mark('dn_inv_done')
                                    for h in range(4):
                                        p.op("pe", lambda h=h: nc.tensor.matmul(bk0[h][:, 0:128], lhsT=Tt[h][:, :], rhs=vb[h][:, :], start=True, stop=True),
                                             reads=["Tt%d" % h, "vb%d" % h], writes=[k0[h]])
                                        p.op("pe", lambda h=h: nc.tensor.matmul(bk0[h][:, 128:256], lhsT=kbg[h][:, :], rhs=Tt[h][:, :], start=True, stop=True),
                                             reads=["Tt%d" % h, "kbg%d" % h], writes=[k0[h]])
                                        p.op("act", lambda h=h: nc.scalar.copy(out=uu[h][:, :], in_=bk0[h][:, 0:128]), reads=[k0[h]], writes=["uu%d" % h])
                                        p.op("dve", lambda h=h: nc.vector.tensor_copy(out=wT[h][:, :], in_=bk0[h][:, 128:256]), reads=[k0[h]], writes=["wT%d" % h])
                                    for h in range(4):
                                        p.op("pe", lambda h=h: nc.tensor.matmul(bk1[h][:, 0:128], lhsT=wT[h][:, :], rhs=Sb[h][:, :], start=True, stop=True),
                                             reads=["wT%d" % h, "Sb%d" % h], writes=[k1[h]])
                                        p.op("dve", lambda h=h: nc.vector.scalar_tensor_tensor(out=vnew[h][:, :], in0=bk1[h][:, 0:128], scalar=-1.0, in1=uu[h][:, :],
                                                                                              op0=ALU.mult, op1=ALU.add),
                                             reads=[k1[h], "uu%d" % h], writes=["vnew%d" % h])
                                    for h in range(4):
                                        p.op("pe", lambda h=h: nc.tensor.matmul(bk1[h][:, 128:256], lhsT=Sb[h][:, :], rhs=qdec[h][:, :], start=True, stop=False),
                                             reads=["Sb%d" % h, "qdec%d" % h], writes=[k1[h]])
                                        p.op("pe", lambda h=h: nc.tensor.matmul(bk1[h][:, 128:256], lhsT=vnew[h][:, :], rhs=M3[h][:, :], start=False, stop=True),
                                             reads=["vnew%d" % h, "M3_%d" % h], writes=[k1[h]])
                                        p.op("act", lambda h=h, ob=ob, cs=cs: nc.scalar.copy(out=ob[:, h, cs], in_=bk1[h][:, 128:256]), reads=[k1[h]], writes=[kob])
                                        p.op("pe", lambda h=h: nc.tensor.matmul(bk1[h][:, 256:384], lhsT=kdec[h][:, :], rhs=vnew[h][:, :], start=True, stop=True),
                                             reads=["kdec%d" % h, "vnew%d" % h], writes=[k1[h]])
                                        p.op("pool", lambda h=h: nc.gpsimd.tensor_scalar(out=Sf[h][:, :], in0=Sf[h][:, :], scalar1=E4[h][:, LAST:LAST + 1], scalar2=None, op0=ALU.mult),
                                             reads=["Sf%d" % h, "E4_%d" % h], writes=["Sf%d" % h])
                                        p.op("dve", lambda h=h: nc.vector.tensor_tensor(out=Sf[h][:, :], in0=bk1[h][:, 256:384], in1=Sf[h][:, :], op=ALU.add),
                                             reads=[k1[h], "Sf%d" % h], writes=["Sf%d" % h])
                                        p.op("pool", lambda h=h: nc.gpsimd.tensor_copy(out=Sb[h][:, :], in_=Sf[h][:, :]), reads=["Sf%d" % h], writes=["Sb%d" % h])
                                p.op("pool", lambda ob=ob, gt0=gt0: nc.gpsimd.dma_start(
                                    out=oT[d, :, gt0:gt0 + TB].rearrange("(h p) t -> p h t", p=128), in_=ob[:, :, :]),
                                    reads=[kob], writes=[("oT%d" % d, gblk)], dma=True)
                        for bi in range(nb):
                            _blk(bi, bi % 2, knq_r[bi % 2], vsT_r[bi % 2])
                    p.flush()
            if stop_after in ("p2", "p3"):
                break
            with contextlib.ExitStack() as st:
                onesb = cstb[:, C_ONE:C_ONE + 128]
                waT = SB(st, "waT", [128, 4, D], BF16)
                wbT = SB(st, "wbT", [128, 4, D], BF16)
                woT = SB(st, "woT", [128, 8, D], BF16)
                GTb = SB(st, "GTb", [128, 8 * 14 * 64], BF16)
                dng = SB(st, "dng", [128, 1], F32)
                p.op("pool", lambda: nc.gpsimd.dma_start(out=waT[:, :, :], in_=w_a[l].rearrange("(c p) m -> p c m", p=128)), writes=["waT"], dma=True)
                p.op("pool", lambda: nc.gpsimd.dma_start(out=wbT[:, :, :], in_=w_b[l].rearrange("(h d) m -> d h m", d=128)), writes=["wbT"], dma=True)
                for m in range(8):
                    p.op("pool", lambda m=m: nc.gpsimd.dma_start(out=woT[:, m, :], in_=w_o[l, m * 128:(m + 1) * 128, :]), writes=["woT"], dma=True)
                for q4 in range(4):
                    p.op("pool", lambda q4=q4: nc.gpsimd.dma_start(out=GTb[:, q4 * 1792:(q4 + 1) * 1792], in_=rpbT[l, :, q4 * 1792:(q4 + 1) * 1792]),
                         writes=["GTb"], dma=True)
                p.op("sp", lambda: nc.sync.dma_start(out=dng[:, :], in_=dn_g[l].rearrange("(p o) -> p o", o=1), allow_slow_non_contiguous=True),
                     writes=["dng"], dma=True)
                GT4 = GTb[:, :].rearrange("p (h m q) -> p h m q", h=8, m=14)
                qbd_r = [SB(st, "qbd%d" % i, [128, 4, 8, 128], BF16) for i in range(2)]
                for i_ in range(2):
                    p.op("pool", lambda i_=i_: nc.gpsimd.memset(qbd_r[i_][:, :, :, :], 0.0), writes=["qbd_%d" % i_])
                kTw_r = [SB(st, "kTw%d" % i, [128, 4, 1024], BF16) for i in range(2)]
                vw = [SB(st, "vw%d" % i, [128, 8, 512], BF16) for i in range(2)]
                zat_r = [SB(st, "zat%d" % i, [128, 4, TB], BF16) for i in range(2)]
                oag = SB(st, "oag", [128, 4, TB], BF16)
                pT = SB(st, "pT", [128, 4, 8, 64], BF16)
                rcp = SB(st, "rcp", [128, 512], F32)
                otmp = SB(st, "otmp", [128, 256], F32)
                ofr = Ring([SB(st, "of%d" % i, [128, TB], F32) for i in range(2)], "of")
                obr4 = Ring([SB(st, "ob4%d" % i, [128, TB], F32) for i in range(2)], "ob4")
                osum = SB(st, "osum", [128, TB], F32)
                osq = SB(st, "osq", [128, TB], BF16)
                orst = SB(st, "orst", [128, TB], F32)
                zbt_r = [SB(st, "zbt%d" % i, [128, 4, TB], BF16) for i in range(2)]
                obg = SB(st, "obg", [128, 4, TB], BF16)
                gar = Ring([SB(st, "ga%d" % i, [128, TB], BF16) for i in range(2)], "ga")
                gbr = Ring([SB(st, "gb%d" % i, [128, TB], BF16) for i in range(2)], "gb")
                t1 = SB(st, "t1", [128, TB], F32)
                t2 = SB(st, "t2", [128, TB], F32)
                mrg = SB(st, "mrg", [128, 8, TB], BF16)
                xrr = Ring([SB(st, "xr%d" % i, [128, D], F32) for i in range(2)], "xr")
                outr4 = Ring([SB(st, "o4_%d" % i, [128, 512], F32) for i in range(2)], "o4")
                Sr = Ring([PS(st, "P_S%d" % i, [128, 512], F32) for i in range(3)], "P_S")
                SUMp = PS(st, "P_SUM", [128, 512], F32)
                OTp = PS(st, "P_OT", [128, 512], F32)
                acc4 = Ring([PS(st, "P_acc%d" % i, [128, 512], F32) for i in range(3)], "P_acc")

                for si, T in enumerate(seqs):
                    rows = T // 64
                    xsrc = xin[si] if l == 0 else x1[bases[si]:bases[si] + T, :]
                    dst = yout[si] if l == L - 1 else x1[bases[si]:bases[si] + T, :]
                    def _blk4(b, bp, qbd, kTw, zat, zbt, si=si, T=T, rows=rows, xsrc=xsrc, dst=dst):
                            t0 = b * TB
                            gt0 = bases[si] + t0
                            gblk = gt0 // TB
                            r0 = b * 8
                            wlo = min(max(r0 - 4, 0), rows - 8)
                            wtok = wlo * 64
                            nk = min(T - wtok, 1024)
                            kblks = sorted(set((bases[si] + wtok + i) // TB for i in range(0, nk, 64)))
                            for h2 in range(2):
                                for c4 in range(4):
                                    p.op("sp", lambda gt0=gt0, h2=h2, c4=c4: nc.sync.dma_start(
                                        out=qbd[h2 * 64:(h2 + 1) * 64, c4, :, h2 * 64:(h2 + 1) * 64],
                                        in_=qkT[c4 * 128 + h2 * 64:c4 * 128 + (h2 + 1) * 64, gt0:gt0 + TB].rearrange("p (r q) -> p r q", q=64)),
                                        reads=[("qkT", gblk)], writes=["qbd_%d" % bp], dma=True)
                            p.op("sp", lambda si=si, wtok=wtok, nk=nk: nc.sync.dma_start(
                                out=kTw[:, :, 0:nk], in_=qkT[512:1024, bases[si] + wtok:bases[si] + wtok + nk].rearrange("(c p) t -> p c t", p=128)),
                                reads=[("qkT", kb) for kb in kblks], writes=["kTw_%d" % bp], dma=True)
                            for par in range(2):
                                vstart = wtok + par * 64
                                nfull = min(T - vstart, 1024) // 128
                                p.op("sp", lambda si=si, par=par, vstart=vstart, nfull=nfull: nc.sync.dma_start(
                                    out=vw[par][:, 0:nfull, :],
                                    in_=va[bases[si] + vstart:bases[si] + vstart + nfull * 128, :].rearrange("(s p) f -> p s f", p=128)),
                                    reads=[("va", kb) for kb in kblks], writes=["vw%d" % par], dma=True)
                            p.op("sp", lambda gt0=gt0: nc.sync.dma_start(out=zat[:, :, :], in_=zaT[:, gt0:gt0 + TB].rearrange("(c p) t -> p c t", p=128)),
                                 reads=[("zaT", gblk)], writes=["zat_%d" % bp], dma=True)
                            p.op("sp", lambda gt0=gt0: nc.sync.dma_start(out=zbt[:, :, :], in_=zbT[:, gt0:gt0 + TB].rearrange("(h d) t -> d h t", d=128)),
                                 reads=[("zbT", gblk)], writes=["zbt_%d" % bp], dma=True)
                            for h in range(4):
                                of_, kof = ofr.next()
                                ob_, kob4 = obr4.next()
                                p.op("sp", lambda h=h, of_=of_, gt0=gt0: nc.sync.dma_start(out=of_[:, :], in_=oT[0, h * 128:(h + 1) * 128, gt0:gt0 + TB]),
                                     reads=[("oT0", gblk)], writes=[kof], dma=True)
                                p.op("sp", lambda h=h, ob_=ob_, gt0=gt0: nc.sync.dma_start(out=ob_[:, :], in_=oT[1, h * 128:(h + 1) * 128, gt0:gt0 + TB]),
                                     reads=[("oT1", gblk)], writes=[kob4], dma=True)
                                p.op("pool", lambda of_=of_, ob_=ob_: nc.gpsimd.tensor_tensor(out=osum[:, :], in0=of_[:, :], in1=ob_[:, :], op=ALU.add),
                                     reads=[kof, kob4], writes=["osum"])
                                p.op("act", lambda: nc.scalar.activation(out=osq[:, :], in_=osum[:, :], func=AF.Square), reads=["osum"], writes=["osq"])
                                acc, kacc = acc4.next()
                                p.op("pe", lambda acc=acc: nc.tensor.matmul(acc[:, :], lhsT=onesb, rhs=osq[:, :], start=True, stop=True),
                                     reads=["osq"], writes=[kacc], cost=0.23)
                                p.op("dve", lambda acc=acc: nc.vector.tensor_scalar(out=orst[:, :], in0=acc[:, :], scalar1=1.0 / 128, scalar2=EPS, op0=ALU.mult, op1=ALU.add),
                                     reads=[kacc], writes=["orst"])
                                p.op("act", lambda: nc.scalar.activation(out=orst[:, :], in_=orst[:, :], func=AF.Ln), reads=["orst"], writes=["orst"], cost=0.57)
                                p.op("act", lambda: nc.scalar.activation(out=orst[:, :], in_=orst[:, :], func=AF.Exp, scale=-0.5), reads=["orst"], writes=["orst"], cost=0.57)
                                p.op("dve", lambda: nc.vector.scalar_tensor_tensor(out=osum[:, :], in0=osum[:, :], scalar=dng[:, 0:1], in1=orst[:, :], op0=ALU.mult, op1=ALU.mult),
                                     reads=["osum", "orst", "dng"], writes=["osum"])
                                p.op("pool", lambda h=h: nc.gpsimd.tensor_tensor(out=obg[:, h, :], in0=osum[:, :], in1=zbt[:, h, :], op=ALU.mult),
                                     reads=["osum", "zbt_%d" % bp], writes=["obg"])
                            for rr in range(8):
                                r = r0 + rr
                                rs = min(max(r - 4, 0), rows - 8)
                                o_ = r - rs
                                m0 = 7 - o_
                                par = (rs - wlo) % 2
                                slot0 = (rs - wlo - par) // 2
                                koff = (rs - wlo) * 64
                                qs = slice(rr * 64, (rr + 1) * 64)
                                for pr in range(4):
                                    S_, kS = Sr.next()
                                    p.op("pe", lambda S_=S_, pr=pr, m0=m0: nc.tensor.matmul(
                                        S_[:, :], lhsT=identb, rhs=GT4[:, 2 * pr:2 * pr + 2, m0:m0 + 7:2, :].rearrange("p h k q -> p k h q"), start=True, stop=False),
                                        reads=["GTb"], writes=[kS], cost=0.23)
                                    for kk in range(4):
                                        p.op("pe", lambda S_=S_, pr=pr, kk=kk, koff=koff, rr=rr: nc.tensor.matmul(
                                            S_[:, kk * 128:(kk + 1) * 128], lhsT=kTw[:, pr, koff + kk * 128:koff + (kk + 1) * 128],
                                            rhs=qbd[:, pr, rr, :], start=False, stop=(kk == 3)),
                                            reads=["kTw_%d" % bp, "qbd_%d" % bp], writes=[kS])
                                    p.op("act", lambda S_=S_, pr=pr: nc.scalar.activation(
                                        out=pT[:, :, 2 * pr:2 * pr + 2, :], in_=S_[:, :].rearrange("p (k h q) -> p k h q", k=4, h=2), func=AF.Exp),
                                        reads=[kS], writes=["pT"], cost=0.57)
                                for kk in range(4):
                                    p.op("pe", lambda kk=kk: nc.tensor.matmul(SUMp[:, :], lhsT=onesb, rhs=pT[:, kk, :, :], start=(kk == 0), stop=(kk == 3)),
                                         reads=["pT"], writes=["P_SUM"], cost=0.23)
                                for h in range(8):
                                    for kk in range(4):
                                        p.op("pe", lambda h=h, kk=kk, par=par, slot0=slot0: nc.tensor.matmul(
                                            OTp[(h % 2) * 64:(h % 2) * 64 + 64, (h // 2) * 64:(h // 2) * 64 + 64],
                                            lhsT=vw[par][:, slot0 + kk, h * 64:(h + 1) * 64], rhs=pT[:, kk, h, :],
                                            start=(kk == 0), stop=(kk == 3)),
                                            reads=["pT", "vw%d" % par], writes=["P_OT"])
                                p.op("act", lambda: nc.scalar.activation(out=rcp[:, :], in_=SUMp[:, :], func=AF.Ln), reads=["P_SUM"], writes=["rcp"], cost=0.57)
                                p.op("act", lambda: nc.scalar.activation(out=rcp[:, :], in_=rcp[:, :], func=AF.Exp, scale=-1.0), reads=["rcp"], writes=["rcp"], cost=0.57)
                                for h2 in range(2):
                                    hs = slice(h2 * 64, (h2 + 1) * 64)
                                    p.op("dve", lambda h2=h2, hs=hs: nc.vector.tensor_tensor(
                                        out=otmp[hs, :].rearrange("p (c q) -> p c q", c=4), in0=OTp[hs, 0:256].rearrange("p (c q) -> p c q", c=4),
                                        in1=rcp[hs, :].rearrange("p (c h q) -> p c h q", c=4, h=2)[:, :, h2, :], op=ALU.mult),
                                        reads=["P_OT", "rcp"], writes=["otmp"])
                                p.op("pool", lambda qs=qs: nc.gpsimd.tensor_tensor(out=oag[:, :, qs], in0=otmp[:, :].rearrange("p (c q) -> p c q", c=4), in1=zat[:, :, qs], op=ALU.mult),
                                     reads=["otmp", "zat_%d" % bp], writes=["oag"])
                            for m in range(8):
                                ms = slice(m * 128, (m + 1) * 128)
                                ga_, kga = gar.next()
                                gb_, kgb = gbr.next()
                                p.op("sp", lambda ga_=ga_, m=m, gt0=gt0: nc.sync.dma_start(out=ga_[:, :], in_=gaT[m * 128:(m + 1) * 128, gt0:gt0 + TB]),
                                     reads=[("gaT", gblk)], writes=[kga], dma=True)
                                p.op("sp", lambda gb_=gb_, m=m, gt0=gt0: nc.sync.dma_start(out=gb_[:, :], in_=gbT[m * 128:(m + 1) * 128, gt0:gt0 + TB]),
                                     reads=[("gbT", gblk)], writes=[kgb], dma=True)
                                ya, kya = acc4.next()
                                for h in range(4):
                                    p.op("pe", lambda ya=ya, h=h, ms=ms: nc.tensor.matmul(ya[:, :], lhsT=waT[:, h, ms], rhs=oag[:, h, :], start=(h == 0), stop=(h == 3)),
                                         reads=["waT", "oag"], writes=[kya], cost=0.23)
                                yb, kyb = acc4.next()
                                for h in range(4):
                                    p.op("pe", lambda yb=yb, h=h, ms=ms: nc.tensor.matmul(yb[:, :], lhsT=wbT[:, h, ms], rhs=obg[:, h, :], start=(h == 0), stop=(h == 3)),
                                         reads=["wbT", "obg"], writes=[kyb], cost=0.23)
                                p.op("dve", lambda ya=ya, ga_=ga_: nc.vector.tensor_tensor(out=t1[:, :], in0=ya[:, :], in1=ga_[:, :], op=ALU.mult),
                                     reads=[kya, kga], writes=["t1"])
                                p.op("dve", lambda yb=yb, gb_=gb_: nc.vector.tensor_tensor(out=t2[:, :], in0=yb[:, :], in1=gb_[:, :], op=ALU.mult),
                                     reads=[kyb, kgb], writes=["t2"])
                                p.op("pool", lambda m=m: nc.gpsimd.tensor_tensor(out=mrg[:, m, :], in0=t1[:, :], in1=t2[:, :], op=ALU.add),
                                     reads=["t1", "t2"], writes=["mrg"])
                            for tt in range(4):
                                xr_, kxr = xrr.next()
                                srck = [] if l == 0 else [("x1", gblk)]
                                p.op("sp", lambda xr_=xr_, tt=tt, t0=t0, xsrc=xsrc: nc.sync.dma_start(out=xr_[:, :], in_=xsrc[t0 + tt * 128:t0 + (tt + 1) * 128, :]),
                                     reads=srck, writes=[kxr], dma=True)
                                for eh in range(2):
                                    po, kpo = acc4.next()
                                    for m in range(8):
                                        p.op("pe", lambda po=po, m=m, tt=tt, eh=eh: nc.tensor.matmul(
                                            po[:, :], lhsT=mrg[:, m, tt * 128:(tt + 1) * 128], rhs=woT[:, m, eh * 512:(eh + 1) * 512], start=(m == 0), stop=(m == 7)),
                                            reads=["mrg", "woT"], writes=[kpo], cost=0.23)
                                    o4, ko4 = outr4.next()
                                    p.op("dve", lambda po=po, o4=o4, xr_=xr_, eh=eh: nc.vector.tensor_tensor(out=o4[:, :], in0=po[:, :], in1=xr_[:, eh * 512:(eh + 1) * 512], op=ALU.add),
                                         reads=[kpo, kxr], writes=[ko4])
                                    wk = [("x1", gblk)] if l < L - 1 else [("y", si, b)]
                                    p.op("pool", lambda o4=o4, tt=tt, eh=eh, t0=t0, dst=dst: nc.gpsimd.dma_start(
                                        out=dst[t0 + tt * 128:t0 + (tt + 1) * 128, eh * 512:(eh + 1) * 512], in_=o4[:, :]),
                                        reads=[ko4], writes=wk, dma=True)
                    for b in range(T // TB):
                        _blk4(b, b % 2, qbd_r[b % 2], kTw_r[b % 2], zat_r[b % 2], zbt_r[b % 2])
                p.flush()

        p.flush()
    return nc


def _rpb_table(rpb):
    L = rpb.shape[0]
    krl = (np.arange(128) // 64)[:, None, None]
    kc = (np.arange(128) % 64)[:, None, None]
    m = np.arange(14)[None, :, None]
    qc = np.arange(64)[None, None, :]
    cs = np.clip(qc - 8, 0, 48)
    valid = np.broadcast_to((kc >= cs) & (kc < cs + 16), (128, 14, 64))
    dc = np.clip(kc - qc + 15, 0, 30)
    out = np.empty((L, 128, 8, 14, 64), np.float32)
    for l in range(L):
        for h in range(8):
            out[l, :, h] = np.where(valid, rpb[l, h][(m + krl), dc], np.float32(NEG))
    return out.reshape(L, 128, 8 * 14 * 64)


_PROG_CACHE = {}


def kernel(x_prompt, x_sample, norm_g, w_in, attn_q_norm_g, attn_k_norm_g, attn_rpb, dn_conv_w,
           dn_a_log, dn_dt_bias, dn_norm_g, w_branch_a, w_branch_b, w_out):
    f = lambda a: np.ascontiguousarray(np.asarray(a, dtype=np.float32))
    x_prompt, x_sample = f(x_prompt), f(x_sample)
    n = 8
    Ts, Tp = x_sample.shape[1], x_prompt.shape[1]
    L = w_in.shape[0]
    key = (Ts, Tp, L)
    if key not in _PROG_CACHE:
        _PROG_CACHE[key] = build_program([Ts, Tp], n_layers=L)
    nc = _PROG_CACHE[key]
    shared = dict(w_in=f(w_in), w_a=f(w_branch_a), w_b=f(w_branch_b), w_o=f(w_out), norm_g=f(norm_g),
                  qg=f(attn_q_norm_g), kg=f(attn_k_norm_g), conv_w=f(dn_conv_w),
                  a_log=f(dn_a_log).reshape(L, 8), dt_bias=f(dn_dt_bias).reshape(L, 8), dn_g=f(dn_norm_g),
                  rpbT=_rpb_table(f(attn_rpb)), consts=make_consts())
    nP = x_prompt.shape[0]
    in_maps = []
    for i in range(n):
        d = dict(shared)
        d["x0"] = x_sample[i]
        d["x1"] = x_prompt[i % nP]
        in_maps.append(d)
    res = run_bass_kernel_spmd(nc, in_maps, core_ids=list(range(n)))
    y_sample = np.stack([np.asarray(res.results[i]["y0"], dtype=np.float32) for i in range(n)], 0)
    y_prompt = np.stack([np.asarray(res.results[i]["y1"], dtype=np.float32) for i in range(nP)], 0)
    return (y_prompt, y_sample)
```
